# Optimizing a Trainium2 kernel written in Bass

```python
import jax
import jax.numpy as jnp
from jax import lax
import numpy as np

D_MODEL = 2048
BATCH = 4
SEQ = 2048
DEPTH = 1

GRID_W = 64
CTX_LEN = 256
N_HEADS_M = 8
DK_M = 128
DV_M = 256
D_QK = N_HEADS_M * DK_M
D_V = N_HEADS_M * DV_M
CHUNK = 64
QK_CONV_W = 3
D_CONV = D_MODEL
CONV_W = 31
D_FF = (8 * D_MODEL + 3 * 256 - 1) // (3 * 256) * 256
ALPHA = (2.0 * DEPTH) ** 0.25
BETA = (8.0 * DEPTH) ** -0.25
LN_EPS = 1e-5
N_GATES_M = 4 * N_HEADS_M
SPLITS = (D_QK, 2 * D_QK, 2 * D_QK + D_V, 2 * D_QK + 2 * D_V, 2 * D_QK + 2 * D_V + N_GATES_M, 2 * D_QK + 2 * D_V + N_GATES_M + 2 * D_CONV)
W_IN_COLS = SPLITS[-1] + 2 * D_MODEL

kernel_name = 'hybrid_mlstm_conformer_dit_block'


def layer_norm(x, g=None, b=None):
    xf = x.astype(jnp.float32)
    mu = xf.mean(-1, keepdims=True)
    var = jnp.square(xf - mu).mean(-1, keepdims=True)
    y = (xf - mu) * lax.rsqrt(var + LN_EPS)
    if g is not None:
        y = y * g.astype(jnp.float32)
    if b is not None:
        y = y + b.astype(jnp.float32)
    return y.astype(x.dtype)


def modulate(y, shift, scale):
    return y * (1.0 + scale) + shift


def dwconv(x, w, b):
    y = lax.conv_general_dilated(x, w[:, None, :].astype(x.dtype), (1,), 'SAME',
                                 dimension_numbers=('NWC', 'WIO', 'NWC'),
                                 feature_group_count=x.shape[-1])
    return y + b


def to_heads(a, d):
    B, T, _ = a.shape
    return a.reshape(B, T, -1, d).transpose(0, 2, 1, 3)


def flip_t(a):
    return jnp.flip(a, axis=2)


def zero_state(B):
    f32 = jnp.float32
    return (jnp.zeros((B, N_HEADS_M, DK_M, DV_M), f32), jnp.zeros((B, N_HEADS_M, DK_M), f32),
            jnp.zeros((B, N_HEADS_M), f32))


def mlstm_chunkwise(q, k, v, log_i, log_f, state):
    B, H, T, _ = q.shape
    f32 = jnp.float32
    tril = jnp.tril(jnp.ones((CHUNK, CHUNK), dtype=bool))

    def to_chunks(a):
        a = a.astype(f32).reshape(B, H, T // CHUNK, CHUNK, *a.shape[3:])
        return jnp.moveaxis(a, 2, 0)

    def step(carry, blk):
        C, n, m = carry
        qc, kc, vc, ic, fc = blk
        b = jnp.cumsum(fc, axis=-1)
        d_log = jnp.where(tril, b[..., :, None] - b[..., None, :] + ic[..., None, :], -jnp.inf)
        inter = b + m[..., None]
        m_t = jnp.maximum(inter, d_log.max(-1))
        w_intra = jnp.exp(d_log - m_t[..., None])
        w_inter = jnp.exp(inter - m_t)
        s = jnp.einsum('bhtd,bhsd->bhts', qc, kc) * w_intra
        num = w_inter[..., None] * jnp.einsum('bhtd,bhde->bhte', qc, C) + jnp.einsum('bhts,bhse->bhte', s, vc)
        den = w_inter * jnp.einsum('bhtd,bhd->bht', qc, n) + s.sum(-1)
        h = num / jnp.maximum(jnp.abs(den), jnp.exp(-m_t))[..., None]
        b_end = b[..., -1]
        w_log = b_end[..., None] - b + ic
        m_new = jnp.maximum(b_end + m, w_log.max(-1))
        decay = jnp.exp(b_end + m - m_new)
        wk = jnp.exp(w_log - m_new[..., None])[..., None] * kc
        C = decay[..., None, None] * C + jnp.einsum('bhsd,bhse->bhde', wk, vc)
        n = decay[..., None] * n + wk.sum(-2)
        return (C, n, m_new), h

    blocks = tuple(to_chunks(a) for a in (q, k, v, log_i, log_f))
    state, h = lax.scan(step, state, blocks)
    h = jnp.moveaxis(h, 0, 2).reshape(B, H, T, -1)
    return h.astype(v.dtype), state


def gate_logs(g):
    g = jnp.moveaxis(g.astype(jnp.float32), -1, 1)
    i_f, f_f, i_b, f_b = jnp.split(g, 4, axis=1)
    return i_f, jax.nn.log_sigmoid(f_f), i_b, jax.nn.log_sigmoid(f_b)


def mixer_inputs(u, w_in, b_if, w_qkc, b_qkc):
    p = u @ w_in
    q, k, v, o, g_if, conv_in, g_merge = jnp.split(p, SPLITS, axis=-1)
    qk = jax.nn.silu(dwconv(jnp.concatenate([q, k], axis=-1), w_qkc, b_qkc))
    q, k = jnp.split(qk, 2, axis=-1)
    q = to_heads(q, DK_M) * (DK_M ** -0.5)
    return q, to_heads(k, DK_M), to_heads(v, DV_M), o, g_if + b_if, conv_in, g_merge


def mixer_output(h, o, conv_in, g_merge, rows, mh_g, w_dw, b_dw, cln_g, cln_b, w_pm, w_pc, w_o):
    B, H, T, _ = h.shape
    hm = layer_norm(h).transpose(0, 2, 1, 3).reshape(B, T, D_V) * mh_g
    y_m = (hm * jax.nn.sigmoid(o)) @ w_pm
    a, gl = jnp.split(conv_in, 2, axis=-1)
    y = a * jax.nn.sigmoid(gl)
    C = y.shape[-1]
    if rows is not None:
        y = dwconv(y.reshape(B * rows, T // rows, C), w_dw, b_dw).reshape(B, T, C)
    else:
        y = dwconv(y, w_dw, b_dw)
    y_c = jax.nn.silu(layer_norm(y, cln_g, cln_b)) @ w_pc
    ga, gb = jnp.split(g_merge, 2, axis=-1)
    return (jax.nn.sigmoid(ga) * y_m + jax.nn.sigmoid(gb) * y_c) @ w_o


def swiglu(u, w_in, w_out):
    a, b = jnp.split(u @ w_in, 2, axis=-1)
    return (jax.nn.silu(a) * b) @ w_out


def setup_inputs(seed: int = 0) -> dict:
    key = jax.random.key(seed)
    ks = jax.random.split(key, 26)
    f32 = jnp.float32
    L = DEPTH

    def nrm(k, shape, scale):
        return jax.random.normal(k, shape, f32) * scale

    fan = D_MODEL ** -0.5
    lin = jnp.linspace(3.0, 6.0, N_HEADS_M, dtype=f32)
    zer = jnp.zeros((N_HEADS_M,), f32)
    if_base = jnp.concatenate([zer, lin, zer, lin])
    return {
        'x': nrm(ks[0], (BATCH, SEQ, D_MODEL), 1.0),
        'c': nrm(ks[1], (BATCH, D_MODEL), 1.0),
        'ctx': nrm(ks[2], (BATCH, CTX_LEN, D_MODEL), 1.0),
        'c_ctx': nrm(ks[3], (D_MODEL,), 1.0),
        'w_mod': nrm(ks[4], (L, D_MODEL, 6 * D_MODEL), 0.5 * fan),
        'b_mod': nrm(ks[5], (L, 6 * D_MODEL), 0.02),
        'w_in': nrm(ks[6], (L, D_MODEL, W_IN_COLS), fan),
        'b_if': if_base[None, :] + nrm(ks[7], (L, N_GATES_M), 0.1),
        'w_qk_conv': nrm(ks[8], (L, QK_CONV_W, 2 * D_QK), QK_CONV_W ** -0.5),
        'b_qk_conv': nrm(ks[9], (L, 2 * D_QK), 0.02),
        'mh_norm_g': 1.0 + nrm(ks[10], (L, D_V), 0.02),
        'w_dw': nrm(ks[11], (L, CONV_W, D_CONV), CONV_W ** -0.5),
        'b_dw': nrm(ks[12], (L, D_CONV), 0.02),
        'conv_ln_g': 1.0 + nrm(ks[13], (L, D_CONV), 0.02),
        'conv_ln_b': nrm(ks[14], (L, D_CONV), 0.02),
        'w_proj_m': nrm(ks[15], (L, D_V, D_MODEL), BETA * D_V ** -0.5),
        'w_proj_c': nrm(ks[16], (L, D_CONV, D_MODEL), BETA * D_CONV ** -0.5),
        'w_out': nrm(ks[17], (L, D_MODEL, D_MODEL), BETA * fan),
        'ln1_g': 1.0 + nrm(ks[18], (L, D_MODEL), 0.02),
        'ln1_b': nrm(ks[19], (L, D_MODEL), 0.02),
        'w_ffn_in': nrm(ks[20], (L, D_MODEL, 2 * D_FF), fan),
        'w_ffn_out': nrm(ks[21], (L, D_FF, D_MODEL), BETA * D_FF ** -0.5),
        'ln2_g': 1.0 + nrm(ks[22], (L, D_MODEL), 0.02),
        'ln2_b': nrm(ks[23], (L, D_MODEL), 0.02),
    }


def reference(x, c, ctx, c_ctx, w_mod, b_mod, w_in, b_if, w_qk_conv, b_qk_conv, mh_norm_g, w_dw, b_dw,
              conv_ln_g, conv_ln_b, w_proj_m, w_proj_c, w_out, ln1_g, ln1_b, w_ffn_in, w_ffn_out, ln2_g, ln2_b):
    B, T, _ = x.shape
    ROWS = T // GRID_W
    for l in range(DEPTH):
        last = l == DEPTH - 1
        sh1, sc1, g1, sh2, sc2, g2 = [m[:, None, :] for m in jnp.split(jax.nn.silu(c) @ w_mod[l] + b_mod[l], 6, axis=-1)]
        csh1, csc1, cg1, csh2, csc2, cg2 = jnp.split(jax.nn.silu(c_ctx) @ w_mod[l] + b_mod[l], 6, axis=-1)
        u = modulate(layer_norm(x), sh1, sc1)
        uc = modulate(layer_norm(ctx), csh1, csc1)
        qx, kx, vx, ox, gx, cvx, mx = mixer_inputs(u, w_in[l], b_if[l], w_qk_conv[l], b_qk_conv[l])
        qc, kc, vc, oc, gc, cvc, mc = mixer_inputs(uc, w_in[l], b_if[l], w_qk_conv[l], b_qk_conv[l])
        ix_f, fx_f, ix_b, fx_b = gate_logs(gx)
        ic_f, fc_f, ic_b, fc_b = gate_logs(gc)
        zero = zero_state(B)
        hc_f, st_f = mlstm_chunkwise(qc, kc, vc, ic_f, fc_f, zero)
        hc_b, st_b = mlstm_chunkwise(*[flip_t(a) for a in (qc, kc, vc, ic_b, fc_b)], zero)
        hx_f, _ = mlstm_chunkwise(qx, kx, vx, ix_f, fx_f, st_f)
        hx_b, _ = mlstm_chunkwise(*[flip_t(a) for a in (qx, kx, vx, ix_b, fx_b)], st_b)
        hx = hx_f + flip_t(hx_b)
        out_params = (mh_norm_g[l], w_dw[l], b_dw[l], conv_ln_g[l], conv_ln_b[l], w_proj_m[l], w_proj_c[l], w_out[l])
        yx = mixer_output(hx, ox, cvx, mx, ROWS, *out_params)
        x = layer_norm(ALPHA * x + g1 * yx, ln1_g[l], ln1_b[l])
        yf = swiglu(modulate(layer_norm(x), sh2, sc2), w_ffn_in[l], w_ffn_out[l])
        x = layer_norm(ALPHA * x + g2 * yf, ln2_g[l], ln2_b[l])
        if not last:
            hc = hc_f + flip_t(hc_b)
            yc = mixer_output(hc, oc, cvc, mc, None, *out_params)
            ctx = layer_norm(ALPHA * ctx + cg1 * yc, ln1_g[l], ln1_b[l])
            yfc = swiglu(modulate(layer_norm(ctx), csh2, csc2), w_ffn_in[l], w_ffn_out[l])
            ctx = layer_norm(ALPHA * ctx + cg2 * yfc, ln2_g[l], ln2_b[l])
    return x
```

```python
import contextlib
import numpy as np
import ml_dtypes
import concourse.bass as bass
import concourse.mybir as mybir
from concourse.bass_utils import run_bass_kernel_spmd

dt = mybir.dt
AF = mybir.ActivationFunctionType
ALU = mybir.AluOpType
AX = mybir.AxisListType
F32 = dt.float32
BF16 = dt.bfloat16

D = 2048
KC = 16
NOWN = 1024
NTILE = 18
W_IN_COLS = 14368
D_FF = 5632
EPS = 1e-5
ALPHA = 2.0 ** 0.25
QSCALE = 128.0 ** -0.5
NPF = 624
KIB = 1024


class Region:
    __slots__ = ("name", "w", "r", "excl")

    def __init__(self, name="", excl=False):
        self.name = name
        self.w = None
        self.r = {}
        self.excl = excl


class Buf:
    __slots__ = ("ap", "rg")

    def __init__(self, ap, rg=None, name=""):
        self.ap = ap
        self.rg = rg if rg is not None else Region(name)


class Sched:
    ENG = ("pe", "act", "dve", "pool", "sp")

    def __init__(self, nc, stack, n_sp=24, n_pool=8):
        self.nc = nc
        self.sem = {e: stack.enter_context(nc.semaphore("c_" + e)) for e in ("pe", "act", "dve", "pool")}
        self.cnt = {e: 0 for e in self.sem}
        self.dsem = [stack.enter_context(nc.semaphore(f"d{i}")) for i in range(n_sp + n_pool)]
        self.dcnt = [0] * (n_sp + n_pool)
        self.n_sp = n_sp
        self.n_pool = n_pool
        self.nx = {"sp": 0, "pool": 0}
        self.prog = {e: [] for e in self.ENG}
        self.waited = {e: {} for e in self.ENG}

    def _semobj(self, key):
        return self.sem[key] if isinstance(key, str) else self.dsem[key]

    def _deps(self, reads, writes, extra):
        d = {}

        def add(k, v):
            if d.get(k, 0) < v:
                d[k] = v

        for r in reads:
            if r.w is not None:
                add(*r.w)
        for w in writes:
            if w.w is not None:
                add(*w.w)
            for k, v in w.r.items():
                add(k, v)
        for t in extra:
            if t is not None:
                add(*t)
        return d

    def _waits(self, eng, d):
        wl = []
        wd = self.waited[eng]
        for k, v in d.items():
            if wd.get(k, 0) < v:
                wd[k] = v
                wl.append((k, v))
        return wl

    def _update(self, tok, reads, writes):
        for w in writes:
            w.w = tok
            w.r = {}
        for r in reads:
            if r.r.get(tok[0], 0) < tok[1]:
                r.r[tok[0]] = tok[1]

    def op(self, eng, fn, reads=(), writes=(), extra=()):
        reads = [b.rg if isinstance(b, Buf) else b for b in reads]
        writes = [b.rg if isinstance(b, Buf) else b for b in writes]
        writes = writes + [r for r in reads if r.excl]
        reads = [r for r in reads if not r.excl]
        d = self._deps(reads, writes, extra)
        wl = self._waits(eng, d)
        self.cnt[eng] += 1
        tok = (eng, self.cnt[eng])
        self.prog[eng].append((wl, fn, (eng, 1)))
        self._update(tok, reads, writes)
        return tok

    def dma(self, q, out, in_, reads=(), writes=(), extra=(), slow=False):
        reads = [b.rg if isinstance(b, Buf) else b for b in reads]
        writes = [b.rg if isinstance(b, Buf) else b for b in writes]
        if q == "sp":
            i = self.nx["sp"]
            self.nx["sp"] = (i + 1) % self.n_sp
        else:
            i = self.n_sp + self.nx["pool"]
            self.nx["pool"] = (self.nx["pool"] + 1) % self.n_pool
        d = self._deps(reads, writes, extra)
        if self.dcnt[i] > 0 and d.get(i, 0) < self.dcnt[i]:
            d[i] = self.dcnt[i]
        wl = self._waits(q, d)
        self.dcnt[i] += 16
        tok = (i, self.dcnt[i])

        def fn(e, out=out, in_=in_, slow=slow):
            if slow:
                return e.dma_start(out=out, in_=in_, allow_slow_non_contiguous=True)
            return e.dma_start(out=out, in_=in_)

        self.prog[q].append((wl, fn, (i, 16)))
        self._update(tok, reads, writes)
        return tok

    def fence(self):
        toks = {}
        for e in ("pe", "act", "dve"):
            if self.cnt[e]:
                toks[e] = self.cnt[e]
        for i in range(self.n_sp):
            if self.dcnt[i]:
                toks[i] = self.dcnt[i]
        for e in ("pe", "act", "dve", "sp"):
            wl = self._waits(e, dict(toks))
            if wl:
                self.prog[e].append((wl, None, None))

    def wait_final(self):
        toks = {}
        for i in range(self.n_sp):
            if self.dcnt[i]:
                toks[i] = self.dcnt[i]
        wl = self._waits("sp", toks)
        self.prog["sp"].append((wl, None, None))

    def emit(self):
        nc = self.nc
        with nc.Block() as block:
            def mk(name):
                def body(e):
                    for wl, fn, inc in self.prog[name]:
                        for k, v in wl:
                            e.wait_ge(self._semobj(k), v)
                        if fn is None:
                            continue
                        ins = fn(e)
                        ins.then_inc(self._semobj(inc[0]), inc[1])
                return body

            block.tensor(mk("pe"))
            block.scalar(mk("act"))
            block.vector(mk("dve"))
            block.gpsimd(mk("pool"))
            block.sync(mk("sp"))


class Builder:
    def __init__(self, dump=None, stop=None):
        self.dump = dump
        self.stop = stop

    def act(self, out, in_, func, reads, writes, scale=None, bias=None):
        def fn(e):
            kw = {}
            if scale is not None:
                kw["scale"] = scale
            if bias is not None:
                kw["bias"] = bias
            return e.activation(out=out, in_=in_, func=func, **kw)
        return self.S.op("act", fn, reads, writes)

    def tt(self, out, in0, in1, op, reads, writes, eng="dve"):
        return self.S.op(eng, lambda e: e.tensor_tensor(out=out, in0=in0, in1=in1, op=op), reads, writes)

    def ts(self, out, in0, s1, op0, reads, writes, s2=None, op1=None, eng="dve"):
        def fn(e):
            if op1 is None:
                return e.tensor_scalar(out=out, in0=in0, scalar1=s1, scalar2=None, op0=op0)
            return e.tensor_scalar(out=out, in0=in0, scalar1=s1, scalar2=s2, op0=op0, op1=op1)
        return self.S.op(eng, fn, reads, writes)

    def stt(self, out, in0, scalar, in1, op0, op1, reads, writes):
        return self.S.op("dve", lambda e: e.scalar_tensor_tensor(out=out, in0=in0, scalar=scalar, in1=in1, op0=op0, op1=op1),
                         reads, writes)

    def cp(self, out, in_, reads, writes, eng="dve"):
        if eng == "act":
            return self.act(out, in_, AF.Copy, reads, writes)
        return self.S.op(eng, lambda e: e.tensor_copy(out=out, in_=in_), reads, writes)

    def mm(self, mms, reads, writes):
        def fn(e):
            ins = None
            for (o, l, r, st, sp) in mms:
                ins = e.matmul(o, lhsT=l, rhs=r, start=st, stop=sp)
            return ins
        return self.S.op("pe", fn, reads, writes)

    def tr(self, trs, reads, writes):
        def fn(e):
            ins = None
            for (o, i, idn) in trs:
                ins = e.transpose(out=o, in_=i, identity=idn)
            return ins
        return self.S.op("pe", fn, reads, writes)

    def at(self, off_kib, shape, dtype, name=""):
        n = int(np.prod(shape))
        esz = 4 if dtype == F32 else 2
        w0 = int(round(off_kib * KIB)) // 4
        nw = (n * esz + 3) // 4
        assert w0 + nw <= self.arena_words, (name, off_kib, shape)
        ap = self.arena[:, w0:w0 + nw]
        if dtype != F32:
            ap = ap.bitcast(dtype)
        ap = ap[:, 0:n]
        if len(shape) == 2:
            ap = ap.rearrange("p (a b) -> p a b", b=shape[1])
        elif len(shape) == 3:
            ap = ap.rearrange("p (a b c) -> p a b c", b=shape[1], c=shape[2])
        return Buf(ap, name=name)

    def bank(self, i, n=1):
        return self.psum[:, i * 512:(i + n) * 512]

    def ws_plan(self, W, r0, kcn, c0, ncols):
        self.wplan.append((W, r0, kcn, c0, ncols))

    def ws_get(self):
        i = self.wcons
        self.wcons += 1
        while self.wissued < len(self.wplan) and self.wissued < i + self.nslot - 1:
            j = self.wissued
            W, r0, kcn, c0, ncols = self.wplan[j]
            sl = j % self.nslot
            dst = self.ring[:, sl, 0:kcn * ncols].rearrange("p (k n) -> p k n", n=ncols)
            src = W[r0:r0 + kcn * 128, c0:c0 + ncols].rearrange("(k p) n -> p k n", p=128)
            self.S.dma("pool", dst, src, writes=[self.ringR[sl]])
            self.wissued += 1
        W, r0, kcn, c0, ncols = self.wplan[i]
        sl = i % self.nslot
        return self.ring[:, sl, 0:kcn * ncols].rearrange("p (k n) -> p k n", n=ncols), self.ringR[sl]

    @staticmethod
    def interleave(gens, width, admit_every):
        active = []
        it = iter(gens)
        more = True
        since = admit_every
        while True:
            if more and len(active) < width and (since >= admit_every or not active):
                try:
                    active.append(next(it))
                    since = 0
                except StopIteration:
                    more = False
            if not active:
                if not more:
                    break
                continue
            for g in list(active):
                try:
                    next(g)
                except StopIteration:
                    active.remove(g)
            since += 1

    def ln_stats_g(self, src, src_rg, parts=4, width=512):
        i = self.lni
        self.lni = (i + 1) % len(self.lnst)
        st = self.lnst[i]
        R = self.lnR[i]
        S = self.S
        stats = st[:, 0:parts * 6].rearrange("p (a b) -> p a b", b=6)
        mv = st[:, 48:50]
        rs = st[:, 50:51]
        nmr = st[:, 51:52]
        for a in range(parts):
            S.op("dve", lambda e, a=a: e.bn_stats(out=stats[:, a, :], in_=src[:, a * width:(a + 1) * width]), [src_rg], [R])
            yield
        S.op("dve", lambda e: e.bn_aggr(out=mv, in_=st[:, 0:parts * 6]), [R], [R])
        yield
        yield from self.rstd_chain_g(mv[:, 1:2], mv[:, 0:1], rs, nmr, R)
        return rs, nmr, R

    def rstd_chain_g(self, var, mean, rs, nmr, R):
        self.act(rs, var, AF.Ln, [R], [R], bias=EPS)
        yield
        self.act(rs, rs, AF.Exp, [R], [R], scale=-0.5)
        yield
        self.stt(nmr, mean, -1.0, rs, ALU.mult, ALU.mult, [R], [R])
        yield

    def ln_stats(self, src, src_rg, parts=4, width=512):
        i = self.lni
        self.lni = (i + 1) % 4
        st = self.lnst[i]
        R = self.lnR[i]
        S = self.S
        stats = st[:, 0:parts * 6].rearrange("p (a b) -> p a b", b=6)
        mv = st[:, 48:50]
        rs = st[:, 50:51]
        nmr = st[:, 51:52]
        for a in range(parts):
            S.op("dve", lambda e, a=a: e.bn_stats(out=stats[:, a, :], in_=src[:, a * width:(a + 1) * width]), [src_rg], [R])
        S.op("dve", lambda e: e.bn_aggr(out=mv, in_=st[:, 0:parts * 6]), [R], [R])
        self.rstd_chain(mv[:, 1:2], mv[:, 0:1], rs, nmr, R)
        return rs, nmr, R

    def rstd_chain(self, var, mean, rs, nmr, R):
        for _ in self.rstd_chain_g(var, mean, rs, nmr, R):
            pass

    def build(self):
        nc = bass.Bass("TRN2", target_bir_lowering=False)
        self.nc = nc
        di = lambda n, sh, d=F32: nc.dram_tensor(n, sh, d, kind="ExternalInput").ap()
        ds = lambda n, sh, d=F32: nc.dram_tensor(n, sh, d, kind="Internal").ap()
        xin = di("xin", [NTILE * 128, D])
        cvec = di("cvec", [128, 32])
        w_mod = di("w_mod", [D, 6 * D])
        bmod = di("bmod", [1, 6 * D])
        w_in = di("w_in", [D, W_IN_COLS])
        w_g = di("w_g", [D, 32])
        bif = di("bif", [1, 32])
        pfm_d = di("pfm", [128, NPF])
        w_pm = di("w_pm", [D, D])
        w_pc = di("w_pc", [D, D])
        w_o = di("w_o", [D, D])
        lnrow = di("lnrow", [4, D])
        w_f1 = di("w_f1", [D, 2 * D_FF])
        w_f2 = di("w_f2", [D_FF, D])
        cf_d = di("cf32", [128, 4 * 128])
        idb_d = di("identb", [128, 128], BF16)
        out = nc.dram_tensor("out", [NOWN, D], F32, kind="ExternalOutput").ap()
        dumps = {}
        if self.dump:
            for n, sh, d in self.dump:
                dumps[n] = nc.dram_tensor("dump_" + n, sh, d, kind="ExternalOutput").ap()

        def scratch(n, sh, d=F32):
            return dumps[n] if n in dumps else ds(n, sh, d)
        mod_d = scratch("mod", [2, 6 * D])
        uT_d = scratch("uT", [128, KC, NTILE * 128], BF16)
        HF_d = scratch("HF", [NOWN, D])
        HB_d = scratch("HB", [NOWN, D])
        HN_d = scratch("HN", [NOWN, D])
        ZM_d = scratch("ZM", [D, NOWN])
        R1_d = scratch("R1", [NOWN, D])
        X1_d = scratch("X1", [NOWN, D])
        R2_d = scratch("R2", [NOWN, D])
        dbg_d = dumps.get("dbg")
        Rd = {n: Region(n) for n in ["mod", "uT", "HF", "HB", "HN", "ZM", "R1", "X1", "R2", "out", "dbg"]}

        with contextlib.ExitStack() as st:
            S = Sched(nc, st)
            self.S = S
            sb = lambda n, sh, d: st.enter_context(nc.sbuf_tensor(n, sh, d))
            self.nslot = 3
            ring = sb("ring", [128, self.nslot, 8192], BF16)
            self.ring = ring
            self.ringR = [Region(f"ring{i}") for i in range(self.nslot)]
            self.wplan, self.wcons, self.wissued = [], 0, 0
            ARENA_KIB = 128
            self.arena_words = int(ARENA_KIB * KIB) // 4
            self.arena = sb("arena", [128, self.arena_words], F32)
            self.psum = st.enter_context(nc.psum_tensor("psum", [128, 4096], F32))
            PR = [Region(f"pb{i}", excl=True) for i in range(8)]
            cf = sb("cf", [128, 4, 128], F32)
            ident, trif, trib, ones = cf[:, 0, :], cf[:, 1, :], cf[:, 2, :], cf[:, 3, :]
            identb = sb("identb_s", [128, 128], BF16)
            pfm = sb("pfm_s", [128, NPF], F32)
            wg = sb("wg", [128, KC, 32], BF16)
            bif_bc = sb("bif_bc", [128, 32], F32)
            dq = sb("dq", [128, 16, 3, 128], BF16)
            modT = sb("modT", [128, 96, 2], F32)
            msc = sb("msc", [128, 6, 16], F32)
            lnst_t = sb("lnst", [128, 8, 64], F32)
            self.lnst = [lnst_t[:, i, :] for i in range(8)]
            self.lnR = [Region(f"lnst{i}") for i in range(8)]
            self.lni = 0
            gex = sb("gex", [128, 5, NTILE, 16], F32)
            gsm = sb("gsm", [128, 7, 96], F32)
            mp = sb("mp", [128, 16], F32)
            halo = sb("halo", [128, KC, 2], BF16)
            CONST, GEX, MSC, HALO, DQ = (Region(n) for n in ["const", "gex", "msc", "halo", "dq"])
            GSM = [Region(f"gsm{i}") for i in range(7)]
            A = self

            S.dma("sp", cf[:].rearrange("p a b -> p (a b)"), cf_d, writes=[CONST])
            S.dma("sp", identb[:], idb_d, writes=[CONST])
            S.dma("sp", pfm[:], pfm_d, writes=[CONST])
            S.dma("sp", bif_bc[:], bif.broadcast_to([128, 32]), writes=[CONST])
            S.dma("pool", wg[:], w_g.rearrange("(k p) n -> p k n", p=128), writes=[CONST])
            cv = A.at(118, [32], F32, "cv")
            csb = A.at(118.25, [KC, 2], BF16, "csb")
            S.dma("sp", cv.ap, cvec, writes=[cv])
            A.act(csb.ap, cv.ap.rearrange("p (k j) -> p k j", j=2), AF.Silu, [cv], [csb])
            for c in range(16):
                for j in range(3):
                    A.ts(dq[:, c, j, :], ident, pfm[:, j * 16 + c:j * 16 + c + 1], ALU.mult, [CONST], [DQ])
            for cb in range(24):
                A.ws_plan(w_mod, 0, 16, cb * 512, 512)
            for blk in range(2):
                A.ws_plan(w_in, 0, 16, 1024 + blk * 512, 512)
            for blk in range(4):
                A.ws_plan(w_in, 0, 16, 2048 + blk * 512, 512)
            for blk in range(4):
                A.ws_plan(w_in, 0, 16, blk * 512, 512)
            for blk in range(4):
                A.ws_plan(w_in, 0, 16, 2048 + blk * 512, 512)
            for blk in range(4):
                A.ws_plan(w_in, 0, 16, 4096 + blk * 512, 512)
            for blk in range(4):
                A.ws_plan(w_pm, 0, 16, blk * 512, 512)
                A.ws_plan(w_in, 0, 16, 10272 + blk * 512, 512)
            for blk in range(8):
                A.ws_plan(w_in, 0, 16, 6176 + (blk // 2) * 512 + (blk % 2) * 2048, 512)
            for blk in range(4):
                A.ws_plan(w_pc, 0, 16, blk * 512, 512)
                A.ws_plan(w_in, 0, 16, 12320 + blk * 512, 512)
            for blk in range(4):
                A.ws_plan(w_o, 0, 16, blk * 512, 512)
            for hh in range(2):
                for r in range(6):
                    ncol = 512 if r < 5 else 256
                    c0 = hh * 2816 + r * 512
                    A.ws_plan(w_f1, 0, 16, c0, ncol)
                    A.ws_plan(w_f1, 0, 16, D_FF + c0, ncol)
                for blk in range(4):
                    for pc in range(2):
                        A.ws_plan(w_f2, hh * 2816 + pc * 1408, 11, blk * 512, 512)

            MODT, MSC2 = Region("modT"), Region("msc2")
            RdMod = [Region(f"mod{i}") for i in range(24)]
            stg = [A.at(119 + 2 * i, [512], F32, f"stg{i}") for i in range(2)]
            bmb = [A.at(123 + 2 * i, [512], F32, f"bmb{i}") for i in range(2)]

            def mod_block(cb):
                wsl, wr = A.ws_get()
                pb = A.bank(6)
                A.mm([(pb[0:2, :], csb.ap[:, kc, :], wsl[:, kc, :], kc == 0, kc == 15) for kc in range(16)],
                     [csb, wr], [PR[6]])
                yield
                s_, bm_ = stg[cb % 2], bmb[cb % 2]
                S.dma("sp", bm_.ap[0:2, :], bmod[0:1, cb * 512:(cb + 1) * 512].broadcast_to([2, 512]), writes=[bm_])
                A.tt(s_.ap[0:2, :], pb[0:2, :], bm_.ap[0:2, :], ALU.add, [PR[6], bm_], [s_])
                yield
                S.dma("sp", mod_d[:, cb * 512:(cb + 1) * 512], s_.ap[0:2, :], reads=[s_], writes=[Rd["mod"], RdMod[cb]])
                pt = A.bank(7)[:, 0:8].rearrange("p (a b) -> p a b", b=2)
                A.tr([(pt[:, j, :], s_.ap[0:2, j * 128:(j + 1) * 128], ident[0:2, 0:2]) for j in range(4)], [s_, CONST], [PR[7]])
                yield
                A.cp(modT[:, cb * 4:(cb + 1) * 4, :], pt, [PR[7]], [MODT])
                yield

            for cb in range(8):
                for _ in mod_block(cb):
                    pass
            A.cp(msc[:, 0, :], modT[:, 0:16, 0], [MODT], [MSC])
            A.ts(msc[:, 1, :], modT[:, 16:32, 0], 1.0, ALU.add, [MODT], [MSC])
            A.cp(msc[:, 2, :], modT[:, 0:16, 1], [MODT], [MSC])
            A.ts(msc[:, 3, :], modT[:, 16:32, 1], 1.0, ALU.add, [MODT], [MSC])

            def modgen():
                for cb in range(8, 24):
                    yield from mod_block(cb)
                    if cb == 19:
                        A.cp(msc[:, 4, :], modT[:, 48:64, 0], [MODT], [MSC2])
                        A.ts(msc[:, 5, :], modT[:, 64:80, 0], 1.0, ALU.add, [MODT], [MSC2])
                    for _ in range(5):
                        yield

            gatb = A.at(72, [12, NTILE, 16], F32, "gat")
            gat, GAT = gatb.ap, gatb.rg
            S.op("dve", lambda e: e.memset(gat.rearrange("p a b c -> p (a b c)"), 0.0), [], [GAT])
            NB_A = 6
            xt = [A.at(0 + 8 * i, [D], F32, f"xt{i}") for i in range(NB_A)]
            yn = xt
            uTt = [A.at(48 + 4 * i, [KC, 128], BF16, f"uTt{i}") for i in range(NB_A)]
            gsmv = [gsm[:, i, :] for i in range(NB_A)]
            GATt = [Region(f"gat{t}") for t in range(NTILE)]
            for r_ in GATt:
                r_.w = GAT.w

            GALL = A.at(86, [NTILE, 32], F32, "GALL")
            GSA = A.at(88.5, [NTILE, 32], F32, "GSA")
            ELA = A.at(91, [NTILE, 16], F32, "ELA")
            LLA = A.at(92.25, [NTILE, 16], F32, "LLA")
            AMX = A.at(93.5, [4], F32, "AMX")
            DGS = A.at(94, [3, 96], F32, "DGS")
            GALLt = [Region(f"gall{t}") for t in range(NTILE)]

            def tileA(t):
                x_b, y_b, u_b = xt[t % NB_A], yn[t % NB_A], uTt[t % NB_A]
                S.dma("sp", x_b.ap, xin[t * 128:(t + 1) * 128, :], writes=[x_b])
                yield
                rs, nmr, R = yield from A.ln_stats_g(x_b.ap, x_b.rg)
                A.act(y_b.ap, x_b.ap, AF.Identity, [x_b, R], [y_b], scale=rs, bias=nmr)
                yield
                isctx = t < 2
                shv = msc[:, 2 if isctx else 0, :]
                scv = msc[:, 3 if isctx else 1, :]
                for g in range(4):
                    bi = (t * 4 + g) % 4
                    pb = A.bank(bi)
                    A.tr([(pb[:, j * 128:(j + 1) * 128], y_b.ap[:, (g * 4 + j) * 128:(g * 4 + j + 1) * 128], ident) for j in range(4)],
                         [y_b, CONST], [PR[bi]])
                    yield
                    for j in range(4):
                        kc = g * 4 + j
                        if (j + g) % 2 == 0:
                            A.ts(u_b.ap[:, kc, :], pb[:, j * 128:(j + 1) * 128], scv[:, kc:kc + 1], ALU.mult,
                                 [PR[bi], MSC], [u_b], s2=shv[:, kc:kc + 1], op1=ALU.add)
                        else:
                            A.act(u_b.ap[:, kc, :], pb[:, j * 128:(j + 1) * 128], AF.Identity, [PR[bi], MSC], [u_b],
                                  scale=scv[:, kc:kc + 1], bias=shv[:, kc:kc + 1])
                        yield
                S.dma("sp", uT_d[:, :, t * 128:(t + 1) * 128], u_b.ap, reads=[u_b], writes=[Rd["uT"]])
                if t == 17:
                    A.cp(halo[:, :, 0:1], u_b.ap[:, :, 127:128], [u_b], [HALO])
                if t == 2:
                    A.cp(halo[:, :, 1:2], u_b.ap[:, :, 0:1], [u_b], [HALO])
                pgb = 4 + t % 2
                pg = A.bank(pgb)
                A.mm([(pg[:, 0:32], u_b.ap[:, kc, :], wg[:, kc, :], kc == 0, kc == 15) for kc in range(16)],
                     [u_b, CONST], [PR[pgb]])
                yield
                A.cp(GALL.ap[:, t, :], pg[:, 0:32], [PR[pgb]], [GALLt[t]], eng="act" if t % 2 else "dve")
                yield

            A.interleave([modgen()] + [tileA(t) for t in range(NTILE)], 6, 5)
            NG = NTILE * 16
            A.tt(GSA.ap, GALL.ap, bif_bc[:].unsqueeze(1).broadcast_to([128, NTILE, 32]), ALU.add, GALLt + [CONST], [GSA])
            GS5 = GSA.ap.rearrange("p t (d j h) -> p t d j h", d=2, j=2)
            A.act(ELA.ap.rearrange("p t (d h) -> p t d h", d=2), GS5[:, :, :, 1, :], AF.Exp, [GSA], [ELA], scale=-1.0)
            A.act(LLA.ap, ELA.ap, AF.Ln, [ELA], [LLA], bias=1.0)
            pB = A.bank(0)
            pT = A.bank(1)
            A.mm([(pB[:, 0:144], trif, LLA.ap[:, :, 0:8], True, True),
                  (pB[:, 144:288], trib, LLA.ap[:, :, 8:16], True, True)], [LLA, CONST], [PR[0]])
            A.mm([(pT[:, 0:NG], ones, LLA.ap, True, True)], [LLA, CONST], [PR[1]])
            for d in range(2):
                pBd = pB[:, d * 144:(d + 1) * 144].rearrange("p (t h) -> p t h", h=8)
                A.tt(gat[:, 0, :, d * 8:(d + 1) * 8], GS5[:, :, d, 0, :], pBd, ALU.add, [GSA, PR[0]], [GAT])
                A.cp(gat[:, 1, :, d * 8:(d + 1) * 8], pBd, [PR[0]], [GAT], eng="act")
            A.cp(gat[:, 2, :, :], pT[:, 0:NG].rearrange("p (t h) -> p t h", h=16), [PR[1]], [GAT], eng="act")
            pX = A.bank(2)
            gA = gat[:, 0, :, :].rearrange("p t h -> p (t h)")
            A.tr([(pX[0:96, k * 128:(k + 1) * 128], gA[:, k * 96:(k + 1) * 96], ident) for k in range(3)], [GAT, CONST], [PR[2]])
            S.op("dve", lambda e: e.reduce_max(out=AMX.ap[0:96, 0:3], in_=pX[0:96, 0:384].rearrange("p (k t) -> p k t", t=128), axis=AX.X),
                 [PR[2]], [AMX])
            for k in range(3):
                A.ts(DGS.ap[0:96, k, :], ident[0:96, 0:96], AMX.ap[0:96, k:k + 1], ALU.mult, [AMX, CONST], [DGS])
            pC = A.bank(3)
            A.mm([(pC[:, k * 96:(k + 1) * 96], ones[0:96, :], DGS.ap[0:96, k, :], True, True) for k in range(3)], [DGS, CONST], [PR[3]])
            A.cp(gat[:, 3, :, :], pC[:, 0:NG].rearrange("p (t h) -> p t h", h=16), [PR[3]], [GAT], eng="act")

            if self.stop == "A3":
                S.dma("sp", dbg_d[:, 0:3456], gat.rearrange("p a b c -> p (a b c)"), reads=[GAT], writes=[Rd["dbg"]])
                return self.finish(nc, S)
            g_A, g_B, g_TOT, g_AMAX, g_SUF, g_TMP, g_MT, g_MNX, g_OFFK, g_LAMN, g_OFFP = range(11)
            seqs = {0: ([0, 1], list(range(10, 18))), 1: ([1, 0] + list(range(9, 1, -1)), list(range(17, 9, -1)))}
            for d in (0, 1):
                cd = slice(d * 8, d * 8 + 8)
                pre, own = seqs[d]
                prev = None
                for c in reversed(pre):
                    if prev is None:
                        A.ts(gat[:, g_SUF, c, cd], gat[:, g_TOT, c, cd], -1.0, ALU.mult, [GAT], [GAT])
                    else:
                        A.tt(gat[:, g_SUF, c, cd], gat[:, g_SUF, prev, cd], gat[:, g_TOT, c, cd], ALU.subtract, [GAT], [GAT])
                    prev = c
                mcur = mp[:, cd]
                for i, c in enumerate(pre):
                    A.tt(gat[:, g_TMP, c, cd], gat[:, g_AMAX, c, cd], gat[:, g_SUF, c, cd], ALU.add, [GAT], [GAT])
                    A.tt(mcur, gat[:, g_SUF, pre[0], cd] if i == 0 else mcur, gat[:, g_TMP, c, cd], ALU.max, [GAT], [GAT])
                mprev = mcur
                for c in own:
                    A.tt(gat[:, g_MT, c, cd], mprev, gat[:, g_AMAX, c, cd], ALU.max, [GAT], [GAT])
                    A.tt(gat[:, g_MNX, c, cd], gat[:, g_MT, c, cd], gat[:, g_TOT, c, cd], ALU.subtract, [GAT], [GAT])
                    mprev = gat[:, g_MNX, c, cd]
                for i, c in enumerate(own[:-1]):
                    nxt = own[i + 1]
                    A.stt(gat[:, g_OFFK, c, cd], gat[:, g_TOT, c, cd], -1.0, gat[:, g_MT, nxt, cd], ALU.mult, ALU.subtract, [GAT], [GAT])
                    A.tt(gat[:, g_LAMN, c, cd], gat[:, g_MNX, c, cd], gat[:, g_MT, nxt, cd], ALU.subtract, [GAT], [GAT])
                for c in pre:
                    A.tt(gat[:, g_OFFP, c, cd], gat[:, g_SUF, c, cd], gat[:, g_MT, own[0], cd], ALU.subtract, [GAT], [GAT])
            if self.stop == "A4":
                S.dma("sp", dbg_d[:, 0:3456], gat.rearrange("p a b c -> p (a b c)"), reads=[GAT], writes=[Rd["dbg"]])
                return self.finish(nc, S)
            fl = lambda ap: ap.rearrange("p a b -> p (a b)")
            own_s = slice(10, 18)
            A.tt(fl(gex[:, 0, own_s, :]), fl(gat[:, g_A, own_s, :]), fl(gat[:, g_MT, own_s, :]), ALU.subtract, [GAT], [GEX])
            A.tt(fl(gex[:, 1, own_s, :]), fl(gat[:, g_A, own_s, :]), fl(gat[:, g_OFFK, own_s, :]), ALU.add, [GAT], [GEX])
            A.tt(fl(gex[:, 2, own_s, :]), fl(gat[:, g_B, own_s, :]), fl(gat[:, g_MT, own_s, :]), ALU.subtract, [GAT], [GEX])
            A.cp(fl(gex[:, 3, own_s, :]), fl(gat[:, g_LAMN, own_s, :]), [GAT], [GEX])
            A.tt(fl(gex[:, 4, 0:10, :]), fl(gat[:, g_A, 0:10, :]), fl(gat[:, g_OFFP, 0:10, :]), ALU.add, [GAT], [GEX])
            for i4 in range(4):
                A.act(fl(gex[:, i4, own_s, :]), fl(gex[:, i4, own_s, :]), AF.Exp, [GEX], [GEX])
            A.act(fl(gex[:, 4, 0:10, :]), fl(gex[:, 4, 0:10, :]), AF.Exp, [GEX], [GEX])
            GS_, GK_, FL_, LAM_, GP_ = (gex[:, i, :, :] for i in range(5))
            if dbg_d is not None:
                S.dma("sp", dbg_d[:, 0:3456], gat.rearrange("p a b c -> p (a b c)"), reads=[GAT], writes=[Rd["dbg"]])
                S.dma("sp", dbg_d[:, 3456:4896], gex[:].rearrange("p a b c -> p (a b c)"), reads=[GEX], writes=[Rd["dbg"]])
            S.fence()
            if self.stop == "A":
                return self.finish(nc, S)

            St = A.at(103.25, [2, 8, 257], F32, "S")
            Sb_ = A.at(119.3125, [2, 8, 257], BF16, "Sb")
            SR = [[Region(f"S{d}{h}") for h in range(8)] for d in range(2)]
            SbR = [[Region(f"Sb{d}{h}") for h in range(8)] for d in range(2)]
            NB = 1281
            uToc = A.at(0, [KC, NB], BF16, "uToc")
            kToc = A.at(40.25, [8, 1280], BF16, "kToc")
            v1oc = A.at(60.25, [10, 8, 257], BF16, "v1oc")
            kpre = A.at(100.5, [1284], BF16, "kpre")
            ktp = Buf(A.at(0, [12, 8, 128], BF16).ap, uToc.rg)
            S.dma("sp", uToc.ap[:, :, 0:256], uT_d[:, :, 0:256], reads=[Rd["uT"]], writes=[uToc])
            S.dma("sp", uToc.ap[:, :, 257:1281], uT_d[:, :, 256:1280], reads=[Rd["uT"]], writes=[uToc])
            A.cp(uToc.ap[:, :, 256:257], halo[:, :, 0:1], [HALO], [uToc])
            S.op("dve", lambda e: e.memset(kpre.ap, 0.0), [], [kpre])
            ntl = [(0, 512), (512, 1024), (1024, NB)]
            if self.stop == "B1":
                return self.finish(nc, S)
            kpre2 = [kpre, A.at(61, [1284], BF16, "kpre2")]
            S.op("dve", lambda e: e.memset(kpre2[1].ap, 0.0), [], [kpre2[1]])
            wcurB = [None, None]

            def projB(h):
                if h % 4 == 0:
                    wcurB[0], wcurB[1] = A.ws_get()
                wsl, wr = wcurB
                sub = h % 4
                pb0 = 0 if h % 2 == 0 else 3
                pp = A.bank(pb0, 3)
                kp = kpre2[h % 2]
                mms = []
                for kc in range(16):
                    for (n0, n1) in ntl:
                        mms.append((pp[:, n0:n1], wsl[:, kc, sub * 128:(sub + 1) * 128], uToc.ap[:, kc, n0:n1], kc == 0, kc == 15))
                A.mm(mms, [uToc, wr], PR[pb0:pb0 + 3])
                A.cp(kp.ap[:, 1:257], pp[:, 0:256], PR[pb0:pb0 + 1], [kp], eng="act")
                A.cp(kp.ap[:, 258:1283], pp[:, 256:1281], PR[pb0:pb0 + 3], [kp])

            def convB(h):
                c = 8 + h
                kp = kpre2[h % 2]
                p6, p7 = A.bank(6), A.bank(7)
                mms = []
                for j in range(3):
                    mms.append((p6[:, 0:256], dq[:, c, j, :], kp.ap[:, j:j + 256], j == 0, j == 2))
                    mms.append((p7, dq[:, c, j, :], kp.ap[:, 258 + j:258 + j + 512], j == 0, j == 2))
                A.mm(mms, [kp, DQ], PR[6:8])
                A.act(kToc.ap[:, h, 0:256], p6[:, 0:256], AF.Silu, [PR[6], CONST], [kToc], bias=pfm[:, 48 + c:49 + c])
                A.act(kToc.ap[:, h, 256:768], p7, AF.Silu, [PR[7], CONST], [kToc], bias=pfm[:, 48 + c:49 + c])
                A.mm([(p6, dq[:, c, j, :], kp.ap[:, 770 + j:770 + j + 512], j == 0, j == 2) for j in range(3)], [kp, DQ], [PR[6]])
                A.act(kToc.ap[:, h, 768:1280], p6, AF.Silu, [PR[6], CONST], [kToc], bias=pfm[:, 48 + c:49 + c])

            projB(0)
            for h in range(8):
                if h + 1 < 8:
                    projB(h + 1)
                convB(h)
            S.op("dve", lambda e: e.memset(v1oc.ap.rearrange("p a b c -> p (a b c)"), 1.0), [], [v1oc, kpre2[1]])
            if self.stop == "B3":
                return self.finish(nc, S)
            tcol = lambda t: t * 128 if t < 2 else 257 + (t - 2) * 128
            for blk in range(4):
                wsl, wr = A.ws_get()
                for t in range(10):
                    b = 6 + t % 2
                    pb = A.bank(b)
                    A.mm([(pb, uToc.ap[:, kc, tcol(t):tcol(t) + 128], wsl[:, kc, :], kc == 0, kc == 15) for kc in range(16)],
                         [uToc, wr], [PR[b]])
                    A.cp(v1oc.ap[:, t, 2 * blk:2 * blk + 2, 0:256], pb.rearrange("p (h e) -> p h e", e=256), [PR[b]], [v1oc],
                         eng="act" if t % 2 else "dve")
            if self.stop == "B4":
                return self.finish(nc, S)
            kidx = {}
            n = 0
            for t in range(10):
                for d in ((0, 1) if t < 2 else (1,)):
                    kidx[(t, d)] = n
                    n += 1
            for t in range(10):
                b = t % 2
                pbb = A.bank(b).bitcast(BF16)
                kc0 = t * 128
                A.tr([(pbb[:, h * 128:(h + 1) * 128], kToc.ap[:, h, kc0:kc0 + 128], identb[:]) for h in range(8)],
                     [kToc, CONST], [PR[b]])
                for d in ((0, 1) if t < 2 else (1,)):
                    A.tt(ktp.ap[:, kidx[(t, d)], :, :], pbb.rearrange("p (h k) -> p h k", k=128),
                         GP_[:, t, d * 8:d * 8 + 8].unsqueeze(2).broadcast_to([128, 8, 128]), ALU.mult, [PR[b], GEX], [ktp])
            if self.stop == "B5":
                return self.finish(nc, S)
            for d in (0, 1):
                pre = seqs[d][0]
                for h in range(8):
                    b = 2 + h % 2
                    pb = A.bank(b)
                    A.mm([(pb[:, 0:257], ktp.ap[:, kidx[(t, d)], h, :], v1oc.ap[:, t, h, :], i == 0, i == len(pre) - 1)
                          for i, t in enumerate(pre)], [ktp, v1oc], [PR[b]])
                    if self.stop == "B6":
                        return self.finish(nc, S)
                    A.cp(St.ap[:, d, h, :], pb[:, 0:257], [PR[b]], [SR[d][h]], eng="act")
                    if self.stop == "B7":
                        return self.finish(nc, S)
                    A.cp(Sb_.ap[:, d, h, :], pb[:, 0:257], [PR[b]], [SbR[d][h]])
                    if self.stop == "B8":
                        return self.finish(nc, S)
            if dbg_d is not None:
                S.dma("sp", dbg_d[:, 4896:4896 + 4112], St.ap.rearrange("p a b c -> p (a b c)"),
                      reads=[r for rr in SR for r in rr], writes=[Rd["dbg"]])
            S.fence()
            if self.stop == "B":
                return self.finish(nc, S)

            qT = A.at(0, [8, NOWN], BF16, "qT")
            kT = A.at(16, [8, NOWN], BF16, "kT")
            v1 = A.at(32, [8, 8, 257], BF16, "v1")
            uTo = A.at(64.5, [KC, 1025], BF16, "uTo")
            pre_ = A.at(96.75, [1026], BF16, "pre")
            tmpq = A.at(99.0, [512], F32, "tmpq")
            S.dma("sp", uTo.ap[:, :, 0:1024], uT_d[:, :, 1280:2304], reads=[Rd["uT"]], writes=[uTo])
            A.cp(uTo.ap[:, :, 1024:1025], halo[:, :, 1:2], [HALO], [uTo])
            S.op("dve", lambda e: e.memset(pre_.ap, 0.0), [], [pre_])
            S.op("dve", lambda e: e.memset(v1.ap.rearrange("p a b c -> p (a b c)"), 1.0), [], [v1])
            ntl = [(0, 512), (512, 1024), (1024, 1025)]
            pre2 = [pre_, A.at(101, [1026], BF16, "pre2")]
            S.op("dve", lambda e: e.memset(pre2[1].ap, 0.0), [], [pre2[1]])
            wcur = [None, None]

            def projC(c):
                if c % 4 == 0:
                    wcur[0], wcur[1] = A.ws_get()
                wsl, wr = wcur
                sub = c % 4
                pb0 = 0 if c % 2 == 0 else 5
                pp = A.bank(pb0, 3)
                mms = []
                for kc in range(16):
                    for (n0, n1) in ntl:
                        mms.append((pp[:, n0:n1], wsl[:, kc, sub * 128:(sub + 1) * 128], uTo.ap[:, kc, n0:n1], kc == 0, kc == 15))
                A.mm(mms, [uTo, wr], PR[pb0:pb0 + 3])
                A.cp(pre2[c % 2].ap[:, 1:1026], pp[:, 0:1025], PR[pb0:pb0 + 3], [pre2[c % 2]])

            def convC(c):
                p_ = pre2[c % 2]
                pc_ = A.bank(3, 2)
                mms = []
                for j in range(3):
                    for hf in range(2):
                        mms.append((pc_[:, hf * 512:(hf + 1) * 512], dq[:, c, j, :], p_.ap[:, hf * 512 + j:hf * 512 + j + 512], j == 0, j == 2))
                A.mm(mms, [p_, DQ], PR[3:5])
                if c < 8:
                    for hf in range(2):
                        A.act(tmpq.ap, pc_[:, hf * 512:(hf + 1) * 512], AF.Silu, [PR[3 + hf], CONST], [tmpq], bias=pfm[:, 48 + c:49 + c])
                        A.ts(qT.ap[:, c, hf * 512:(hf + 1) * 512], tmpq.ap, QSCALE, ALU.mult, [tmpq], [qT])
                else:
                    A.act(kT.ap[:, c - 8, :], pc_, AF.Silu, PR[3:5] + [CONST], [kT], bias=pfm[:, 48 + c:49 + c])

            projC(0)
            for c in range(16):
                if c + 1 < 16:
                    projC(c + 1)
                convC(c)
            for blk in range(4):
                wsl, wr = A.ws_get()
                for t in range(8):
                    b = 6 + t % 2
                    pb = A.bank(b)
                    A.mm([(pb, uTo.ap[:, kc, t * 128:(t + 1) * 128], wsl[:, kc, :], kc == 0, kc == 15) for kc in range(16)],
                         [uTo, wr], [PR[b]])
                    A.cp(v1.ap[:, t, 2 * blk:2 * blk + 2, 0:256], pb.rearrange("p (h e) -> p h e", e=256), [PR[b]], [v1],
                         eng="act" if t % 2 else "dve")
            S.fence()

            ktl = [[A.at(65 + 4 * d + 2 * i, [8, 128], BF16, f"ktl{d}{i}") for i in range(2)] for d in range(2)]
            HT = [A.at(73 + 8 * d, [8, 256], F32, f"HT{d}") for d in range(2)]
            sTb = [[A.at(89 + 0.25 * (2 * d + i), [128], BF16, f"sTb{d}{i}") for i in range(2)] for d in range(2)]
            dn = [[A.at(90 + 0.0625 * (2 * d + i), [2], F32, f"dn{d}{i}") for i in range(2)] for d in range(2)]

            def scan_dir(d):
                b0 = 4 * d
                for i in range(8):
                    ti = i if d == 0 else 7 - i
                    tc_ = slice(ti * 128, (ti + 1) * 128)
                    gt = 10 + ti
                    mask = trif if d == 0 else trib
                    kt_ = ktl[d][i % 2]
                    if i < 7:
                        b = b0 + 3
                        pbb = A.bank(b).bitcast(BF16)
                        A.tr([(pbb[:, h * 128:(h + 1) * 128], kT.ap[:, h, tc_], identb[:]) for h in range(8)], [kT, CONST], [PR[b]])
                        yield
                        A.tt(kt_.ap, pbb.rearrange("p (h k) -> p h k", k=128),
                             GK_[:, gt, d * 8:d * 8 + 8].unsqueeze(2).broadcast_to([128, 8, 128]), ALU.mult, [PR[b], GEX], [kt_])
                        yield
                    for h in range(8):
                        col = d * 8 + h
                        psc = A.bank(b0)[:, (h % 2) * 128:(h % 2) * 128 + 128]
                        A.mm([(psc, kT.ap[:, h, tc_], qT.ap[:, h, tc_], True, True)], [kT, qT], [PR[b0]])
                        yield
                        sb_ = sTb[d][h % 2]
                        dn_ = dn[d][h % 2]
                        A.stt(sb_.ap, psc, GS_[:, gt, col:col + 1], mask, ALU.mult, ALU.mult, [PR[b0], GEX, CONST], [sb_])
                        yield
                        if i < 7:
                            bu = b0 + 2
                            pu = A.bank(bu)[:, 0:257]
                            A.mm([(pu, kt_.ap[:, h, :], v1.ap[:, ti, h, :], True, True)], [kt_, v1], [PR[bu]])
                            yield
                        bn = b0 + 1
                        pnd = A.bank(bn)[:, 0:257]
                        A.mm([(pnd, sb_.ap, v1.ap[:, ti, h, :], True, False),
                              (pnd, qT.ap[:, h, tc_], Sb_.ap[:, d, h, :], False, True)], [sb_, v1, qT, SbR[d][h]], [PR[bn]])
                        yield
                        S.op("dve", lambda e, dn_=dn_, pnd=pnd: e.tensor_reduce(out=dn_.ap[:, 0:1], in_=pnd[:, 256:257], axis=AX.X, op=ALU.max,
                                                                                apply_absolute_value=True), [PR[bn]], [dn_])
                        yield
                        A.ts(dn_.ap[:, 0:1], dn_.ap[:, 0:1], FL_[:, gt, col:col + 1], ALU.max, [dn_, GEX], [dn_])
                        yield
                        S.op("dve", lambda e, dn_=dn_: e.reciprocal(out=dn_.ap[:, 1:2], in_=dn_.ap[:, 0:1]), [dn_], [dn_])
                        yield
                        A.act(HT[d].ap[:, h, :], pnd[:, 0:256], AF.Copy, [PR[bn], dn_], [HT[d]], scale=dn_.ap[:, 1:2])
                        yield
                        if i < 7:
                            A.stt(St.ap[:, d, h, :], St.ap[:, d, h, :], LAM_[:, gt, col:col + 1], pu, ALU.mult, ALU.add,
                                  [SR[d][h], PR[bu], GEX], [SR[d][h]])
                            yield
                            A.cp(Sb_.ap[:, d, h, :], St.ap[:, d, h, :], [SR[d][h]], [SbR[d][h]], eng="act")
                            yield
                    S.dma("sp", (HF_d if d == 0 else HB_d)[ti * 128:(ti + 1) * 128, :], HT[d].ap.rearrange("p h e -> p (h e)"),
                          reads=[HT[d]], writes=[Rd["HF" if d == 0 else "HB"]])
                    yield

            A.interleave([scan_dir(0), scan_dir(1)], 2, 0)
            S.fence()
            NBC = 4
            hfb = [A.at(0 + 8 * i, [D], F32, f"hfb{i}") for i in range(NBC)]
            hbb = [A.at(32 + 8 * i, [D], F32, f"hbb{i}") for i in range(NBC)]
            hnb = [A.at(64 + 8 * i, [D], F32, f"hnb{i}") for i in range(NBC)]
            h8 = [A.at(96 + 0.5 * i, [96], F32, f"h8{i}") for i in range(NBC)]

            def tileC(ti):
                a_, b_, n_, s8 = hfb[ti % NBC], hbb[ti % NBC], hnb[ti % NBC], h8[ti % NBC]
                rows = slice(ti * 128, (ti + 1) * 128)
                S.dma("sp", a_.ap, HF_d[rows, :], reads=[Rd["HF"]], writes=[a_])
                S.dma("sp", b_.ap, HB_d[rows, :], reads=[Rd["HB"]], writes=[b_])
                yield
                A.tt(a_.ap, a_.ap, b_.ap, ALU.add, [a_, b_], [a_])
                yield
                st8 = s8.ap[:, 0:48].rearrange("p (h s) -> p h s", s=6)
                mv8 = s8.ap[:, 48:64].rearrange("p (h s) -> p h s", s=2)
                rs8 = s8.ap[:, 64:72]
                nm8 = s8.ap[:, 72:80]
                for h in range(8):
                    S.op("dve", lambda e, h=h, a_=a_, st8=st8: e.bn_stats(out=st8[:, h, :], in_=a_.ap[:, h * 256:(h + 1) * 256]), [a_], [s8])
                    yield
                for h in range(8):
                    S.op("dve", lambda e, h=h, st8=st8, mv8=mv8: e.bn_aggr(out=mv8[:, h, :], in_=st8[:, h, :]), [s8], [s8])
                    yield
                A.act(rs8, mv8[:, :, 1], AF.Ln, [s8], [s8], bias=EPS)
                yield
                A.act(rs8, rs8, AF.Exp, [s8], [s8], scale=-0.5)
                yield
                A.stt(nm8, mv8[:, :, 0], -1.0, rs8, ALU.mult, ALU.mult, [s8], [s8])
                yield
                for h in range(8):
                    if h % 2 == 0:
                        A.act(n_.ap[:, h * 256:(h + 1) * 256], a_.ap[:, h * 256:(h + 1) * 256], AF.Identity, [a_, s8], [n_],
                              scale=rs8[:, h:h + 1], bias=nm8[:, h:h + 1])
                    else:
                        A.ts(n_.ap[:, h * 256:(h + 1) * 256], a_.ap[:, h * 256:(h + 1) * 256], rs8[:, h:h + 1], ALU.mult, [a_, s8], [n_],
                             s2=nm8[:, h:h + 1], op1=ALU.add)
                    yield
                S.dma("sp", HN_d[rows, :], n_.ap, reads=[n_], writes=[Rd["HN"]])
                yield

            A.interleave([tileC(ti) for ti in range(8)], 4, 9)
            S.fence()
            if self.stop == "C":
                return self.finish(nc, S)

            uT = A.at(0, [KC, NOWN], BF16, "uT")
            hmoT = A.at(32, [KC, NOWN], BF16, "hmoT")
            S.dma("sp", uT.ap, uT_d[:, :, 1280:2304], reads=[Rd["uT"]], writes=[uT])
            so = [A.at(64 + 2 * i, [512], F32, f"so{i}") for i in range(4)]
            hnk = [A.at(72 + 2 * i, [512], F32, f"hnk{i}") for i in range(4)]
            wD = [None, None]

            def d_mm(n):
                blk, t = divmod(n, 8)
                if t == 0:
                    wD[0], wD[1] = A.ws_get()
                wsl, wr = wD
                cols = slice(blk * 512, (blk + 1) * 512)
                so_, hn_ = so[n % 4], hnk[n % 4]
                b = n % 4
                pb = A.bank(b)
                S.dma("sp", hn_.ap, HN_d[t * 128:(t + 1) * 128, cols], reads=[Rd["HN"]], writes=[hn_])
                A.mm([(pb, uT.ap[:, kc, t * 128:(t + 1) * 128], wsl[:, kc, :], kc == 0, kc == 15) for kc in range(16)],
                     [uT, wr], [PR[b]])
                A.act(so_.ap, pb, AF.Sigmoid, [PR[b]], [so_])
                A.tt(so_.ap, so_.ap, hn_.ap, ALU.mult, [so_, hn_], [so_])

            def d_rest(n):
                blk, t = divmod(n, 8)
                so_ = so[n % 4]
                b2 = 4 + n % 4
                pt = A.bank(b2)
                A.tr([(pt[:, j * 128:(j + 1) * 128], so_.ap[:, j * 128:(j + 1) * 128], ident) for j in range(4)], [so_, CONST], [PR[b2]])
                A.tt(hmoT.ap[:, blk * 4:blk * 4 + 4, t * 128:(t + 1) * 128], pt.rearrange("p (a b) -> p a b", b=128),
                     pfm[:, 576 + blk * 4:576 + blk * 4 + 4].unsqueeze(2).broadcast_to([128, 4, 128]), ALU.mult,
                     [PR[b2], CONST], [hmoT])

            d_mm(0)
            d_mm(1)
            for n in range(32):
                if n + 2 < 32:
                    d_mm(n + 2)
                d_rest(n)
            S.fence()

            sg = [A.at(64 + 4 * i, [NOWN], F32, f"sg{i}") for i in range(2)]
            zmb = [A.at(72 + 4 * i, [NOWN], F32, f"zmb{i}") for i in range(2)]
            n = 0
            for blk in range(4):
                w1, r1_ = A.ws_get()
                w2, r2_ = A.ws_get()
                for sub in range(4):
                    c = blk * 4 + sub
                    sg_, zm_ = sg[n % 2], zmb[n % 2]
                    n += 1
                    pbase = 4 * (n % 2)
                    py = A.bank(pbase, 2)
                    pg_ = A.bank(pbase + 2, 2)
                    mms = []
                    for kc in range(16):
                        for hf in range(2):
                            mms.append((py[:, hf * 512:(hf + 1) * 512], w1[:, kc, sub * 128:(sub + 1) * 128],
                                        hmoT.ap[:, kc, hf * 512:(hf + 1) * 512], kc == 0, kc == 15))
                    A.mm(mms, [hmoT, r1_], PR[pbase:pbase + 2])
                    mms = []
                    for kc in range(16):
                        for hf in range(2):
                            mms.append((pg_[:, hf * 512:(hf + 1) * 512], w2[:, kc, sub * 128:(sub + 1) * 128],
                                        uT.ap[:, kc, hf * 512:(hf + 1) * 512], kc == 0, kc == 15))
                    A.mm(mms, [uT, r2_], PR[pbase + 2:pbase + 4])
                    A.act(sg_.ap, pg_, AF.Sigmoid, PR[pbase + 2:pbase + 4], [sg_])
                    A.tt(zm_.ap, sg_.ap, py, ALU.mult, [sg_] + PR[pbase:pbase + 2], [zm_])
                    S.dma("sp", ZM_d[c * 128:(c + 1) * 128, :], zm_.ap, reads=[zm_], writes=[Rd["ZM"]])
            S.fence()

            yc = A.at(32, [KC, NOWN], F32, "yc")
            ycin = Buf(A.at(0, [KC, NOWN], BF16).ap, uT.rg)
            ypad = [A.at(96 + 3 * i, [16, 94], BF16, f"ypad{i}") for i in range(2)]
            sgl = [A.at(102, [512], F32, "sgl0")] * 2
            sq = [A.at(104 + 4 * i, [NOWN], F32, f"sq{i}") for i in range(2)]
            dw = [A.at(112 + 7.75 * i, [31, 128], BF16, f"dw{i}") for i in range(2)]
            ypR = [[Region(f"ypR{i}{hf}") for hf in range(2)] for i in range(2)]
            for i in range(2):
                S.op("dve", lambda e, i=i: e.memset(ypad[i].ap.rearrange("p a b -> p (a b)"), 0.0), [], [ypad[i]] + ypR[i])
            STAT = PR[4:8]
            psum_s = A.bank(4, 2)
            psum_q = A.bank(6, 2)
            n = 0
            for c in range(16):
                wa, ra = A.ws_get() if c % 4 == 0 else (wa, ra)
                wl_, rl = A.ws_get() if c % 4 == 0 else (wl_, rl)
                sub = c % 4
                yp, dw_, sq_ = ypad[c % 2], dw[c % 2], sq[c % 2]
                for j in range(31):
                    A.ts(dw_.ap[:, j, :], ident, pfm[:, 64 + j * 16 + c:65 + j * 16 + c], ALU.mult, [CONST], [dw_])
                for hf in range(2):
                    pa = A.bank(0)
                    pl = A.bank(1)
                    sgl_ = sgl[n % 2]
                    n += 1
                    mms = []
                    for kc in range(16):
                        mms.append((pa, wa[:, kc, sub * 128:(sub + 1) * 128], uT.ap[:, kc, hf * 512:(hf + 1) * 512], kc == 0, kc == 15))
                        mms.append((pl, wl_[:, kc, sub * 128:(sub + 1) * 128], uT.ap[:, kc, hf * 512:(hf + 1) * 512], kc == 0, kc == 15))
                    A.mm(mms, [uT, ra, rl], PR[0:2])
                    A.act(sgl_.ap, pl, AF.Sigmoid, [PR[1]], [sgl_])
                    A.tt(yp.ap[:, hf * 8:(hf + 1) * 8, 15:79], sgl_.ap.rearrange("p (r t) -> p r t", t=64),
                         pa.rearrange("p (r t) -> p r t", t=64), ALU.mult, [sgl_, PR[0]], [ypR[c % 2][hf]])
                pcv = A.bank(2, 2)
                for hf in range(2):
                    A.mm([(pcv[:, hf * 512:(hf + 1) * 512], dw_.ap[:, j, :], yp.ap[:, hf * 8:(hf + 1) * 8, j:j + 64], j == 0, j == 30)
                          for j in range(31)], [ypR[c % 2][hf], dw_], [PR[2 + hf]])
                A.act(yc.ap[:, c, :], pcv, AF.Identity, PR[2:4] + [CONST], [yc], bias=pfm[:, 560 + c:561 + c])
                A.act(sq_.ap, yc.ap[:, c, :], AF.Square, [yc], [sq_])
                mms = []
                for hf in range(2):
                    mms.append((psum_s[:, hf * 512:(hf + 1) * 512], ones, yc.ap[:, c, hf * 512:(hf + 1) * 512], c == 0, c == 15))
                    mms.append((psum_q[:, hf * 512:(hf + 1) * 512], ones, sq_.ap[:, hf * 512:(hf + 1) * 512], c == 0, c == 15))
                A.mm(mms, [yc, sq_, CONST], STAT)
            mean_bc = A.at(96, [NOWN], F32, "mean_bc")
            rstd_bc = A.at(100, [NOWN], F32, "rstd_bc")
            tb = [A.at(104 + 4 * i, [NOWN], F32, f"tb{i}") for i in range(2)]
            S.fence()
            A.ts(mean_bc.ap, psum_s, 1.0 / D, ALU.mult, PR[4:6], [mean_bc])
            A.ts(rstd_bc.ap, psum_q, 1.0 / D, ALU.mult, PR[6:8], [rstd_bc])
            A.tt(tb[0].ap, mean_bc.ap, mean_bc.ap, ALU.mult, [mean_bc], [tb[0]])
            A.tt(rstd_bc.ap, rstd_bc.ap, tb[0].ap, ALU.subtract, [rstd_bc, tb[0]], [rstd_bc])
            A.act(rstd_bc.ap, rstd_bc.ap, AF.Ln, [rstd_bc], [rstd_bc], bias=EPS)
            A.act(rstd_bc.ap, rstd_bc.ap, AF.Exp, [rstd_bc], [rstd_bc], scale=-0.5)
            for c in range(16):
                t_ = tb[c % 2]
                A.tt(t_.ap, yc.ap[:, c, :], mean_bc.ap, ALU.subtract, [yc, mean_bc], [t_])
                A.tt(t_.ap, t_.ap, rstd_bc.ap, ALU.mult, [t_, rstd_bc], [t_])
                A.act(ycin.ap[:, c, :], t_.ap, AF.Silu, [t_, CONST], [ycin], scale=pfm[:, 592 + c:593 + c], bias=pfm[:, 608 + c:609 + c])
            S.fence()

            uT2 = A.at(32, [KC, NOWN], BF16, "uT2")
            zT = A.at(64, [KC, NOWN], BF16, "zT")
            S.dma("sp", uT2.ap, uT_d[:, :, 1280:2304], reads=[Rd["uT"]], writes=[uT2])
            sg = [A.at(96 + 4 * i, [NOWN], F32, f"sgb{i}") for i in range(2)]
            zmb = [A.at(104 + 4 * i, [NOWN], F32, f"zml{i}") for i in range(2)]
            n = 0
            for blk in range(4):
                w1, r1_ = A.ws_get()
                w2, r2_ = A.ws_get()
                for sub in range(4):
                    c = blk * 4 + sub
                    sg_, zm_ = sg[n % 2], zmb[n % 2]
                    n += 1
                    pbase = 4 * (n % 2)
                    py = A.bank(pbase, 2)
                    pg_ = A.bank(pbase + 2, 2)
                    S.dma("sp", zm_.ap, ZM_d[c * 128:(c + 1) * 128, :], reads=[Rd["ZM"]], writes=[zm_])
                    mms = []
                    for kc in range(16):
                        for hf in range(2):
                            mms.append((py[:, hf * 512:(hf + 1) * 512], w1[:, kc, sub * 128:(sub + 1) * 128],
                                        ycin.ap[:, kc, hf * 512:(hf + 1) * 512], kc == 0, kc == 15))
                    A.mm(mms, [ycin, r1_], PR[pbase:pbase + 2])
                    mms = []
                    for kc in range(16):
                        for hf in range(2):
                            mms.append((pg_[:, hf * 512:(hf + 1) * 512], w2[:, kc, sub * 128:(sub + 1) * 128],
                                        uT2.ap[:, kc, hf * 512:(hf + 1) * 512], kc == 0, kc == 15))
                    A.mm(mms, [uT2, r2_], PR[pbase + 2:pbase + 4])
                    A.act(sg_.ap, pg_, AF.Sigmoid, PR[pbase + 2:pbase + 4], [sg_])
                    A.tt(sg_.ap, sg_.ap, py, ALU.mult, [sg_] + PR[pbase:pbase + 2], [sg_])
                    A.tt(zT.ap[:, c, :], sg_.ap, zm_.ap, ALU.add, [sg_, zm_], [zT])
            S.fence()

            g1bc = A.at(0, [D], F32, "g1bc")
            S.dma("sp", g1bc.ap, mod_d[0:1, 2 * D:3 * D].broadcast_to([128, D]), reads=[Rd["mod"]], writes=[g1bc])
            xb = [A.at(8 + 2 * i, [512], F32, f"xb{i}") for i in range(2)]
            t1 = [A.at(12 + 2 * i, [512], F32, f"t1{i}") for i in range(2)]
            st1 = A.at(127, [8, 4, 6], F32, "st1")
            n = 0
            for blk in range(4):
                wsl, wr = A.ws_get()
                cols = slice(blk * 512, (blk + 1) * 512)
                for t in range(8):
                    b = n % 2
                    x_, t_ = xb[n % 2], t1[n % 2]
                    n += 1
                    pb = A.bank(b)
                    S.dma("sp", x_.ap, xin[(10 + t) * 128:(11 + t) * 128, cols], writes=[x_])
                    A.mm([(pb, zT.ap[:, kc, t * 128:(t + 1) * 128], wsl[:, kc, :], kc == 0, kc == 15) for kc in range(16)],
                         [zT, wr], [PR[b]])
                    A.tt(t_.ap, pb, g1bc.ap[:, cols], ALU.mult, [PR[b], g1bc], [t_])
                    A.stt(t_.ap, x_.ap, ALPHA, t_.ap, ALU.mult, ALU.add, [x_, t_], [t_])
                    S.op("dve", lambda e, t=t, blk=blk, t_=t_: e.bn_stats(out=st1.ap[:, t, blk, :], in_=t_.ap), [t_], [st1])
                    S.dma("sp", R1_d[t * 128:(t + 1) * 128, cols], t_.ap, reads=[t_], writes=[Rd["R1"]])
            S.fence()

            xmT = A.at(96, [KC, NOWN], BF16, "xmT")
            lg = A.at(0, [D], F32, "ln1g")
            lb = A.at(8, [D], F32, "ln1b")
            S.dma("sp", lg.ap, lnrow[0:1, :].broadcast_to([128, D]), writes=[lg])
            S.dma("sp", lb.ap, lnrow[1:2, :].broadcast_to([128, D]), writes=[lb])
            NBH = 4
            rt = [A.at(16 + 8 * i, [D], F32, f"rt{i}") for i in range(NBH)]
            x1b = [A.at(48 + 8 * i, [D], F32, f"x1b{i}") for i in range(NBH)]
            s1 = [A.at(88 + 0.0625 * i, [4], F32, f"s1{i}") for i in range(NBH)]

            def tileH(t):
                r_, x_, s_ = rt[t % NBH], x1b[t % NBH], s1[t % NBH]
                y_ = r_
                rows = slice(t * 128, (t + 1) * 128)
                S.dma("sp", r_.ap, R1_d[rows, :], reads=[Rd["R1"]], writes=[r_])
                S.op("dve", lambda e, t=t, s_=s_: e.bn_aggr(out=s_.ap[:, 0:2], in_=st1.ap[:, t, :, :].rearrange("p a b -> p (a b)")), [st1], [s_])
                yield
                yield from A.rstd_chain_g(s_.ap[:, 1:2], s_.ap[:, 0:1], s_.ap[:, 2:3], s_.ap[:, 3:4], s_.rg)
                A.act(x_.ap, r_.ap, AF.Identity, [r_, s_], [x_], scale=s_.ap[:, 2:3], bias=s_.ap[:, 3:4])
                yield
                A.tt(x_.ap, x_.ap, lg.ap, ALU.mult, [x_, lg], [x_])
                yield
                A.tt(x_.ap, x_.ap, lb.ap, ALU.add, [x_, lb], [x_])
                yield
                S.dma("sp", X1_d[rows, :], x_.ap, reads=[x_], writes=[Rd["X1"]])
                rs, nmr, R = yield from A.ln_stats_g(x_.ap, x_.rg)
                A.act(y_.ap, x_.ap, AF.Identity, [x_, R], [y_], scale=rs, bias=nmr)
                yield
                for g in range(4):
                    bi = (t * 4 + g) % 8
                    pb = A.bank(bi)
                    A.tr([(pb[:, j * 128:(j + 1) * 128], y_.ap[:, (g * 4 + j) * 128:(g * 4 + j + 1) * 128], ident) for j in range(4)],
                         [y_, CONST], [PR[bi]])
                    yield
                    for j in range(4):
                        kc = g * 4 + j
                        if (j + g) % 2 == 0:
                            A.ts(xmT.ap[:, kc, rows], pb[:, j * 128:(j + 1) * 128], msc[:, 5, kc:kc + 1], ALU.mult,
                                 [PR[bi], MSC2], [xmTR[t]], s2=msc[:, 4, kc:kc + 1], op1=ALU.add)
                        else:
                            A.act(xmT.ap[:, kc, rows], pb[:, j * 128:(j + 1) * 128], AF.Identity, [PR[bi], MSC2], [xmTR[t]],
                                  scale=msc[:, 5, kc:kc + 1], bias=msc[:, 4, kc:kc + 1])
                        yield

            xmTR = [Region(f"xmT{t}") for t in range(8)]
            A.interleave([tileH(t) for t in range(8)], 3, 14)
            S.op("dve", lambda e: e.memset(mp[:, 0:1], 0.0), xmTR, [xmT])
            S.fence()
            if self.stop == "H":
                return self.finish(nc, S)

            hid = A.at(0, [22, NOWN], BF16, "hid")
            sa = [A.at(44 + 2 * i, [512], F32, f"sa{i}") for i in range(2)]
            g2bc = A.at(48, [D], F32, "g2bc")
            S.dma("sp", g2bc.ap, mod_d[0:1, 5 * D:6 * D].broadcast_to([128, D]), reads=[Rd["mod"]], writes=[g2bc])
            xb = [A.at(56 + 2 * i, [512], F32, f"x1k{i}") for i in range(2)]
            t1 = [A.at(60 + 2 * i, [512], F32, f"t2{i}") for i in range(2)]
            st2 = A.at(127, [8, 4, 6], F32, "st2")
            for hh in range(2):
                n = 0
                for r in range(6):
                    wa, ra = A.ws_get()
                    wb_, rb = A.ws_get()
                    nsub = 4 if r < 5 else 2
                    for sub in range(nsub):
                        j = r * 4 + sub
                        for hf in range(2):
                            b = (n % 2) * 2
                            sa_ = sa[n % 2]
                            n += 1
                            pa, pb_ = A.bank(b), A.bank(b + 1)
                            mms = []
                            for kc in range(16):
                                mms.append((pa, wa[:, kc, sub * 128:(sub + 1) * 128], xmT.ap[:, kc, hf * 512:(hf + 1) * 512], kc == 0, kc == 15))
                                mms.append((pb_, wb_[:, kc, sub * 128:(sub + 1) * 128], xmT.ap[:, kc, hf * 512:(hf + 1) * 512], kc == 0, kc == 15))
                            A.mm(mms, [xmT, ra, rb], PR[b:b + 2])
                            A.act(sa_.ap, pa, AF.Silu, [PR[b]], [sa_])
                            A.tt(hid.ap[:, j, hf * 512:(hf + 1) * 512], sa_.ap, pb_, ALU.mult, [sa_, PR[b + 1]], [hid])
                S.fence()
                n = 0
                for blk in range(4):
                    cols = slice(blk * 512, (blk + 1) * 512)
                    for pc in range(2):
                        wsl, wr = A.ws_get()
                        for t in range(8):
                            pb = A.bank(t)
                            A.mm([(pb, hid.ap[:, pc * 11 + k, t * 128:(t + 1) * 128], wsl[:, k, :], pc == 0 and k == 0, pc == 1 and k == 10)
                                  for k in range(11)], [hid, wr], [PR[t]])
                    for t in range(8):
                        pb = A.bank(t)
                        x_, t_ = xb[n % 2], t1[n % 2]
                        n += 1
                        rows = slice(t * 128, (t + 1) * 128)
                        A.tt(t_.ap, pb, g2bc.ap[:, cols], ALU.mult, [PR[t], g2bc], [t_])
                        if hh == 0:
                            S.dma("sp", x_.ap, X1_d[rows, cols], reads=[Rd["X1"]], writes=[x_])
                            A.stt(t_.ap, x_.ap, ALPHA, t_.ap, ALU.mult, ALU.add, [x_, t_], [t_])
                        else:
                            S.dma("sp", x_.ap, R2_d[rows, cols], reads=[Rd["R2"]], writes=[x_])
                            A.tt(t_.ap, t_.ap, x_.ap, ALU.add, [x_, t_], [t_])
                            S.op("dve", lambda e, t=t, blk=blk, t_=t_: e.bn_stats(out=st2.ap[:, t, blk, :], in_=t_.ap), [t_], [st2])
                        S.dma("sp", R2_d[rows, cols], t_.ap, reads=[t_], writes=[Rd["R2"]])
                S.fence()

            lg = A.at(0, [D], F32, "ln2g")
            lb = A.at(8, [D], F32, "ln2b")
            S.dma("sp", lg.ap, lnrow[2:3, :].broadcast_to([128, D]), writes=[lg])
            S.dma("sp", lb.ap, lnrow[3:4, :].broadcast_to([128, D]), writes=[lb])
            NBJ = 4
            rt = [A.at(16 + 8 * i, [D], F32, f"rt2{i}") for i in range(NBJ)]
            ob = [A.at(48 + 8 * i, [D], F32, f"ob{i}") for i in range(NBJ)]
            s1 = [A.at(88 + 0.0625 * i, [4], F32, f"s2{i}") for i in range(NBJ)]

            def tileJ(t):
                r_, o_, s_ = rt[t % NBJ], ob[t % NBJ], s1[t % NBJ]
                rows = slice(t * 128, (t + 1) * 128)
                S.dma("sp", r_.ap, R2_d[rows, :], reads=[Rd["R2"]], writes=[r_])
                S.op("dve", lambda e, t=t, s_=s_: e.bn_aggr(out=s_.ap[:, 0:2], in_=st2.ap[:, t, :, :].rearrange("p a b -> p (a b)")), [st2], [s_])
                yield
                yield from A.rstd_chain_g(s_.ap[:, 1:2], s_.ap[:, 0:1], s_.ap[:, 2:3], s_.ap[:, 3:4], s_.rg)
                A.act(o_.ap, r_.ap, AF.Identity, [r_, s_], [o_], scale=s_.ap[:, 2:3], bias=s_.ap[:, 3:4])
                yield
                A.tt(o_.ap, o_.ap, lg.ap, ALU.mult, [o_, lg], [o_])
                yield
                A.tt(o_.ap, o_.ap, lb.ap, ALU.add, [o_, lb], [o_])
                yield
                S.dma("sp", out[rows, :], o_.ap, reads=[o_], writes=[Rd["out"]])
                yield

            A.interleave([tileJ(t) for t in range(8)], 4, 2)
            return self.finish(nc, S)

    def finish(self, nc, S):
        S.wait_final()
        S.emit()
        return nc


def _fm(v):
    return np.ascontiguousarray(v.reshape(16, 128).T)


def make_in_maps(x, c, ctx, c_ctx, w_mod, b_mod, w_in, b_if, w_qk_conv, b_qk_conv, mh_norm_g, w_dw, b_dw,
                 conv_ln_g, conv_ln_b, w_proj_m, w_proj_c, w_out, ln1_g, ln1_b, w_ffn_in, w_ffn_out, ln2_g, ln2_b):
    f32 = np.float32
    A = lambda a: np.ascontiguousarray(np.asarray(a, dtype=f32))
    w_mod0, w_in0 = A(w_mod[0]), A(w_in[0])
    w_g_n = np.ascontiguousarray(w_in0[:, 6144:6176])
    w_g_f = np.ascontiguousarray(np.concatenate([w_in0[:, 6160:6176], w_in0[:, 6144:6160]], axis=1))
    bif_n = A(b_if[0]).reshape(1, 32)
    bif_f = np.ascontiguousarray(np.concatenate([bif_n[:, 16:32], bif_n[:, 0:16]], axis=1))
    cf = np.zeros((128, 4, 128), f32)
    cf[:, 0, :] = np.eye(128)
    idx = np.arange(128)
    cf[:, 1, :] = (idx[:, None] <= idx[None, :])
    cf[:, 2, :] = (idx[:, None] >= idx[None, :])
    cf[:, 3, :] = 1.0
    cf = cf.reshape(128, 512)
    identb = np.eye(128, dtype=f32).astype(ml_dtypes.bfloat16)
    lnrow = A(np.stack([ln1_g[0], ln1_b[0], ln2_g[0], ln2_b[0]]))
    shared = dict(w_mod=w_mod0, bmod=A(b_mod[0]).reshape(1, -1), w_in=w_in0, w_pm=A(w_proj_m[0]), w_pc=A(w_proj_c[0]),
                  w_o=A(w_out[0]), lnrow=lnrow, w_f1=A(w_ffn_in[0]), w_f2=A(w_ffn_out[0]), cf32=cf, identb=identb)

    def pfm_of(flip):
        p = np.zeros((128, NPF), f32)
        wq = A(w_qk_conv[0])
        wd = A(w_dw[0])
        if flip:
            wq = wq[::-1]
            wd = wd[::-1]
        for j in range(3):
            p[:, j * 16:(j + 1) * 16] = _fm(wq[j])
        p[:, 48:64] = _fm(A(b_qk_conv[0]))
        for j in range(31):
            p[:, 64 + j * 16:64 + (j + 1) * 16] = _fm(wd[j])
        p[:, 560:576] = _fm(A(b_dw[0]))
        p[:, 576:592] = _fm(A(mh_norm_g[0]))
        p[:, 592:608] = _fm(A(conv_ln_g[0]))
        p[:, 608:624] = _fm(A(conv_ln_b[0]))
        return p
    pfms = [pfm_of(False), pfm_of(True)]
    x, ctx, c, c_ctx = A(x), A(ctx), A(c), A(c_ctx)
    in_maps = []
    for r in range(8):
        b, s = r // 2, r % 2
        xb, cb = x[b], ctx[b]
        if s:
            xb, cb = xb[::-1], cb[::-1]
        xin = np.ascontiguousarray(np.concatenate([cb, xb[1024:], xb[:1024]], axis=0))
        cv = np.zeros((128, 16, 2), f32)
        cv[:, :, 0] = _fm(c[b])
        cv[:, :, 1] = _fm(c_ctx)
        m = dict(shared)
        m.update(xin=xin, cvec=cv.reshape(128, 32), w_g=w_g_f if s else w_g_n, bif=bif_f if s else bif_n, pfm=pfms[s])
        in_maps.append(m)
    return in_maps


_NC_CACHE = {}


def kernel(**inputs):
    if "nc" not in _NC_CACHE:
        bld = Builder()
        bld.stop = None
        _NC_CACHE["nc"] = bld.build()
    nc = _NC_CACHE["nc"]
    in_maps = make_in_maps(**inputs)
    res = run_bass_kernel_spmd(nc, in_maps, core_ids=list(range(8)))
    out = np.zeros((4, 2048, 2048), np.float32)
    for r in range(8):
        b, s = r // 2, r % 2
        o = np.asarray(res.results[r]["out"], dtype=np.float32)
        if s:
            out[b, 1024:] = o[::-1]
        else:
            out[b, :1024] = o
    return out
```

```python
import contextlib
import numpy as np
import ml_dtypes
import concourse.bass as bass
import concourse.mybir as mybir
from concourse.bass_utils import run_bass_kernel_spmd

dt = mybir.dt
AF = mybir.ActivationFunctionType
ALU = mybir.AluOpType
AX = mybir.AxisListType
F32 = dt.float32
BF16 = dt.bfloat16

D = 2048
KC = 16
NOWN = 1024
NTILE = 18
W_IN_COLS = 14368
D_FF = 5632
EPS = 1e-5
ALPHA = 2.0 ** 0.25
QSCALE = 128.0 ** -0.5
NPF = 624
KIB = 1024


class Region:
    __slots__ = ("name", "w", "r", "excl")

    def __init__(self, name="", excl=False):
        self.name = name
        self.w = None
        self.r = {}
        self.excl = excl


class Buf:
    __slots__ = ("ap", "rg")

    def __init__(self, ap, rg=None, name=""):
        self.ap = ap
        self.rg = rg if rg is not None else Region(name)


class Sched:
    ENG = ("pe", "act", "dve", "pool", "sp")

    def __init__(self, nc, stack, n_sp=24, n_pool=8):
        self.nc = nc
        self.sem = {e: stack.enter_context(nc.semaphore("c_" + e)) for e in ("pe", "act", "dve", "pool")}
        self.cnt = {e: 0 for e in self.sem}
        self.dsem = [stack.enter_context(nc.semaphore(f"d{i}")) for i in range(n_sp + n_pool)]
        self.dcnt = [0] * (n_sp + n_pool)
        self.n_sp = n_sp
        self.n_pool = n_pool
        self.nx = {"sp": 0, "pool": 0}
        self.prog = {e: [] for e in self.ENG}
        self.waited = {e: {} for e in self.ENG}

    def _semobj(self, key):
        return self.sem[key] if isinstance(key, str) else self.dsem[key]

    def _deps(self, reads, writes, extra):
        d = {}

        def add(k, v):
            if d.get(k, 0) < v:
                d[k] = v

        for r in reads:
            if r.w is not None:
                add(*r.w)
        for w in writes:
            if w.w is not None:
                add(*w.w)
            for k, v in w.r.items():
                add(k, v)
        for t in extra:
            if t is not None:
                add(*t)
        return d

    def _waits(self, eng, d):
        wl = []
        wd = self.waited[eng]
        for k, v in d.items():
            if wd.get(k, 0) < v:
                wd[k] = v
                wl.append((k, v))
        return wl

    def _update(self, tok, reads, writes):
        for w in writes:
            w.w = tok
            w.r = {}
        for r in reads:
            if r.r.get(tok[0], 0) < tok[1]:
                r.r[tok[0]] = tok[1]

    def op(self, eng, fn, reads=(), writes=(), extra=()):
        reads = [b.rg if isinstance(b, Buf) else b for b in reads]
        writes = [b.rg if isinstance(b, Buf) else b for b in writes]
        writes = writes + [r for r in reads if r.excl]
        reads = [r for r in reads if not r.excl]
        d = self._deps(reads, writes, extra)
        wl = self._waits(eng, d)
        self.cnt[eng] += 1
        tok = (eng, self.cnt[eng])
        self.prog[eng].append((wl, fn, (eng, 1)))
        self._update(tok, reads, writes)
        return tok

    def dma(self, q, out, in_, reads=(), writes=(), extra=(), slow=False):
        reads = [b.rg if isinstance(b, Buf) else b for b in reads]
        writes = [b.rg if isinstance(b, Buf) else b for b in writes]
        if q == "sp":
            i = self.nx["sp"]
            self.nx["sp"] = (i + 1) % self.n_sp
        else:
            i = self.n_sp + self.nx["pool"]
            self.nx["pool"] = (self.nx["pool"] + 1) % self.n_pool
        d = self._deps(reads, writes, extra)
        if self.dcnt[i] > 0 and d.get(i, 0) < self.dcnt[i]:
            d[i] = self.dcnt[i]
        wl = self._waits(q, d)
        self.dcnt[i] += 16
        tok = (i, self.dcnt[i])

        def fn(e, out=out, in_=in_, slow=slow):
            if slow:
                return e.dma_start(out=out, in_=in_, allow_slow_non_contiguous=True)
            return e.dma_start(out=out, in_=in_)

        self.prog[q].append((wl, fn, (i, 16)))
        self._update(tok, reads, writes)
        return tok

    def fence(self):
        toks = {}
        for e in ("pe", "act", "dve"):
            if self.cnt[e]:
                toks[e] = self.cnt[e]
        for i in range(self.n_sp):
            if self.dcnt[i]:
                toks[i] = self.dcnt[i]
        for e in ("pe", "act", "dve", "sp"):
            wl = self._waits(e, dict(toks))
            if wl:
                self.prog[e].append((wl, None, None))

    def wait_final(self):
        toks = {}
        for i in range(self.n_sp):
            if self.dcnt[i]:
                toks[i] = self.dcnt[i]
        wl = self._waits("sp", toks)
        self.prog["sp"].append((wl, None, None))

    def emit(self):
        nc = self.nc
        with nc.Block() as block:
            def mk(name):
                def body(e):
                    for wl, fn, inc in self.prog[name]:
                        for k, v in wl:
                            e.wait_ge(self._semobj(k), v)
                        if fn is None:
                            continue
                        ins = fn(e)
                        ins.then_inc(self._semobj(inc[0]), inc[1])
                return body

            block.tensor(mk("pe"))
            block.scalar(mk("act"))
            block.vector(mk("dve"))
            block.gpsimd(mk("pool"))
            block.sync(mk("sp"))


class Builder:
    def __init__(self, dump=None, stop=None):
        self.dump = dump
        self.stop = stop

    def act(self, out, in_, func, reads, writes, scale=None, bias=None):
        def fn(e):
            kw = {}
            if scale is not None:
                kw["scale"] = scale
            if bias is not None:
                kw["bias"] = bias
            return e.activation(out=out, in_=in_, func=func, **kw)
        return self.S.op("act", fn, reads, writes)

    def tt(self, out, in0, in1, op, reads, writes, eng="dve"):
        return self.S.op(eng, lambda e: e.tensor_tensor(out=out, in0=in0, in1=in1, op=op), reads, writes)

    def ts(self, out, in0, s1, op0, reads, writes, s2=None, op1=None, eng="dve"):
        def fn(e):
            if op1 is None:
                return e.tensor_scalar(out=out, in0=in0, scalar1=s1, scalar2=None, op0=op0)
            return e.tensor_scalar(out=out, in0=in0, scalar1=s1, scalar2=s2, op0=op0, op1=op1)
        return self.S.op(eng, fn, reads, writes)

    def stt(self, out, in0, scalar, in1, op0, op1, reads, writes):
        return self.S.op("dve", lambda e: e.scalar_tensor_tensor(out=out, in0=in0, scalar=scalar, in1=in1, op0=op0, op1=op1),
                         reads, writes)

    def cp(self, out, in_, reads, writes, eng="dve"):
        if eng == "act":
            return self.act(out, in_, AF.Copy, reads, writes)
        return self.S.op(eng, lambda e: e.tensor_copy(out=out, in_=in_), reads, writes)

    def mm(self, mms, reads, writes):
        def fn(e):
            ins = None
            for (o, l, r, st, sp) in mms:
                ins = e.matmul(o, lhsT=l, rhs=r, start=st, stop=sp)
            return ins
        return self.S.op("pe", fn, reads, writes)

    def tr(self, trs, reads, writes):
        def fn(e):
            ins = None
            for (o, i, idn) in trs:
                ins = e.transpose(out=o, in_=i, identity=idn)
            return ins
        return self.S.op("pe", fn, reads, writes)

    def at(self, off_kib, shape, dtype, name=""):
        n = int(np.prod(shape))
        esz = 4 if dtype == F32 else 2
        w0 = int(round(off_kib * KIB)) // 4
        nw = (n * esz + 3) // 4
        assert w0 + nw <= self.arena_words, (name, off_kib, shape)
        ap = self.arena[:, w0:w0 + nw]
        if dtype != F32:
            ap = ap.bitcast(dtype)
        ap = ap[:, 0:n]
        if len(shape) == 2:
            ap = ap.rearrange("p (a b) -> p a b", b=shape[1])
        elif len(shape) == 3:
            ap = ap.rearrange("p (a b c) -> p a b c", b=shape[1], c=shape[2])
        return Buf(ap, name=name)

    def bank(self, i, n=1):
        return self.psum[:, i * 512:(i + n) * 512]

    def ws_plan(self, W, r0, kcn, c0, ncols):
        self.wplan.append((W, r0, kcn, c0, ncols))

    def ws_get(self):
        i = self.wcons
        self.wcons += 1
        while self.wissued < len(self.wplan) and self.wissued < i + self.nslot - 1:
            j = self.wissued
            W, r0, kcn, c0, ncols = self.wplan[j]
            sl = j % self.nslot
            dst = self.ring[:, sl, 0:kcn * ncols].rearrange("p (k n) -> p k n", n=ncols)
            src = W[r0:r0 + kcn * 128, c0:c0 + ncols].rearrange("(k p) n -> p k n", p=128)
            self.S.dma("pool", dst, src, writes=[self.ringR[sl]])
            self.wissued += 1
        W, r0, kcn, c0, ncols = self.wplan[i]
        sl = i % self.nslot
        return self.ring[:, sl, 0:kcn * ncols].rearrange("p (k n) -> p k n", n=ncols), self.ringR[sl]

    @staticmethod
    def interleave(gens, width, admit_every):
        active = []
        it = iter(gens)
        more = True
        since = admit_every
        while True:
            if more and len(active) < width and (since >= admit_every or not active):
                try:
                    active.append(next(it))
                    since = 0
                except StopIteration:
                    more = False
            if not active:
                if not more:
                    break
                continue
            for g in list(active):
                try:
                    next(g)
                except StopIteration:
                    active.remove(g)
            since += 1

    def ln_stats_g(self, src, src_rg, parts=4, width=512):
        i = self.lni
        self.lni = (i + 1) % len(self.lnst)
        st = self.lnst[i]
        R = self.lnR[i]
        S = self.S
        stats = st[:, 0:parts * 6].rearrange("p (a b) -> p a b", b=6)
        mv = st[:, 48:50]
        rs = st[:, 50:51]
        nmr = st[:, 51:52]
        for a in range(parts):
            S.op("dve", lambda e, a=a: e.bn_stats(out=stats[:, a, :], in_=src[:, a * width:(a + 1) * width]), [src_rg], [R])
            yield
        S.op("dve", lambda e: e.bn_aggr(out=mv, in_=st[:, 0:parts * 6]), [R], [R])
        yield
        yield from self.rstd_chain_g(mv[:, 1:2], mv[:, 0:1], rs, nmr, R)
        return rs, nmr, R

    def rstd_chain_g(self, var, mean, rs, nmr, R):
        self.act(rs, var, AF.Ln, [R], [R], bias=EPS)
        yield
        self.act(rs, rs, AF.Exp, [R], [R], scale=-0.5)
        yield
        self.stt(nmr, mean, -1.0, rs, ALU.mult, ALU.mult, [R], [R])
        yield

    def ln_stats(self, src, src_rg, parts=4, width=512):
        i = self.lni
        self.lni = (i + 1) % 4
        st = self.lnst[i]
        R = self.lnR[i]
        S = self.S
        stats = st[:, 0:parts * 6].rearrange("p (a b) -> p a b", b=6)
        mv = st[:, 48:50]
        rs = st[:, 50:51]
        nmr = st[:, 51:52]
        for a in range(parts):
            S.op("dve", lambda e, a=a: e.bn_stats(out=stats[:, a, :], in_=src[:, a * width:(a + 1) * width]), [src_rg], [R])
        S.op("dve", lambda e: e.bn_aggr(out=mv, in_=st[:, 0:parts * 6]), [R], [R])
        self.rstd_chain(mv[:, 1:2], mv[:, 0:1], rs, nmr, R)
        return rs, nmr, R

    def rstd_chain(self, var, mean, rs, nmr, R):
        for _ in self.rstd_chain_g(var, mean, rs, nmr, R):
            pass

    def build(self):
        nc = bass.Bass("TRN2", target_bir_lowering=False)
        self.nc = nc
        di = lambda n, sh, d=F32: nc.dram_tensor(n, sh, d, kind="ExternalInput").ap()
        ds = lambda n, sh, d=F32: nc.dram_tensor(n, sh, d, kind="Internal").ap()
        xin = di("xin", [NTILE * 128, D])
        cvec = di("cvec", [128, 32])
        w_mod = di("w_mod", [D, 6 * D])
        bmod = di("bmod", [1, 6 * D])
        w_in = di("w_in", [D, W_IN_COLS])
        w_g = di("w_g", [D, 32])
        bif = di("bif", [1, 32])
        pfm_d = di("pfm", [128, NPF])
        w_pm = di("w_pm", [D, D])
        w_pc = di("w_pc", [D, D])
        w_o = di("w_o", [D, D])
        lnrow = di("lnrow", [4, D])
        w_f1 = di("w_f1", [D, 2 * D_FF])
        w_f2 = di("w_f2", [D_FF, D])
        cf_d = di("cf32", [128, 4 * 128])
        idb_d = di("identb", [128, 128], BF16)
        out = nc.dram_tensor("out", [NOWN, D], F32, kind="ExternalOutput").ap()
        dumps = {}
        if self.dump:
            for n, sh, d in self.dump:
                dumps[n] = nc.dram_tensor("dump_" + n, sh, d, kind="ExternalOutput").ap()

        def scratch(n, sh, d=F32):
            return dumps[n] if n in dumps else ds(n, sh, d)
        mod_d = scratch("mod", [2, 6 * D])
        uT_d = scratch("uT", [128, KC, NTILE * 128], BF16)
        HF_d = scratch("HF", [NOWN, D])
        HB_d = scratch("HB", [NOWN, D])
        HN_d = scratch("HN", [NOWN, D])
        ZM_d = scratch("ZM", [D, NOWN])
        R1_d = scratch("R1", [NOWN, D])
        X1_d = scratch("X1", [NOWN, D])
        R2_d = scratch("R2", [NOWN, D])
        dbg_d = dumps.get("dbg")
        Rd = {n: Region(n) for n in ["mod", "uT", "HF", "HB", "HN", "ZM", "R1", "X1", "R2", "out", "dbg"]}

        with contextlib.ExitStack() as st:
            S = Sched(nc, st)
            self.S = S
            sb = lambda n, sh, d: st.enter_context(nc.sbuf_tensor(n, sh, d))
            self.nslot = 3
            ring = sb("ring", [128, self.nslot, 8192], BF16)
            self.ring = ring
            self.ringR = [Region(f"ring{i}") for i in range(self.nslot)]
            self.wplan, self.wcons, self.wissued = [], 0, 0
            ARENA_KIB = 128
            self.arena_words = int(ARENA_KIB * KIB) // 4
            self.arena = sb("arena", [128, self.arena_words], F32)
            self.psum = st.enter_context(nc.psum_tensor("psum", [128, 4096], F32))
            PR = [Region(f"pb{i}", excl=True) for i in range(8)]
            cf = sb("cf", [128, 4, 128], F32)
            ident, trif, trib, ones = cf[:, 0, :], cf[:, 1, :], cf[:, 2, :], cf[:, 3, :]
            identb = sb("identb_s", [128, 128], BF16)
            pfm = sb("pfm_s", [128, NPF], F32)
            wg = sb("wg", [128, KC, 32], BF16)
            bif_bc = sb("bif_bc", [128, 32], F32)
            dq = sb("dq", [128, 16, 3, 128], BF16)
            modT = sb("modT", [128, 96, 2], F32)
            msc = sb("msc", [128, 6, 16], F32)
            lnst_t = sb("lnst", [128, 8, 64], F32)
            self.lnst = [lnst_t[:, i, :] for i in range(8)]
            self.lnR = [Region(f"lnst{i}") for i in range(8)]
            self.lni = 0
            gex = sb("gex", [128, 5, NTILE, 16], F32)
            gsm = sb("gsm", [128, 7, 96], F32)
            mp = sb("mp", [128, 16], F32)
            halo = sb("halo", [128, KC, 2], BF16)
            CONST, GEX, MSC, HALO, DQ = (Region(n) for n in ["const", "gex", "msc", "halo", "dq"])
            GSM = [Region(f"gsm{i}") for i in range(7)]
            A = self

            S.dma("sp", cf[:].rearrange("p a b -> p (a b)"), cf_d, writes=[CONST])
            S.dma("sp", identb[:], idb_d, writes=[CONST])
            S.dma("sp", pfm[:], pfm_d, writes=[CONST])
            S.dma("sp", bif_bc[:], bif.broadcast_to([128, 32]), writes=[CONST])
            S.dma("pool", wg[:], w_g.rearrange("(k p) n -> p k n", p=128), writes=[CONST])
            cv = A.at(118, [32], F32, "cv")
            csb = A.at(118.25, [KC, 2], BF16, "csb")
            S.dma("sp", cv.ap, cvec, writes=[cv])
            A.act(csb.ap, cv.ap.rearrange("p (k j) -> p k j", j=2), AF.Silu, [cv], [csb])
            for c in range(16):
                for j in range(3):
                    A.ts(dq[:, c, j, :], ident, pfm[:, j * 16 + c:j * 16 + c + 1], ALU.mult, [CONST], [DQ])
            for cb in range(24):
                A.ws_plan(w_mod, 0, 16, cb * 512, 512)
            for blk in range(2):
                A.ws_plan(w_in, 0, 16, 1024 + blk * 512, 512)
            for blk in range(4):
                A.ws_plan(w_in, 0, 16, 2048 + blk * 512, 512)
            for blk in range(4):
                A.ws_plan(w_in, 0, 16, blk * 512, 512)
            for blk in range(4):
                A.ws_plan(w_in, 0, 16, 2048 + blk * 512, 512)
            for blk in range(4):
                A.ws_plan(w_in, 0, 16, 4096 + blk * 512, 512)
            for blk in range(4):
                A.ws_plan(w_pm, 0, 16, blk * 512, 512)
                A.ws_plan(w_in, 0, 16, 10272 + blk * 512, 512)
            for blk in range(8):
                A.ws_plan(w_in, 0, 16, 6176 + (blk // 2) * 512 + (blk % 2) * 2048, 512)
            for blk in range(4):
                A.ws_plan(w_pc, 0, 16, blk * 512, 512)
                A.ws_plan(w_in, 0, 16, 12320 + blk * 512, 512)
            for blk in range(4):
                A.ws_plan(w_o, 0, 16, blk * 512, 512)
            for hh in range(2):
                for r in range(6):
                    ncol = 512 if r < 5 else 256
                    c0 = hh * 2816 + r * 512
                    A.ws_plan(w_f1, 0, 16, c0, ncol)
                    A.ws_plan(w_f1, 0, 16, D_FF + c0, ncol)
                for blk in range(4):
                    for pc in range(2):
                        A.ws_plan(w_f2, hh * 2816 + pc * 1408, 11, blk * 512, 512)

            MODT, MSC2 = Region("modT"), Region("msc2")
            RdMod = [Region(f"mod{i}") for i in range(24)]
            stg = [A.at(119 + 2 * i, [512], F32, f"stg{i}") for i in range(2)]
            bmb = [A.at(123 + 2 * i, [512], F32, f"bmb{i}") for i in range(2)]

            def mod_block(cb):
                wsl, wr = A.ws_get()
                pb = A.bank(6)
                A.mm([(pb[0:2, :], csb.ap[:, kc, :], wsl[:, kc, :], kc == 0, kc == 15) for kc in range(16)],
                     [csb, wr], [PR[6]])
                yield
                s_, bm_ = stg[cb % 2], bmb[cb % 2]
                S.dma("sp", bm_.ap[0:2, :], bmod[0:1, cb * 512:(cb + 1) * 512].broadcast_to([2, 512]), writes=[bm_])
                A.tt(s_.ap[0:2, :], pb[0:2, :], bm_.ap[0:2, :], ALU.add, [PR[6], bm_], [s_])
                yield
                S.dma("sp", mod_d[:, cb * 512:(cb + 1) * 512], s_.ap[0:2, :], reads=[s_], writes=[Rd["mod"], RdMod[cb]])
                pt = A.bank(7)[:, 0:8].rearrange("p (a b) -> p a b", b=2)
                A.tr([(pt[:, j, :], s_.ap[0:2, j * 128:(j + 1) * 128], ident[0:2, 0:2]) for j in range(4)], [s_, CONST], [PR[7]])
                yield
                A.cp(modT[:, cb * 4:(cb + 1) * 4, :], pt, [PR[7]], [MODT])
                yield

            for cb in range(8):
                for _ in mod_block(cb):
                    pass
            A.cp(msc[:, 0, :], modT[:, 0:16, 0], [MODT], [MSC])
            A.ts(msc[:, 1, :], modT[:, 16:32, 0], 1.0, ALU.add, [MODT], [MSC])
            A.cp(msc[:, 2, :], modT[:, 0:16, 1], [MODT], [MSC])
            A.ts(msc[:, 3, :], modT[:, 16:32, 1], 1.0, ALU.add, [MODT], [MSC])

            def modgen():
                for cb in range(8, 24):
                    yield from mod_block(cb)
                    if cb == 19:
                        A.cp(msc[:, 4, :], modT[:, 48:64, 0], [MODT], [MSC2])
                        A.ts(msc[:, 5, :], modT[:, 64:80, 0], 1.0, ALU.add, [MODT], [MSC2])
                    for _ in range(5):
                        yield

            gatb = A.at(72, [12, NTILE, 16], F32, "gat")
            gat, GAT = gatb.ap, gatb.rg
            S.op("dve", lambda e: e.memset(gat.rearrange("p a b c -> p (a b c)"), 0.0), [], [GAT])
            NB_A = 6
            xt = [A.at(0 + 8 * i, [D], F32, f"xt{i}") for i in range(NB_A)]
            yn = xt
            uTt = [A.at(48 + 4 * i, [KC, 128], BF16, f"uTt{i}") for i in range(NB_A)]
            gsmv = [gsm[:, i, :] for i in range(NB_A)]
            GATt = [Region(f"gat{t}") for t in range(NTILE)]
            for r_ in GATt:
                r_.w = GAT.w

            GALL = A.at(86, [NTILE, 32], F32, "GALL")
            GSA = A.at(88.5, [NTILE, 32], F32, "GSA")
            ELA = A.at(91, [NTILE, 16], F32, "ELA")
            LLA = A.at(92.25, [NTILE, 16], F32, "LLA")
            AMX = A.at(93.5, [4], F32, "AMX")
            DGS = A.at(94, [3, 96], F32, "DGS")
            GALLt = [Region(f"gall{t}") for t in range(NTILE)]

            def tileA(t):
                x_b, y_b, u_b = xt[t % NB_A], yn[t % NB_A], uTt[t % NB_A]
                S.dma("sp", x_b.ap, xin[t * 128:(t + 1) * 128, :], writes=[x_b])
                yield
                rs, nmr, R = yield from A.ln_stats_g(x_b.ap, x_b.rg)
                A.act(y_b.ap, x_b.ap, AF.Identity, [x_b, R], [y_b], scale=rs, bias=nmr)
                yield
                isctx = t < 2
                shv = msc[:, 2 if isctx else 0, :]
                scv = msc[:, 3 if isctx else 1, :]
                for g in range(4):
                    bi = (t * 4 + g) % 4
                    pb = A.bank(bi)
                    A.tr([(pb[:, j * 128:(j + 1) * 128], y_b.ap[:, (g * 4 + j) * 128:(g * 4 + j + 1) * 128], ident) for j in range(4)],
                         [y_b, CONST], [PR[bi]])
                    yield
                    for j in range(4):
                        kc = g * 4 + j
                        if (j + g) % 2 == 0:
                            A.ts(u_b.ap[:, kc, :], pb[:, j * 128:(j + 1) * 128], scv[:, kc:kc + 1], ALU.mult,
                                 [PR[bi], MSC], [u_b], s2=shv[:, kc:kc + 1], op1=ALU.add)
                        else:
                            A.act(u_b.ap[:, kc, :], pb[:, j * 128:(j + 1) * 128], AF.Identity, [PR[bi], MSC], [u_b],
                                  scale=scv[:, kc:kc + 1], bias=shv[:, kc:kc + 1])
                        yield
                S.dma("sp", uT_d[:, :, t * 128:(t + 1) * 128], u_b.ap, reads=[u_b], writes=[Rd["uT"]])
                if t == 17:
                    A.cp(halo[:, :, 0:1], u_b.ap[:, :, 127:128], [u_b], [HALO])
                if t == 2:
                    A.cp(halo[:, :, 1:2], u_b.ap[:, :, 0:1], [u_b], [HALO])
                pgb = 4 + t % 2
                pg = A.bank(pgb)
                A.mm([(pg[:, 0:32], u_b.ap[:, kc, :], wg[:, kc, :], kc == 0, kc == 15) for kc in range(16)],
                     [u_b, CONST], [PR[pgb]])
                yield
                A.cp(GALL.ap[:, t, :], pg[:, 0:32], [PR[pgb]], [GALLt[t]], eng="act" if t % 2 else "dve")
                yield

            A.interleave([modgen()] + [tileA(t) for t in range(NTILE)], 6, 5)
            NG = NTILE * 16
            A.tt(GSA.ap, GALL.ap, bif_bc[:].unsqueeze(1).broadcast_to([128, NTILE, 32]), ALU.add, GALLt + [CONST], [GSA])
            GS5 = GSA.ap.rearrange("p t (d j h) -> p t d j h", d=2, j=2)
            A.act(ELA.ap.rearrange("p t (d h) -> p t d h", d=2), GS5[:, :, :, 1, :], AF.Exp, [GSA], [ELA], scale=-1.0)
            A.act(LLA.ap, ELA.ap, AF.Ln, [ELA], [LLA], bias=1.0)
            pB = A.bank(0)
            pT = A.bank(1)
            A.mm([(pB[:, 0:144], trif, LLA.ap[:, :, 0:8], True, True),
                  (pB[:, 144:288], trib, LLA.ap[:, :, 8:16], True, True)], [LLA, CONST], [PR[0]])
            A.mm([(pT[:, 0:NG], ones, LLA.ap, True, True)], [LLA, CONST], [PR[1]])
            for d in range(2):
                pBd = pB[:, d * 144:(d + 1) * 144].rearrange("p (t h) -> p t h", h=8)
                A.tt(gat[:, 0, :, d * 8:(d + 1) * 8], GS5[:, :, d, 0, :], pBd, ALU.add, [GSA, PR[0]], [GAT])
                A.cp(gat[:, 1, :, d * 8:(d + 1) * 8], pBd, [PR[0]], [GAT], eng="act")
            A.cp(gat[:, 2, :, :], pT[:, 0:NG].rearrange("p (t h) -> p t h", h=16), [PR[1]], [GAT], eng="act")
            pX = A.bank(2)
            gA = gat[:, 0, :, :].rearrange("p t h -> p (t h)")
            A.tr([(pX[0:96, k * 128:(k + 1) * 128], gA[:, k * 96:(k + 1) * 96], ident) for k in range(3)], [GAT, CONST], [PR[2]])
            S.op("dve", lambda e: e.reduce_max(out=AMX.ap[0:96, 0:3], in_=pX[0:96, 0:384].rearrange("p (k t) -> p k t", t=128), axis=AX.X),
                 [PR[2]], [AMX])
            for k in range(3):
                A.ts(DGS.ap[0:96, k, :], ident[0:96, 0:96], AMX.ap[0:96, k:k + 1], ALU.mult, [AMX, CONST], [DGS])
            pC = A.bank(3)
            A.mm([(pC[:, k * 96:(k + 1) * 96], ones[0:96, :], DGS.ap[0:96, k, :], True, True) for k in range(3)], [DGS, CONST], [PR[3]])
            A.cp(gat[:, 3, :, :], pC[:, 0:NG].rearrange("p (t h) -> p t h", h=16), [PR[3]], [GAT], eng="act")

            if self.stop == "A3":
                S.dma("sp", dbg_d[:, 0:3456], gat.rearrange("p a b c -> p (a b c)"), reads=[GAT], writes=[Rd["dbg"]])
                return self.finish(nc, S)
            g_A, g_B, g_TOT, g_AMAX, g_SUF, g_TMP, g_MT, g_MNX, g_OFFK, g_LAMN, g_OFFP = range(11)
            seqs = {0: ([0, 1], list(range(10, 18))), 1: ([1, 0] + list(range(9, 1, -1)), list(range(17, 9, -1)))}
            for d in (0, 1):
                cd = slice(d * 8, d * 8 + 8)
                pre, own = seqs[d]
                prev = None
                for c in reversed(pre):
                    if prev is None:
                        A.ts(gat[:, g_SUF, c, cd], gat[:, g_TOT, c, cd], -1.0, ALU.mult, [GAT], [GAT])
                    else:
                        A.tt(gat[:, g_SUF, c, cd], gat[:, g_SUF, prev, cd], gat[:, g_TOT, c, cd], ALU.subtract, [GAT], [GAT])
                    prev = c
                mcur = mp[:, cd]
                for i, c in enumerate(pre):
                    A.tt(gat[:, g_TMP, c, cd], gat[:, g_AMAX, c, cd], gat[:, g_SUF, c, cd], ALU.add, [GAT], [GAT])
                    A.tt(mcur, gat[:, g_SUF, pre[0], cd] if i == 0 else mcur, gat[:, g_TMP, c, cd], ALU.max, [GAT], [GAT])
                mprev = mcur
                for c in own:
                    A.tt(gat[:, g_MT, c, cd], mprev, gat[:, g_AMAX, c, cd], ALU.max, [GAT], [GAT])
                    A.tt(gat[:, g_MNX, c, cd], gat[:, g_MT, c, cd], gat[:, g_TOT, c, cd], ALU.subtract, [GAT], [GAT])
                    mprev = gat[:, g_MNX, c, cd]
                for i, c in enumerate(own[:-1]):
                    nxt = own[i + 1]
                    A.stt(gat[:, g_OFFK, c, cd], gat[:, g_TOT, c, cd], -1.0, gat[:, g_MT, nxt, cd], ALU.mult, ALU.subtract, [GAT], [GAT])
                    A.tt(gat[:, g_LAMN, c, cd], gat[:, g_MNX, c, cd], gat[:, g_MT, nxt, cd], ALU.subtract, [GAT], [GAT])
                for c in pre:
                    A.tt(gat[:, g_OFFP, c, cd], gat[:, g_SUF, c, cd], gat[:, g_MT, own[0], cd], ALU.subtract, [GAT], [GAT])
            if self.stop == "A4":
                S.dma("sp", dbg_d[:, 0:3456], gat.rearrange("p a b c -> p (a b c)"), reads=[GAT], writes=[Rd["dbg"]])
                return self.finish(nc, S)
            fl = lambda ap: ap.rearrange("p a b -> p (a b)")
            own_s = slice(10, 18)
            A.tt(fl(gex[:, 0, own_s, :]), fl(gat[:, g_A, own_s, :]), fl(gat[:, g_MT, own_s, :]), ALU.subtract, [GAT], [GEX])
            A.tt(fl(gex[:, 1, own_s, :]), fl(gat[:, g_A, own_s, :]), fl(gat[:, g_OFFK, own_s, :]), ALU.add, [GAT], [GEX])
            A.tt(fl(gex[:, 2, own_s, :]), fl(gat[:, g_B, own_s, :]), fl(gat[:, g_MT, own_s, :]), ALU.subtract, [GAT], [GEX])
            A.cp(fl(gex[:, 3, own_s, :]), fl(gat[:, g_LAMN, own_s, :]), [GAT], [GEX])
            A.tt(fl(gex[:, 4, 0:10, :]), fl(gat[:, g_A, 0:10, :]), fl(gat[:, g_OFFP, 0:10, :]), ALU.add, [GAT], [GEX])
            for i4 in range(4):
                A.act(fl(gex[:, i4, own_s, :]), fl(gex[:, i4, own_s, :]), AF.Exp, [GEX], [GEX])
            A.act(fl(gex[:, 4, 0:10, :]), fl(gex[:, 4, 0:10, :]), AF.Exp, [GEX], [GEX])
            GS_, GK_, FL_, LAM_, GP_ = (gex[:, i, :, :] for i in range(5))
            if dbg_d is not None:
                S.dma("sp", dbg_d[:, 0:3456], gat.rearrange("p a b c -> p (a b c)"), reads=[GAT], writes=[Rd["dbg"]])
                S.dma("sp", dbg_d[:, 3456:4896], gex[:].rearrange("p a b c -> p (a b c)"), reads=[GEX], writes=[Rd["dbg"]])
            S.fence()
            if self.stop == "A":
                return self.finish(nc, S)

            St = A.at(103.25, [2, 8, 257], F32, "S")
            Sb_ = A.at(119.3125, [2, 8, 257], BF16, "Sb")
            SR = [[Region(f"S{d}{h}") for h in range(8)] for d in range(2)]
            SbR = [[Region(f"Sb{d}{h}") for h in range(8)] for d in range(2)]
            NB = 1281
            uToc = A.at(0, [KC, NB], BF16, "uToc")
            kToc = A.at(40.25, [8, 1280], BF16, "kToc")
            v1oc = A.at(60.25, [10, 8, 257], BF16, "v1oc")
            kpre = A.at(100.5, [1284], BF16, "kpre")
            ktp = Buf(A.at(0, [12, 8, 128], BF16).ap, uToc.rg)
            S.dma("sp", uToc.ap[:, :, 0:256], uT_d[:, :, 0:256], reads=[Rd["uT"]], writes=[uToc])
            S.dma("sp", uToc.ap[:, :, 257:1281], uT_d[:, :, 256:1280], reads=[Rd["uT"]], writes=[uToc])
            A.cp(uToc.ap[:, :, 256:257], halo[:, :, 0:1], [HALO], [uToc])
            S.op("dve", lambda e: e.memset(kpre.ap, 0.0), [], [kpre])
            ntl = [(0, 512), (512, 1024), (1024, NB)]
            if self.stop == "B1":
                return self.finish(nc, S)
            kpre2 = [kpre, A.at(61, [1284], BF16, "kpre2")]
            S.op("dve", lambda e: e.memset(kpre2[1].ap, 0.0), [], [kpre2[1]])
            wcurB = [None, None]

            def projB(h):
                if h % 4 == 0:
                    wcurB[0], wcurB[1] = A.ws_get()
                wsl, wr = wcurB
                sub = h % 4
                pb0 = 0 if h % 2 == 0 else 3
                pp = A.bank(pb0, 3)
                kp = kpre2[h % 2]
                mms = []
                for kc in range(16):
                    for (n0, n1) in ntl:
                        mms.append((pp[:, n0:n1], wsl[:, kc, sub * 128:(sub + 1) * 128], uToc.ap[:, kc, n0:n1], kc == 0, kc == 15))
                A.mm(mms, [uToc, wr], PR[pb0:pb0 + 3])
                A.cp(kp.ap[:, 1:257], pp[:, 0:256], PR[pb0:pb0 + 1], [kp], eng="act")
                A.cp(kp.ap[:, 258:1283], pp[:, 256:1281], PR[pb0:pb0 + 3], [kp])

            def convB(h):
                c = 8 + h
                kp = kpre2[h % 2]
                p6, p7 = A.bank(6), A.bank(7)
                mms = []
                for j in range(3):
                    mms.append((p6[:, 0:256], dq[:, c, j, :], kp.ap[:, j:j + 256], j == 0, j == 2))
                    mms.append((p7, dq[:, c, j, :], kp.ap[:, 258 + j:258 + j + 512], j == 0, j == 2))
                A.mm(mms, [kp, DQ], PR[6:8])
                A.act(kToc.ap[:, h, 0:256], p6[:, 0:256], AF.Silu, [PR[6], CONST], [kToc], bias=pfm[:, 48 + c:49 + c])
                A.act(kToc.ap[:, h, 256:768], p7, AF.Silu, [PR[7], CONST], [kToc], bias=pfm[:, 48 + c:49 + c])
                A.mm([(p6, dq[:, c, j, :], kp.ap[:, 770 + j:770 + j + 512], j == 0, j == 2) for j in range(3)], [kp, DQ], [PR[6]])
                A.act(kToc.ap[:, h, 768:1280], p6, AF.Silu, [PR[6], CONST], [kToc], bias=pfm[:, 48 + c:49 + c])

            projB(0)
            for h in range(8):
                if h + 1 < 8:
                    projB(h + 1)
                convB(h)
            S.op("dve", lambda e: e.memset(v1oc.ap.rearrange("p a b c -> p (a b c)"), 1.0), [], [v1oc, kpre2[1]])
            if self.stop == "B3":
                return self.finish(nc, S)
            tcol = lambda t: t * 128 if t < 2 else 257 + (t - 2) * 128
            for blk in range(4):
                wsl, wr = A.ws_get()
                for t in range(10):
                    b = 6 + t % 2
                    pb = A.bank(b)
                    A.mm([(pb, uToc.ap[:, kc, tcol(t):tcol(t) + 128], wsl[:, kc, :], kc == 0, kc == 15) for kc in range(16)],
                         [uToc, wr], [PR[b]])
                    A.cp(v1oc.ap[:, t, 2 * blk:2 * blk + 2, 0:256], pb.rearrange("p (h e) -> p h e", e=256), [PR[b]], [v1oc],
                         eng="act" if t % 2 else "dve")
            if self.stop == "B4":
                return self.finish(nc, S)
            kidx = {}
            n = 0
            for t in range(10):
                for d in ((0, 1) if t < 2 else (1,)):
                    kidx[(t, d)] = n
                    n += 1
            for t in range(10):
                b = t % 2
                pbb = A.bank(b).bitcast(BF16)
                kc0 = t * 128
                A.tr([(pbb[:, h * 128:(h + 1) * 128], kToc.ap[:, h, kc0:kc0 + 128], identb[:]) for h in range(8)],
                     [kToc, CONST], [PR[b]])
                for d in ((0, 1) if t < 2 else (1,)):
                    A.tt(ktp.ap[:, kidx[(t, d)], :, :], pbb.rearrange("p (h k) -> p h k", k=128),
                         GP_[:, t, d * 8:d * 8 + 8].unsqueeze(2).broadcast_to([128, 8, 128]), ALU.mult, [PR[b], GEX], [ktp])
            if self.stop == "B5":
                return self.finish(nc, S)
            for d in (0, 1):
                pre = seqs[d][0]
                for h in range(8):
                    b = 2 + h % 2
                    pb = A.bank(b)
                    A.mm([(pb[:, 0:257], ktp.ap[:, kidx[(t, d)], h, :], v1oc.ap[:, t, h, :], i == 0, i == len(pre) - 1)
                          for i, t in enumerate(pre)], [ktp, v1oc], [PR[b]])
                    if self.stop == "B6":
                        return self.finish(nc, S)
                    A.cp(St.ap[:, d, h, :], pb[:, 0:257], [PR[b]], [SR[d][h]], eng="act")
                    if self.stop == "B7":
                        return self.finish(nc, S)
                    A.cp(Sb_.ap[:, d, h, :], pb[:, 0:257], [PR[b]], [SbR[d][h]])
                    if self.stop == "B8":
                        return self.finish(nc, S)
            if dbg_d is not None:
                S.dma("sp", dbg_d[:, 4896:4896 + 4112], St.ap.rearrange("p a b c -> p (a b c)"),
                      reads=[r for rr in SR for r in rr], writes=[Rd["dbg"]])
            S.fence()
            if self.stop == "B":
                return self.finish(nc, S)

            qT = A.at(0, [8, NOWN], BF16, "qT")
            kT = A.at(16, [8, NOWN], BF16, "kT")
            v1 = A.at(32, [8, 8, 257], BF16, "v1")
            uTo = A.at(64.5, [KC, 1025], BF16, "uTo")
            pre_ = A.at(96.75, [1026], BF16, "pre")
            tmpq = A.at(99.0, [512], F32, "tmpq")
            S.dma("sp", uTo.ap[:, :, 0:1024], uT_d[:, :, 1280:2304], reads=[Rd["uT"]], writes=[uTo])
            A.cp(uTo.ap[:, :, 1024:1025], halo[:, :, 1:2], [HALO], [uTo])
            S.op("dve", lambda e: e.memset(pre_.ap, 0.0), [], [pre_])
            S.op("dve", lambda e: e.memset(v1.ap.rearrange("p a b c -> p (a b c)"), 1.0), [], [v1])
            ntl = [(0, 512), (512, 1024), (1024, 1025)]
            pre2 = [pre_, A.at(101, [1026], BF16, "pre2")]
            S.op("dve", lambda e: e.memset(pre2[1].ap, 0.0), [], [pre2[1]])
            wcur = [None, None]

            def projC(c):
                if c % 4 == 0:
                    wcur[0], wcur[1] = A.ws_get()
                wsl, wr = wcur
                sub = c % 4
                pb0 = 0 if c % 2 == 0 else 5
                pp = A.bank(pb0, 3)
                mms = []
                for kc in range(16):
                    for (n0, n1) in ntl:
                        mms.append((pp[:, n0:n1], wsl[:, kc, sub * 128:(sub + 1) * 128], uTo.ap[:, kc, n0:n1], kc == 0, kc == 15))
                A.mm(mms, [uTo, wr], PR[pb0:pb0 + 3])
                A.cp(pre2[c % 2].ap[:, 1:1026], pp[:, 0:1025], PR[pb0:pb0 + 3], [pre2[c % 2]])

            def convC(c):
                p_ = pre2[c % 2]
                pc_ = A.bank(3, 2)
                mms = []
                for j in range(3):
                    for hf in range(2):
                        mms.append((pc_[:, hf * 512:(hf + 1) * 512], dq[:, c, j, :], p_.ap[:, hf * 512 + j:hf * 512 + j + 512], j == 0, j == 2))
                A.mm(mms, [p_, DQ], PR[3:5])
                if c < 8:
                    for hf in range(2):
                        A.act(tmpq.ap, pc_[:, hf * 512:(hf + 1) * 512], AF.Silu, [PR[3 + hf], CONST], [tmpq], bias=pfm[:, 48 + c:49 + c])
                        A.ts(qT.ap[:, c, hf * 512:(hf + 1) * 512], tmpq.ap, QSCALE, ALU.mult, [tmpq], [qT])
                else:
                    A.act(kT.ap[:, c - 8, :], pc_, AF.Silu, PR[3:5] + [CONST], [kT], bias=pfm[:, 48 + c:49 + c])

            projC(0)
            for c in range(16):
                if c + 1 < 16:
                    projC(c + 1)
                convC(c)
            for blk in range(4):
                wsl, wr = A.ws_get()
                for t in range(8):
                    b = 6 + t % 2
                    pb = A.bank(b)
                    A.mm([(pb, uTo.ap[:, kc, t * 128:(t + 1) * 128], wsl[:, kc, :], kc == 0, kc == 15) for kc in range(16)],
                         [uTo, wr], [PR[b]])
                    A.cp(v1.ap[:, t, 2 * blk:2 * blk + 2, 0:256], pb.rearrange("p (h e) -> p h e", e=256), [PR[b]], [v1],
                         eng="act" if t % 2 else "dve")
            S.fence()

            ktl = [[A.at(65 + 4 * d + 2 * i, [8, 128], BF16, f"ktl{d}{i}") for i in range(2)] for d in range(2)]
            HT = [A.at(73 + 8 * d, [8, 256], F32, f"HT{d}") for d in range(2)]
            sTb = [[A.at(89 + 0.25 * (2 * d + i), [128], BF16, f"sTb{d}{i}") for i in range(2)] for d in range(2)]
            dn = [[A.at(90 + 0.0625 * (2 * d + i), [2], F32, f"dn{d}{i}") for i in range(2)] for d in range(2)]

            HTR = [[Region(f"HT{d}{g}") for g in range(2)] for d in range(2)]

            def scan_stream(d, hg):
                k_ = d * 2 + hg
                bX, bY = 2 * k_, 2 * k_ + 1
                X, Y = A.bank(bX), A.bank(bY)
                Ybf = Y.bitcast(BF16)
                heads = range(hg * 4, hg * 4 + 4)
                for i in range(8):
                    ti = i if d == 0 else 7 - i
                    tc_ = slice(ti * 128, (ti + 1) * 128)
                    gt = 10 + ti
                    mask = trif if d == 0 else trib
                    kt_ = ktl[d][i % 2]
                    ktR = ktlR[d][i % 2][hg]
                    if i < 7:
                        A.tr([(Ybf[:, 512 + j * 128:512 + (j + 1) * 128], kT.ap[:, h, tc_], identb[:]) for j, h in enumerate(heads)],
                             [kT, CONST], [PR[bY]])
                        yield
                        A.tt(kt_.ap[:, hg * 4:hg * 4 + 4, :], Ybf[:, 512:1024].rearrange("p (h k) -> p h k", k=128),
                             GK_[:, gt, d * 8 + hg * 4:d * 8 + hg * 4 + 4].unsqueeze(2).broadcast_to([128, 4, 128]), ALU.mult,
                             [PR[bY], GEX], [ktR])
                        yield
                    for h in heads:
                        col = d * 8 + h
                        psc = X[:, 0:128]
                        A.mm([(psc, kT.ap[:, h, tc_], qT.ap[:, h, tc_], True, True)], [kT, qT], [PR[bX]])
                        yield
                        sb_ = sTb[d][hg]
                        dn_ = dn[d][hg]
                        A.stt(sb_.ap, psc, GS_[:, gt, col:col + 1], mask, ALU.mult, ALU.mult, [PR[bX], GEX, CONST], [sb_])
                        yield
                        if i < 7:
                            A.mm([(Y[:, 0:256], kt_.ap[:, h, :], v1.ap[:, ti, h, 0:256], True, True)], [ktR, v1], [PR[bY]])
                            yield
                        pnd = X[:, 128:385]
                        mms = [(pnd, sb_.ap, v1.ap[:, ti, h, :], True, False),
                               (pnd, qT.ap[:, h, tc_], Sb_.ap[:, d, h, :], False, True)]
                        if i < 7:
                            mms.append((X[:, 385:386], kt_.ap[:, h, :], v1.ap[:, ti, h, 256:257], True, True))
                        A.mm(mms, [sb_, v1, qT, SbR[d][h], ktR], [PR[bX]])
                        yield
                        S.op("dve", lambda e, dn_=dn_, pnd=pnd: e.tensor_reduce(out=dn_.ap[:, 0:1], in_=pnd[:, 256:257], axis=AX.X, op=ALU.max,
                                                                                apply_absolute_value=True), [PR[bX]], [dn_])
                        yield
                        A.ts(dn_.ap[:, 0:1], dn_.ap[:, 0:1], FL_[:, gt, col:col + 1], ALU.max, [dn_, GEX], [dn_])
                        yield
                        S.op("dve", lambda e, dn_=dn_: e.reciprocal(out=dn_.ap[:, 1:2], in_=dn_.ap[:, 0:1]), [dn_], [dn_])
                        yield
                        A.act(HT[d].ap[:, h, :], pnd[:, 0:256], AF.Copy, [PR[bX], dn_], [HTR[d][hg]], scale=dn_.ap[:, 1:2])
                        yield
                        if i < 7:
                            A.stt(St.ap[:, d, h, 256:257], St.ap[:, d, h, 256:257], LAM_[:, gt, col:col + 1], X[:, 385:386], ALU.mult, ALU.add,
                                  [SR[d][h], PR[bX], GEX], [SR[d][h]])
                            yield
                            A.stt(St.ap[:, d, h, 0:256], St.ap[:, d, h, 0:256], LAM_[:, gt, col:col + 1], Y[:, 0:256], ALU.mult, ALU.add,
                                  [SR[d][h], PR[bY], GEX], [SR[d][h]])
                            yield
                            A.cp(Sb_.ap[:, d, h, :], St.ap[:, d, h, :], [SR[d][h]], [SbR[d][h]], eng="act")
                            yield
                    S.dma("sp", (HF_d if d == 0 else HB_d)[ti * 128:(ti + 1) * 128, hg * 1024:(hg + 1) * 1024],
                          HT[d].ap[:, hg * 4:hg * 4 + 4, :].rearrange("p h e -> p (h e)"),
                          reads=[HTR[d][hg]], writes=[Rd["HF" if d == 0 else "HB"]])
                    yield

            ktlR = [[[Region(f"ktl{d}{i}{g}") for g in range(2)] for i in range(2)] for d in range(2)]
            A.interleave([scan_stream(0, 0), scan_stream(1, 0), scan_stream(0, 1), scan_stream(1, 1)], 4, 0)
            S.fence()
            NBC = 4
            hfb = [A.at(0 + 8 * i, [D], F32, f"hfb{i}") for i in range(NBC)]
            hbb = [A.at(32 + 8 * i, [D], F32, f"hbb{i}") for i in range(NBC)]
            hnb = [A.at(64 + 8 * i, [D], F32, f"hnb{i}") for i in range(NBC)]
            h8 = [A.at(96 + 0.5 * i, [96], F32, f"h8{i}") for i in range(NBC)]

            def tileC(ti):
                a_, b_, n_, s8 = hfb[ti % NBC], hbb[ti % NBC], hnb[ti % NBC], h8[ti % NBC]
                rows = slice(ti * 128, (ti + 1) * 128)
                S.dma("sp", a_.ap, HF_d[rows, :], reads=[Rd["HF"]], writes=[a_])
                S.dma("sp", b_.ap, HB_d[rows, :], reads=[Rd["HB"]], writes=[b_])
                yield
                A.tt(a_.ap, a_.ap, b_.ap, ALU.add, [a_, b_], [a_])
                yield
                st8 = s8.ap[:, 0:48].rearrange("p (h s) -> p h s", s=6)
                mv8 = s8.ap[:, 48:64].rearrange("p (h s) -> p h s", s=2)
                rs8 = s8.ap[:, 64:72]
                nm8 = s8.ap[:, 72:80]
                for h in range(8):
                    S.op("dve", lambda e, h=h, a_=a_, st8=st8: e.bn_stats(out=st8[:, h, :], in_=a_.ap[:, h * 256:(h + 1) * 256]), [a_], [s8])
                    yield
                for h in range(8):
                    S.op("dve", lambda e, h=h, st8=st8, mv8=mv8: e.bn_aggr(out=mv8[:, h, :], in_=st8[:, h, :]), [s8], [s8])
                    yield
                A.act(rs8, mv8[:, :, 1], AF.Ln, [s8], [s8], bias=EPS)
                yield
                A.act(rs8, rs8, AF.Exp, [s8], [s8], scale=-0.5)
                yield
                A.stt(nm8, mv8[:, :, 0], -1.0, rs8, ALU.mult, ALU.mult, [s8], [s8])
                yield
                for h in range(8):
                    if h % 2 == 0:
                        A.act(n_.ap[:, h * 256:(h + 1) * 256], a_.ap[:, h * 256:(h + 1) * 256], AF.Identity, [a_, s8], [n_],
                              scale=rs8[:, h:h + 1], bias=nm8[:, h:h + 1])
                    else:
                        A.ts(n_.ap[:, h * 256:(h + 1) * 256], a_.ap[:, h * 256:(h + 1) * 256], rs8[:, h:h + 1], ALU.mult, [a_, s8], [n_],
                             s2=nm8[:, h:h + 1], op1=ALU.add)
                    yield
                S.dma("sp", HN_d[rows, :], n_.ap, reads=[n_], writes=[Rd["HN"]])
                yield

            A.interleave([tileC(ti) for ti in range(8)], 4, 9)
            S.fence()
            if self.stop == "C":
                return self.finish(nc, S)

            uT = A.at(0, [KC, NOWN], BF16, "uT")
            hmoT = A.at(32, [KC, NOWN], BF16, "hmoT")
            S.dma("sp", uT.ap, uT_d[:, :, 1280:2304], reads=[Rd["uT"]], writes=[uT])
            so = [A.at(64 + 2 * i, [512], F32, f"so{i}") for i in range(4)]
            hnk = [A.at(72 + 2 * i, [512], F32, f"hnk{i}") for i in range(4)]
            wD = [None, None]

            def d_mm(n):
                blk, t = divmod(n, 8)
                if t == 0:
                    wD[0], wD[1] = A.ws_get()
                wsl, wr = wD
                cols = slice(blk * 512, (blk + 1) * 512)
                so_, hn_ = so[n % 4], hnk[n % 4]
                b = n % 4
                pb = A.bank(b)
                S.dma("sp", hn_.ap, HN_d[t * 128:(t + 1) * 128, cols], reads=[Rd["HN"]], writes=[hn_])
                A.mm([(pb, uT.ap[:, kc, t * 128:(t + 1) * 128], wsl[:, kc, :], kc == 0, kc == 15) for kc in range(16)],
                     [uT, wr], [PR[b]])
                A.act(so_.ap, pb, AF.Sigmoid, [PR[b]], [so_])
                A.tt(so_.ap, so_.ap, hn_.ap, ALU.mult, [so_, hn_], [so_])

            def d_rest(n):
                blk, t = divmod(n, 8)
                so_ = so[n % 4]
                b2 = 4 + n % 4
                pt = A.bank(b2)
                A.tr([(pt[:, j * 128:(j + 1) * 128], so_.ap[:, j * 128:(j + 1) * 128], ident) for j in range(4)], [so_, CONST], [PR[b2]])
                A.tt(hmoT.ap[:, blk * 4:blk * 4 + 4, t * 128:(t + 1) * 128], pt.rearrange("p (a b) -> p a b", b=128),
                     pfm[:, 576 + blk * 4:576 + blk * 4 + 4].unsqueeze(2).broadcast_to([128, 4, 128]), ALU.mult,
                     [PR[b2], CONST], [hmoT])

            d_mm(0)
            d_mm(1)
            for n in range(32):
                if n + 2 < 32:
                    d_mm(n + 2)
                d_rest(n)
            S.fence()

            sg = [A.at(64 + 4 * i, [NOWN], F32, f"sg{i}") for i in range(2)]
            zmb = [A.at(72 + 4 * i, [NOWN], F32, f"zmb{i}") for i in range(2)]
            n = 0
            for blk in range(4):
                w1, r1_ = A.ws_get()
                w2, r2_ = A.ws_get()
                for sub in range(4):
                    c = blk * 4 + sub
                    sg_, zm_ = sg[n % 2], zmb[n % 2]
                    n += 1
                    pbase = 4 * (n % 2)
                    py = A.bank(pbase, 2)
                    pg_ = A.bank(pbase + 2, 2)
                    mms = []
                    for kc in range(16):
                        for hf in range(2):
                            mms.append((py[:, hf * 512:(hf + 1) * 512], w1[:, kc, sub * 128:(sub + 1) * 128],
                                        hmoT.ap[:, kc, hf * 512:(hf + 1) * 512], kc == 0, kc == 15))
                    A.mm(mms, [hmoT, r1_], PR[pbase:pbase + 2])
                    mms = []
                    for kc in range(16):
                        for hf in range(2):
                            mms.append((pg_[:, hf * 512:(hf + 1) * 512], w2[:, kc, sub * 128:(sub + 1) * 128],
                                        uT.ap[:, kc, hf * 512:(hf + 1) * 512], kc == 0, kc == 15))
                    A.mm(mms, [uT, r2_], PR[pbase + 2:pbase + 4])
                    A.act(sg_.ap, pg_, AF.Sigmoid, PR[pbase + 2:pbase + 4], [sg_])
                    A.tt(zm_.ap, sg_.ap, py, ALU.mult, [sg_] + PR[pbase:pbase + 2], [zm_])
                    S.dma("sp", ZM_d[c * 128:(c + 1) * 128, :], zm_.ap, reads=[zm_], writes=[Rd["ZM"]])
            S.fence()

            yc = A.at(32, [KC, NOWN], F32, "yc")
            ycin = Buf(A.at(0, [KC, NOWN], BF16).ap, uT.rg)
            ypad = [A.at(96 + 3 * i, [16, 94], BF16, f"ypad{i}") for i in range(2)]
            sgl = [A.at(102, [512], F32, "sgl0")] * 2
            sq = [A.at(104 + 4 * i, [NOWN], F32, f"sq{i}") for i in range(2)]
            dw = [A.at(112 + 7.75 * i, [31, 128], BF16, f"dw{i}") for i in range(2)]
            ypR = [[Region(f"ypR{i}{hf}") for hf in range(2)] for i in range(2)]
            for i in range(2):
                S.op("dve", lambda e, i=i: e.memset(ypad[i].ap.rearrange("p a b -> p (a b)"), 0.0), [], [ypad[i]] + ypR[i])
            STAT = PR[4:8]
            psum_s = A.bank(4, 2)
            psum_q = A.bank(6, 2)
            n = 0
            for c in range(16):
                wa, ra = A.ws_get() if c % 4 == 0 else (wa, ra)
                wl_, rl = A.ws_get() if c % 4 == 0 else (wl_, rl)
                sub = c % 4
                yp, dw_, sq_ = ypad[c % 2], dw[c % 2], sq[c % 2]
                for j in range(31):
                    A.ts(dw_.ap[:, j, :], ident, pfm[:, 64 + j * 16 + c:65 + j * 16 + c], ALU.mult, [CONST], [dw_])
                for hf in range(2):
                    pa = A.bank(0)
                    pl = A.bank(1)
                    sgl_ = sgl[n % 2]
                    n += 1
                    mms = []
                    for kc in range(16):
                        mms.append((pa, wa[:, kc, sub * 128:(sub + 1) * 128], uT.ap[:, kc, hf * 512:(hf + 1) * 512], kc == 0, kc == 15))
                        mms.append((pl, wl_[:, kc, sub * 128:(sub + 1) * 128], uT.ap[:, kc, hf * 512:(hf + 1) * 512], kc == 0, kc == 15))
                    A.mm(mms, [uT, ra, rl], PR[0:2])
                    A.act(sgl_.ap, pl, AF.Sigmoid, [PR[1]], [sgl_])
                    A.tt(yp.ap[:, hf * 8:(hf + 1) * 8, 15:79], sgl_.ap.rearrange("p (r t) -> p r t", t=64),
                         pa.rearrange("p (r t) -> p r t", t=64), ALU.mult, [sgl_, PR[0]], [ypR[c % 2][hf]])
                pcv = A.bank(2, 2)
                for hf in range(2):
                    A.mm([(pcv[:, hf * 512:(hf + 1) * 512], dw_.ap[:, j, :], yp.ap[:, hf * 8:(hf + 1) * 8, j:j + 64], j == 0, j == 30)
                          for j in range(31)], [ypR[c % 2][hf], dw_], [PR[2 + hf]])
                A.act(yc.ap[:, c, :], pcv, AF.Identity, PR[2:4] + [CONST], [yc], bias=pfm[:, 560 + c:561 + c])
                A.act(sq_.ap, yc.ap[:, c, :], AF.Square, [yc], [sq_])
                mms = []
                for hf in range(2):
                    mms.append((psum_s[:, hf * 512:(hf + 1) * 512], ones, yc.ap[:, c, hf * 512:(hf + 1) * 512], c == 0, c == 15))
                    mms.append((psum_q[:, hf * 512:(hf + 1) * 512], ones, sq_.ap[:, hf * 512:(hf + 1) * 512], c == 0, c == 15))
                A.mm(mms, [yc, sq_, CONST], STAT)
            mean_bc = A.at(96, [NOWN], F32, "mean_bc")
            rstd_bc = A.at(100, [NOWN], F32, "rstd_bc")
            tb = [A.at(104 + 4 * i, [NOWN], F32, f"tb{i}") for i in range(2)]
            S.fence()
            A.ts(mean_bc.ap, psum_s, 1.0 / D, ALU.mult, PR[4:6], [mean_bc])
            A.ts(rstd_bc.ap, psum_q, 1.0 / D, ALU.mult, PR[6:8], [rstd_bc])
            A.tt(tb[0].ap, mean_bc.ap, mean_bc.ap, ALU.mult, [mean_bc], [tb[0]])
            A.tt(rstd_bc.ap, rstd_bc.ap, tb[0].ap, ALU.subtract, [rstd_bc, tb[0]], [rstd_bc])
            A.act(rstd_bc.ap, rstd_bc.ap, AF.Ln, [rstd_bc], [rstd_bc], bias=EPS)
            A.act(rstd_bc.ap, rstd_bc.ap, AF.Exp, [rstd_bc], [rstd_bc], scale=-0.5)
            for c in range(16):
                t_ = tb[c % 2]
                A.tt(t_.ap, yc.ap[:, c, :], mean_bc.ap, ALU.subtract, [yc, mean_bc], [t_])
                A.tt(t_.ap, t_.ap, rstd_bc.ap, ALU.mult, [t_, rstd_bc], [t_])
                A.act(ycin.ap[:, c, :], t_.ap, AF.Silu, [t_, CONST], [ycin], scale=pfm[:, 592 + c:593 + c], bias=pfm[:, 608 + c:609 + c])
            S.fence()

            uT2 = A.at(32, [KC, NOWN], BF16, "uT2")
            zT = A.at(64, [KC, NOWN], BF16, "zT")
            S.dma("sp", uT2.ap, uT_d[:, :, 1280:2304], reads=[Rd["uT"]], writes=[uT2])
            sg = [A.at(96 + 4 * i, [NOWN], F32, f"sgb{i}") for i in range(2)]
            zmb = [A.at(104 + 4 * i, [NOWN], F32, f"zml{i}") for i in range(2)]
            n = 0
            for blk in range(4):
                w1, r1_ = A.ws_get()
                w2, r2_ = A.ws_get()
                for sub in range(4):
                    c = blk * 4 + sub
                    sg_, zm_ = sg[n % 2], zmb[n % 2]
                    n += 1
                    pbase = 4 * (n % 2)
                    py = A.bank(pbase, 2)
                    pg_ = A.bank(pbase + 2, 2)
                    S.dma("sp", zm_.ap, ZM_d[c * 128:(c + 1) * 128, :], reads=[Rd["ZM"]], writes=[zm_])
                    mms = []
                    for kc in range(16):
                        for hf in range(2):
                            mms.append((py[:, hf * 512:(hf + 1) * 512], w1[:, kc, sub * 128:(sub + 1) * 128],
                                        ycin.ap[:, kc, hf * 512:(hf + 1) * 512], kc == 0, kc == 15))
                    A.mm(mms, [ycin, r1_], PR[pbase:pbase + 2])
                    mms = []
                    for kc in range(16):
                        for hf in range(2):
                            mms.append((pg_[:, hf * 512:(hf + 1) * 512], w2[:, kc, sub * 128:(sub + 1) * 128],
                                        uT2.ap[:, kc, hf * 512:(hf + 1) * 512], kc == 0, kc == 15))
                    A.mm(mms, [uT2, r2_], PR[pbase + 2:pbase + 4])
                    A.act(sg_.ap, pg_, AF.Sigmoid, PR[pbase + 2:pbase + 4], [sg_])
                    A.tt(sg_.ap, sg_.ap, py, ALU.mult, [sg_] + PR[pbase:pbase + 2], [sg_])
                    A.tt(zT.ap[:, c, :], sg_.ap, zm_.ap, ALU.add, [sg_, zm_], [zT])
            S.fence()

            g1bc = A.at(0, [D], F32, "g1bc")
            S.dma("sp", g1bc.ap, mod_d[0:1, 2 * D:3 * D].broadcast_to([128, D]), reads=[Rd["mod"]], writes=[g1bc])
            xb = [A.at(8 + 2 * i, [512], F32, f"xb{i}") for i in range(2)]
            t1 = [A.at(12 + 2 * i, [512], F32, f"t1{i}") for i in range(2)]
            st1 = A.at(127, [8, 4, 6], F32, "st1")
            n = 0
            for blk in range(4):
                wsl, wr = A.ws_get()
                cols = slice(blk * 512, (blk + 1) * 512)
                for t in range(8):
                    b = n % 2
                    x_, t_ = xb[n % 2], t1[n % 2]
                    n += 1
                    pb = A.bank(b)
                    S.dma("sp", x_.ap, xin[(10 + t) * 128:(11 + t) * 128, cols], writes=[x_])
                    A.mm([(pb, zT.ap[:, kc, t * 128:(t + 1) * 128], wsl[:, kc, :], kc == 0, kc == 15) for kc in range(16)],
                         [zT, wr], [PR[b]])
                    A.tt(t_.ap, pb, g1bc.ap[:, cols], ALU.mult, [PR[b], g1bc], [t_])
                    A.stt(t_.ap, x_.ap, ALPHA, t_.ap, ALU.mult, ALU.add, [x_, t_], [t_])
                    S.op("dve", lambda e, t=t, blk=blk, t_=t_: e.bn_stats(out=st1.ap[:, t, blk, :], in_=t_.ap), [t_], [st1])
                    S.dma("sp", R1_d[t * 128:(t + 1) * 128, cols], t_.ap, reads=[t_], writes=[Rd["R1"]])
            S.fence()

            xmT = A.at(96, [KC, NOWN], BF16, "xmT")
            lg = A.at(0, [D], F32, "ln1g")
            lb = A.at(8, [D], F32, "ln1b")
            S.dma("sp", lg.ap, lnrow[0:1, :].broadcast_to([128, D]), writes=[lg])
            S.dma("sp", lb.ap, lnrow[1:2, :].broadcast_to([128, D]), writes=[lb])
            NBH = 4
            rt = [A.at(16 + 8 * i, [D], F32, f"rt{i}") for i in range(NBH)]
            x1b = [A.at(48 + 8 * i, [D], F32, f"x1b{i}") for i in range(NBH)]
            s1 = [A.at(88 + 0.0625 * i, [4], F32, f"s1{i}") for i in range(NBH)]

            def tileH(t):
                r_, x_, s_ = rt[t % NBH], x1b[t % NBH], s1[t % NBH]
                y_ = r_
                rows = slice(t * 128, (t + 1) * 128)
                S.dma("sp", r_.ap, R1_d[rows, :], reads=[Rd["R1"]], writes=[r_])
                S.op("dve", lambda e, t=t, s_=s_: e.bn_aggr(out=s_.ap[:, 0:2], in_=st1.ap[:, t, :, :].rearrange("p a b -> p (a b)")), [st1], [s_])
                yield
                yield from A.rstd_chain_g(s_.ap[:, 1:2], s_.ap[:, 0:1], s_.ap[:, 2:3], s_.ap[:, 3:4], s_.rg)
                A.act(x_.ap, r_.ap, AF.Identity, [r_, s_], [x_], scale=s_.ap[:, 2:3], bias=s_.ap[:, 3:4])
                yield
                A.tt(x_.ap, x_.ap, lg.ap, ALU.mult, [x_, lg], [x_])
                yield
                A.tt(x_.ap, x_.ap, lb.ap, ALU.add, [x_, lb], [x_])
                yield
                S.dma("sp", X1_d[rows, :], x_.ap, reads=[x_], writes=[Rd["X1"]])
                rs, nmr, R = yield from A.ln_stats_g(x_.ap, x_.rg)
                A.act(y_.ap, x_.ap, AF.Identity, [x_, R], [y_], scale=rs, bias=nmr)
                yield
                for g in range(4):
                    bi = (t * 4 + g) % 8
                    pb = A.bank(bi)
                    A.tr([(pb[:, j * 128:(j + 1) * 128], y_.ap[:, (g * 4 + j) * 128:(g * 4 + j + 1) * 128], ident) for j in range(4)],
                         [y_, CONST], [PR[bi]])
                    yield
                    for j in range(4):
                        kc = g * 4 + j
                        if (j + g) % 2 == 0:
                            A.ts(xmT.ap[:, kc, rows], pb[:, j * 128:(j + 1) * 128], msc[:, 5, kc:kc + 1], ALU.mult,
                                 [PR[bi], MSC2], [xmTR[t]], s2=msc[:, 4, kc:kc + 1], op1=ALU.add)
                        else:
                            A.act(xmT.ap[:, kc, rows], pb[:, j * 128:(j + 1) * 128], AF.Identity, [PR[bi], MSC2], [xmTR[t]],
                                  scale=msc[:, 5, kc:kc + 1], bias=msc[:, 4, kc:kc + 1])
                        yield

            xmTR = [Region(f"xmT{t}") for t in range(8)]
            A.interleave([tileH(t) for t in range(8)], 3, 14)
            S.op("dve", lambda e: e.memset(mp[:, 0:1], 0.0), xmTR, [xmT])
            S.fence()
            if self.stop == "H":
                return self.finish(nc, S)

            hid = A.at(0, [22, NOWN], BF16, "hid")
            sa = [A.at(44 + 2 * i, [512], F32, f"sa{i}") for i in range(2)]
            g2bc = A.at(48, [D], F32, "g2bc")
            S.dma("sp", g2bc.ap, mod_d[0:1, 5 * D:6 * D].broadcast_to([128, D]), reads=[Rd["mod"]], writes=[g2bc])
            xb = [A.at(56 + 2 * i, [512], F32, f"x1k{i}") for i in range(2)]
            t1 = [A.at(60 + 2 * i, [512], F32, f"t2{i}") for i in range(2)]
            st2 = A.at(127, [8, 4, 6], F32, "st2")
            for hh in range(2):
                n = 0
                for r in range(6):
                    wa, ra = A.ws_get()
                    wb_, rb = A.ws_get()
                    nsub = 4 if r < 5 else 2
                    for sub in range(nsub):
                        j = r * 4 + sub
                        for hf in range(2):
                            b = (n % 2) * 2
                            sa_ = sa[n % 2]
                            n += 1
                            pa, pb_ = A.bank(b), A.bank(b + 1)
                            mms = []
                            for kc in range(16):
                                mms.append((pa, wa[:, kc, sub * 128:(sub + 1) * 128], xmT.ap[:, kc, hf * 512:(hf + 1) * 512], kc == 0, kc == 15))
                                mms.append((pb_, wb_[:, kc, sub * 128:(sub + 1) * 128], xmT.ap[:, kc, hf * 512:(hf + 1) * 512], kc == 0, kc == 15))
                            A.mm(mms, [xmT, ra, rb], PR[b:b + 2])
                            A.act(sa_.ap, pa, AF.Silu, [PR[b]], [sa_])
                            A.tt(hid.ap[:, j, hf * 512:(hf + 1) * 512], sa_.ap, pb_, ALU.mult, [sa_, PR[b + 1]], [hid])
                S.fence()
                n = 0
                for blk in range(4):
                    cols = slice(blk * 512, (blk + 1) * 512)
                    for pc in range(2):
                        wsl, wr = A.ws_get()
                        for t in range(8):
                            pb = A.bank(t)
                            A.mm([(pb, hid.ap[:, pc * 11 + k, t * 128:(t + 1) * 128], wsl[:, k, :], pc == 0 and k == 0, pc == 1 and k == 10)
                                  for k in range(11)], [hid, wr], [PR[t]])
                    for t in range(8):
                        pb = A.bank(t)
                        x_, t_ = xb[n % 2], t1[n % 2]
                        n += 1
                        rows = slice(t * 128, (t + 1) * 128)
                        A.tt(t_.ap, pb, g2bc.ap[:, cols], ALU.mult, [PR[t], g2bc], [t_])
                        if hh == 0:
                            S.dma("sp", x_.ap, X1_d[rows, cols], reads=[Rd["X1"]], writes=[x_])
                            A.stt(t_.ap, x_.ap, ALPHA, t_.ap, ALU.mult, ALU.add, [x_, t_], [t_])
                        else:
                            S.dma("sp", x_.ap, R2_d[rows, cols], reads=[Rd["R2"]], writes=[x_])
                            A.tt(t_.ap, t_.ap, x_.ap, ALU.add, [x_, t_], [t_])
                            S.op("dve", lambda e, t=t, blk=blk, t_=t_: e.bn_stats(out=st2.ap[:, t, blk, :], in_=t_.ap), [t_], [st2])
                        S.dma("sp", R2_d[rows, cols], t_.ap, reads=[t_], writes=[Rd["R2"]])
                S.fence()

            lg = A.at(0, [D], F32, "ln2g")
            lb = A.at(8, [D], F32, "ln2b")
            S.dma("sp", lg.ap, lnrow[2:3, :].broadcast_to([128, D]), writes=[lg])
            S.dma("sp", lb.ap, lnrow[3:4, :].broadcast_to([128, D]), writes=[lb])
            NBJ = 4
            rt = [A.at(16 + 8 * i, [D], F32, f"rt2{i}") for i in range(NBJ)]
            ob = [A.at(48 + 8 * i, [D], F32, f"ob{i}") for i in range(NBJ)]
            s1 = [A.at(88 + 0.0625 * i, [4], F32, f"s2{i}") for i in range(NBJ)]

            def tileJ(t):
                r_, o_, s_ = rt[t % NBJ], ob[t % NBJ], s1[t % NBJ]
                rows = slice(t * 128, (t + 1) * 128)
                S.dma("sp", r_.ap, R2_d[rows, :], reads=[Rd["R2"]], writes=[r_])
                S.op("dve", lambda e, t=t, s_=s_: e.bn_aggr(out=s_.ap[:, 0:2], in_=st2.ap[:, t, :, :].rearrange("p a b -> p (a b)")), [st2], [s_])
                yield
                yield from A.rstd_chain_g(s_.ap[:, 1:2], s_.ap[:, 0:1], s_.ap[:, 2:3], s_.ap[:, 3:4], s_.rg)
                A.act(o_.ap, r_.ap, AF.Identity, [r_, s_], [o_], scale=s_.ap[:, 2:3], bias=s_.ap[:, 3:4])
                yield
                A.tt(o_.ap, o_.ap, lg.ap, ALU.mult, [o_, lg], [o_])
                yield
                A.tt(o_.ap, o_.ap, lb.ap, ALU.add, [o_, lb], [o_])
                yield
                S.dma("sp", out[rows, :], o_.ap, reads=[o_], writes=[Rd["out"]])
                yield

            A.interleave([tileJ(t) for t in range(8)], 4, 2)
            return self.finish(nc, S)

    def finish(self, nc, S):
        S.wait_final()
        S.emit()
        return nc


def _fm(v):
    return np.ascontiguousarray(v.reshape(16, 128).T)


def make_in_maps(x, c, ctx, c_ctx, w_mod, b_mod, w_in, b_if, w_qk_conv, b_qk_conv, mh_norm_g, w_dw, b_dw,
                 conv_ln_g, conv_ln_b, w_proj_m, w_proj_c, w_out, ln1_g, ln1_b, w_ffn_in, w_ffn_out, ln2_g, ln2_b):
    f32 = np.float32
    A = lambda a: np.ascontiguousarray(np.asarray(a, dtype=f32))
    w_mod0, w_in0 = A(w_mod[0]), A(w_in[0])
    w_g_n = np.ascontiguousarray(w_in0[:, 6144:6176])
    w_g_f = np.ascontiguousarray(np.concatenate([w_in0[:, 6160:6176], w_in0[:, 6144:6160]], axis=1))
    bif_n = A(b_if[0]).reshape(1, 32)
    bif_f = np.ascontiguousarray(np.concatenate([bif_n[:, 16:32], bif_n[:, 0:16]], axis=1))
    cf = np.zeros((128, 4, 128), f32)
    cf[:, 0, :] = np.eye(128)
    idx = np.arange(128)
    cf[:, 1, :] = (idx[:, None] <= idx[None, :])
    cf[:, 2, :] = (idx[:, None] >= idx[None, :])
    cf[:, 3, :] = 1.0
    cf = cf.reshape(128, 512)
    identb = np.eye(128, dtype=f32).astype(ml_dtypes.bfloat16)
    lnrow = A(np.stack([ln1_g[0], ln1_b[0], ln2_g[0], ln2_b[0]]))
    shared = dict(w_mod=w_mod0, bmod=A(b_mod[0]).reshape(1, -1), w_in=w_in0, w_pm=A(w_proj_m[0]), w_pc=A(w_proj_c[0]),
                  w_o=A(w_out[0]), lnrow=lnrow, w_f1=A(w_ffn_in[0]), w_f2=A(w_ffn_out[0]), cf32=cf, identb=identb)

    def pfm_of(flip):
        p = np.zeros((128, NPF), f32)
        wq = A(w_qk_conv[0])
        wd = A(w_dw[0])
        if flip:
            wq = wq[::-1]
            wd = wd[::-1]
        for j in range(3):
            p[:, j * 16:(j + 1) * 16] = _fm(wq[j])
        p[:, 48:64] = _fm(A(b_qk_conv[0]))
        for j in range(31):
            p[:, 64 + j * 16:64 + (j + 1) * 16] = _fm(wd[j])
        p[:, 560:576] = _fm(A(b_dw[0]))
        p[:, 576:592] = _fm(A(mh_norm_g[0]))
        p[:, 592:608] = _fm(A(conv_ln_g[0]))
        p[:, 608:624] = _fm(A(conv_ln_b[0]))
        return p
    pfms = [pfm_of(False), pfm_of(True)]
    x, ctx, c, c_ctx = A(x), A(ctx), A(c), A(c_ctx)
    in_maps = []
    for r in range(8):
        b, s = r // 2, r % 2
        xb, cb = x[b], ctx[b]
        if s:
            xb, cb = xb[::-1], cb[::-1]
        xin = np.ascontiguousarray(np.concatenate([cb, xb[1024:], xb[:1024]], axis=0))
        cv = np.zeros((128, 16, 2), f32)
        cv[:, :, 0] = _fm(c[b])
        cv[:, :, 1] = _fm(c_ctx)
        m = dict(shared)
        m.update(xin=xin, cvec=cv.reshape(128, 32), w_g=w_g_f if s else w_g_n, bif=bif_f if s else bif_n, pfm=pfms[s])
        in_maps.append(m)
    return in_maps


_NC_CACHE = {}


def kernel(**inputs):
    if "nc" not in _NC_CACHE:
        bld = Builder()
        bld.stop = None
        _NC_CACHE["nc"] = bld.build()
    nc = _NC_CACHE["nc"]
    in_maps = make_in_maps(**inputs)
    res = run_bass_kernel_spmd(nc, in_maps, core_ids=list(range(8)))
    out = np.zeros((4, 2048, 2048), np.float32)
    for r in range(8):
        b, s = r // 2, r % 2
        o = np.asarray(res.results[r]["out"], dtype=np.float32)
        if s:
            out[b, 1024:] = o[::-1]
        else:
            out[b, :1024] = o
    return out
```

```python
import contextlib
import numpy as np
import ml_dtypes
import concourse.bass as bass
import concourse.mybir as mybir
from concourse.bass_utils import run_bass_kernel_spmd

dt = mybir.dt
AF = mybir.ActivationFunctionType
ALU = mybir.AluOpType
AX = mybir.AxisListType
F32 = dt.float32
BF16 = dt.bfloat16

D = 2048
KC = 16
NOWN = 1024
NTILE = 18
W_IN_COLS = 14368
D_FF = 5632
EPS = 1e-5
ALPHA = 2.0 ** 0.25
QSCALE = 128.0 ** -0.5
NPF = 624
KIB = 1024


class Region:
    __slots__ = ("name", "w", "r", "excl")

    def __init__(self, name="", excl=False):
        self.name = name
        self.w = None
        self.r = {}
        self.excl = excl


class Buf:
    __slots__ = ("ap", "rg")

    def __init__(self, ap, rg=None, name=""):
        self.ap = ap
        self.rg = rg if rg is not None else Region(name)


class Sched:
    ENG = ("pe", "act", "dve", "pool", "sp")

    def __init__(self, nc, stack, n_sp=24, n_pool=8):
        self.nc = nc
        self.sem = {e: stack.enter_context(nc.semaphore("c_" + e)) for e in ("pe", "act", "dve", "pool")}
        self.cnt = {e: 0 for e in self.sem}
        self.dsem = [stack.enter_context(nc.semaphore(f"d{i}")) for i in range(n_sp + n_pool)]
        self.dcnt = [0] * (n_sp + n_pool)
        self.n_sp = n_sp
        self.n_pool = n_pool
        self.nx = {"sp": 0, "pool": 0}
        self.prog = {e: [] for e in self.ENG}
        self.waited = {e: {} for e in self.ENG}

    def _semobj(self, key):
        return self.sem[key] if isinstance(key, str) else self.dsem[key]

    def _deps(self, reads, writes, extra):
        d = {}

        def add(k, v):
            if d.get(k, 0) < v:
                d[k] = v

        for r in reads:
            if r.w is not None:
                add(*r.w)
        for w in writes:
            if w.w is not None:
                add(*w.w)
            for k, v in w.r.items():
                add(k, v)
        for t in extra:
            if t is not None:
                add(*t)
        return d

    def _waits(self, eng, d):
        wl = []
        wd = self.waited[eng]
        for k, v in d.items():
            if wd.get(k, 0) < v:
                wd[k] = v
                wl.append((k, v))
        return wl

    def _update(self, tok, reads, writes):
        for w in writes:
            w.w = tok
            w.r = {}
        for r in reads:
            if r.r.get(tok[0], 0) < tok[1]:
                r.r[tok[0]] = tok[1]

    def op(self, eng, fn, reads=(), writes=(), extra=()):
        reads = [b.rg if isinstance(b, Buf) else b for b in reads]
        writes = [b.rg if isinstance(b, Buf) else b for b in writes]
        writes = writes + [r for r in reads if r.excl]
        reads = [r for r in reads if not r.excl]
        d = self._deps(reads, writes, extra)
        wl = self._waits(eng, d)
        self.cnt[eng] += 1
        tok = (eng, self.cnt[eng])
        self.prog[eng].append((wl, fn, (eng, 1)))
        self._update(tok, reads, writes)
        return tok

    def dma(self, q, out, in_, reads=(), writes=(), extra=(), slow=False):
        reads = [b.rg if isinstance(b, Buf) else b for b in reads]
        writes = [b.rg if isinstance(b, Buf) else b for b in writes]
        if q == "sp":
            i = self.nx["sp"]
            self.nx["sp"] = (i + 1) % self.n_sp
        else:
            i = self.n_sp + self.nx["pool"]
            self.nx["pool"] = (self.nx["pool"] + 1) % self.n_pool
        d = self._deps(reads, writes, extra)
        if self.dcnt[i] > 0 and d.get(i, 0) < self.dcnt[i]:
            d[i] = self.dcnt[i]
        wl = self._waits(q, d)
        self.dcnt[i] += 16
        tok = (i, self.dcnt[i])

        def fn(e, out=out, in_=in_, slow=slow):
            if slow:
                return e.dma_start(out=out, in_=in_, allow_slow_non_contiguous=True)
            return e.dma_start(out=out, in_=in_)

        self.prog[q].append((wl, fn, (i, 16)))
        self._update(tok, reads, writes)
        return tok

    def fence(self):
        toks = {}
        for e in ("pe", "act", "dve"):
            if self.cnt[e]:
                toks[e] = self.cnt[e]
        for i in range(self.n_sp):
            if self.dcnt[i]:
                toks[i] = self.dcnt[i]
        for e in ("pe", "act", "dve", "sp"):
            wl = self._waits(e, dict(toks))
            if wl:
                self.prog[e].append((wl, None, None))

    def wait_final(self):
        toks = {}
        for i in range(self.n_sp):
            if self.dcnt[i]:
                toks[i] = self.dcnt[i]
        wl = self._waits("sp", toks)
        self.prog["sp"].append((wl, None, None))

    def emit(self):
        nc = self.nc
        with nc.Block() as block:
            def mk(name):
                def body(e):
                    for wl, fn, inc in self.prog[name]:
                        for k, v in wl:
                            e.wait_ge(self._semobj(k), v)
                        if fn is None:
                            continue
                        ins = fn(e)
                        ins.then_inc(self._semobj(inc[0]), inc[1])
                return body

            block.tensor(mk("pe"))
            block.scalar(mk("act"))
            block.vector(mk("dve"))
            block.gpsimd(mk("pool"))
            block.sync(mk("sp"))


class Builder:
    def __init__(self, dump=None, stop=None):
        self.dump = dump
        self.stop = stop

    def act(self, out, in_, func, reads, writes, scale=None, bias=None):
        def fn(e):
            kw = {}
            if scale is not None:
                kw["scale"] = scale
            if bias is not None:
                kw["bias"] = bias
            return e.activation(out=out, in_=in_, func=func, **kw)
        return self.S.op("act", fn, reads, writes)

    def tt(self, out, in0, in1, op, reads, writes, eng="dve"):
        return self.S.op(eng, lambda e: e.tensor_tensor(out=out, in0=in0, in1=in1, op=op), reads, writes)

    def ts(self, out, in0, s1, op0, reads, writes, s2=None, op1=None, eng="dve"):
        def fn(e):
            if op1 is None:
                return e.tensor_scalar(out=out, in0=in0, scalar1=s1, scalar2=None, op0=op0)
            return e.tensor_scalar(out=out, in0=in0, scalar1=s1, scalar2=s2, op0=op0, op1=op1)
        return self.S.op(eng, fn, reads, writes)

    def stt(self, out, in0, scalar, in1, op0, op1, reads, writes):
        return self.S.op("dve", lambda e: e.scalar_tensor_tensor(out=out, in0=in0, scalar=scalar, in1=in1, op0=op0, op1=op1),
                         reads, writes)

    def cp(self, out, in_, reads, writes, eng="dve"):
        if eng == "act":
            return self.act(out, in_, AF.Copy, reads, writes)
        return self.S.op(eng, lambda e: e.tensor_copy(out=out, in_=in_), reads, writes)

    def mm(self, mms, reads, writes):
        def fn(e):
            ins = None
            for (o, l, r, st, sp) in mms:
                ins = e.matmul(o, lhsT=l, rhs=r, start=st, stop=sp)
            return ins
        return self.S.op("pe", fn, reads, writes)

    def tr(self, trs, reads, writes):
        def fn(e):
            ins = None
            for (o, i, idn) in trs:
                ins = e.transpose(out=o, in_=i, identity=idn)
            return ins
        return self.S.op("pe", fn, reads, writes)

    def at(self, off_kib, shape, dtype, name=""):
        n = int(np.prod(shape))
        esz = 4 if dtype == F32 else 2
        w0 = int(round(off_kib * KIB)) // 4
        nw = (n * esz + 3) // 4
        assert w0 + nw <= self.arena_words, (name, off_kib, shape)
        ap = self.arena[:, w0:w0 + nw]
        if dtype != F32:
            ap = ap.bitcast(dtype)
        ap = ap[:, 0:n]
        if len(shape) == 2:
            ap = ap.rearrange("p (a b) -> p a b", b=shape[1])
        elif len(shape) == 3:
            ap = ap.rearrange("p (a b c) -> p a b c", b=shape[1], c=shape[2])
        return Buf(ap, name=name)

    def bank(self, i, n=1):
        return self.psum[:, i * 512:(i + n) * 512]

    def ws_plan(self, W, r0, kcn, c0, ncols):
        self.wplan.append((W, r0, kcn, c0, ncols))

    def ws_get(self):
        i = self.wcons
        self.wcons += 1
        while self.wissued < len(self.wplan) and self.wissued < i + self.nslot - 1:
            j = self.wissued
            W, r0, kcn, c0, ncols = self.wplan[j]
            sl = j % self.nslot
            dst = self.ring[:, sl, 0:kcn * ncols].rearrange("p (k n) -> p k n", n=ncols)
            src = W[r0:r0 + kcn * 128, c0:c0 + ncols].rearrange("(k p) n -> p k n", p=128)
            self.S.dma("pool", dst, src, writes=[self.ringR[sl]])
            self.wissued += 1
        W, r0, kcn, c0, ncols = self.wplan[i]
        sl = i % self.nslot
        return self.ring[:, sl, 0:kcn * ncols].rearrange("p (k n) -> p k n", n=ncols), self.ringR[sl]

    @staticmethod
    def interleave(gens, width, admit_every):
        active = []
        it = iter(gens)
        more = True
        since = admit_every
        while True:
            if more and len(active) < width and (since >= admit_every or not active):
                try:
                    active.append(next(it))
                    since = 0
                except StopIteration:
                    more = False
            if not active:
                if not more:
                    break
                continue
            for g in list(active):
                try:
                    next(g)
                except StopIteration:
                    active.remove(g)
            since += 1

    def ln_stats_g(self, src, src_rg, parts=4, width=512):
        i = self.lni
        self.lni = (i + 1) % len(self.lnst)
        st = self.lnst[i]
        R = self.lnR[i]
        S = self.S
        stats = st[:, 0:parts * 6].rearrange("p (a b) -> p a b", b=6)
        mv = st[:, 48:50]
        rs = st[:, 50:51]
        nmr = st[:, 51:52]
        for a in range(parts):
            S.op("dve", lambda e, a=a: e.bn_stats(out=stats[:, a, :], in_=src[:, a * width:(a + 1) * width]), [src_rg], [R])
            yield
        S.op("dve", lambda e: e.bn_aggr(out=mv, in_=st[:, 0:parts * 6]), [R], [R])
        yield
        yield from self.rstd_chain_g(mv[:, 1:2], mv[:, 0:1], rs, nmr, R)
        return rs, nmr, R

    def rstd_chain_g(self, var, mean, rs, nmr, R):
        self.act(rs, var, AF.Ln, [R], [R], bias=EPS)
        yield
        self.act(rs, rs, AF.Exp, [R], [R], scale=-0.5)
        yield
        self.stt(nmr, mean, -1.0, rs, ALU.mult, ALU.mult, [R], [R])
        yield

    def ln_stats(self, src, src_rg, parts=4, width=512):
        i = self.lni
        self.lni = (i + 1) % 4
        st = self.lnst[i]
        R = self.lnR[i]
        S = self.S
        stats = st[:, 0:parts * 6].rearrange("p (a b) -> p a b", b=6)
        mv = st[:, 48:50]
        rs = st[:, 50:51]
        nmr = st[:, 51:52]
        for a in range(parts):
            S.op("dve", lambda e, a=a: e.bn_stats(out=stats[:, a, :], in_=src[:, a * width:(a + 1) * width]), [src_rg], [R])
        S.op("dve", lambda e: e.bn_aggr(out=mv, in_=st[:, 0:parts * 6]), [R], [R])
        self.rstd_chain(mv[:, 1:2], mv[:, 0:1], rs, nmr, R)
        return rs, nmr, R

    def rstd_chain(self, var, mean, rs, nmr, R):
        for _ in self.rstd_chain_g(var, mean, rs, nmr, R):
            pass

    def build(self):
        nc = bass.Bass("TRN2", target_bir_lowering=False)
        self.nc = nc
        di = lambda n, sh, d=F32: nc.dram_tensor(n, sh, d, kind="ExternalInput").ap()
        ds = lambda n, sh, d=F32: nc.dram_tensor(n, sh, d, kind="Internal").ap()
        xin = di("xin", [NTILE * 128, D])
        cvec = di("cvec", [128, 32])
        w_mod = di("w_mod", [D, 6 * D])
        bmod = di("bmod", [1, 6 * D])
        w_in = di("w_in", [D, W_IN_COLS])
        w_g = di("w_g", [D, 32])
        bif = di("bif", [1, 32])
        pfm_d = di("pfm", [128, NPF])
        w_pm = di("w_pm", [D, D])
        w_pc = di("w_pc", [D, D])
        w_o = di("w_o", [D, D])
        lnrow = di("lnrow", [4, D])
        w_f1 = di("w_f1", [D, 2 * D_FF])
        w_f2 = di("w_f2", [D_FF, D])
        cf_d = di("cf32", [128, 4 * 128])
        idb_d = di("identb", [128, 128], BF16)
        out = nc.dram_tensor("out", [NOWN, D], F32, kind="ExternalOutput").ap()
        dumps = {}
        if self.dump:
            for n, sh, d in self.dump:
                dumps[n] = nc.dram_tensor("dump_" + n, sh, d, kind="ExternalOutput").ap()

        def scratch(n, sh, d=F32):
            return dumps[n] if n in dumps else ds(n, sh, d)
        mod_d = scratch("mod", [2, 6 * D])
        uT_d = scratch("uT", [128, KC, NTILE * 128], BF16)
        HF_d = scratch("HF", [NOWN, D])
        HB_d = scratch("HB", [NOWN, D])
        HN_d = scratch("HN", [NOWN, D])
        ZM_d = scratch("ZM", [D, NOWN])
        R1_d = scratch("R1", [NOWN, D])
        X1_d = scratch("X1", [NOWN, D])
        R2_d = scratch("R2", [NOWN, D])
        dbg_d = dumps.get("dbg")
        Rd = {n: Region(n) for n in ["mod", "uT", "HF", "HB", "HN", "ZM", "R1", "X1", "R2", "out", "dbg"]}

        with contextlib.ExitStack() as st:
            S = Sched(nc, st)
            self.S = S
            sb = lambda n, sh, d: st.enter_context(nc.sbuf_tensor(n, sh, d))
            self.nslot = 3
            ring = sb("ring", [128, self.nslot, 8192], BF16)
            self.ring = ring
            self.ringR = [Region(f"ring{i}") for i in range(self.nslot)]
            self.wplan, self.wcons, self.wissued = [], 0, 0
            ARENA_KIB = 128
            self.arena_words = int(ARENA_KIB * KIB) // 4
            self.arena = sb("arena", [128, self.arena_words], F32)
            self.psum = st.enter_context(nc.psum_tensor("psum", [128, 4096], F32))
            PR = [Region(f"pb{i}", excl=True) for i in range(8)]
            cf = sb("cf", [128, 4, 128], F32)
            ident, trif, trib, ones = cf[:, 0, :], cf[:, 1, :], cf[:, 2, :], cf[:, 3, :]
            identb = sb("identb_s", [128, 128], BF16)
            pfm = sb("pfm_s", [128, NPF], F32)
            wg = sb("wg", [128, KC, 32], BF16)
            bif_bc = sb("bif_bc", [128, 32], F32)
            dq = sb("dq", [128, 16, 3, 128], BF16)
            modT = sb("modT", [128, 96, 2], F32)
            msc = sb("msc", [128, 6, 16], F32)
            lnst_t = sb("lnst", [128, 8, 64], F32)
            self.lnst = [lnst_t[:, i, :] for i in range(8)]
            self.lnR = [Region(f"lnst{i}") for i in range(8)]
            self.lni = 0
            gex = sb("gex", [128, 5, NTILE, 16], F32)
            gsm = sb("gsm", [128, 7, 96], F32)
            mp = sb("mp", [128, 16], F32)
            halo = sb("halo", [128, KC, 2], BF16)
            CONST, GEX, MSC, HALO, DQ = (Region(n) for n in ["const", "gex", "msc", "halo", "dq"])
            GSM = [Region(f"gsm{i}") for i in range(7)]
            A = self

            S.dma("sp", cf[:].rearrange("p a b -> p (a b)"), cf_d, writes=[CONST])
            S.dma("sp", identb[:], idb_d, writes=[CONST])
            S.dma("sp", pfm[:], pfm_d, writes=[CONST])
            S.dma("sp", bif_bc[:], bif.broadcast_to([128, 32]), writes=[CONST])
            S.dma("pool", wg[:], w_g.rearrange("(k p) n -> p k n", p=128), writes=[CONST])
            cv = A.at(118, [32], F32, "cv")
            csb = A.at(118.25, [KC, 2], BF16, "csb")
            S.dma("sp", cv.ap, cvec, writes=[cv])
            A.act(csb.ap, cv.ap.rearrange("p (k j) -> p k j", j=2), AF.Silu, [cv], [csb])
            for c in range(16):
                for j in range(3):
                    A.ts(dq[:, c, j, :], ident, pfm[:, j * 16 + c:j * 16 + c + 1], ALU.mult, [CONST], [DQ])
            for cb in range(24):
                A.ws_plan(w_mod, 0, 16, cb * 512, 512)
            for blk in range(2):
                A.ws_plan(w_in, 0, 16, 1024 + blk * 512, 512)
            for blk in range(4):
                A.ws_plan(w_in, 0, 16, 2048 + blk * 512, 512)
            for blk in range(4):
                A.ws_plan(w_in, 0, 16, blk * 512, 512)
            for blk in range(4):
                A.ws_plan(w_in, 0, 16, 2048 + blk * 512, 512)
            for blk in range(4):
                A.ws_plan(w_in, 0, 16, 4096 + blk * 512, 512)
            for blk in range(4):
                A.ws_plan(w_pm, 0, 16, blk * 512, 512)
                A.ws_plan(w_in, 0, 16, 10272 + blk * 512, 512)
            for blk in range(8):
                A.ws_plan(w_in, 0, 16, 6176 + (blk // 2) * 512 + (blk % 2) * 2048, 512)
            for blk in range(4):
                A.ws_plan(w_pc, 0, 16, blk * 512, 512)
                A.ws_plan(w_in, 0, 16, 12320 + blk * 512, 512)
            for blk in range(4):
                A.ws_plan(w_o, 0, 16, blk * 512, 512)
            for hh in range(2):
                for r in range(6):
                    ncol = 512 if r < 5 else 256
                    c0 = hh * 2816 + r * 512
                    A.ws_plan(w_f1, 0, 16, c0, ncol)
                    A.ws_plan(w_f1, 0, 16, D_FF + c0, ncol)
                for blk in range(4):
                    for pc in range(2):
                        A.ws_plan(w_f2, hh * 2816 + pc * 1408, 11, blk * 512, 512)

            MODT, MSC2 = Region("modT"), Region("msc2")
            RdMod = [Region(f"mod{i}") for i in range(24)]
            stg = [A.at(119 + 2 * i, [512], F32, f"stg{i}") for i in range(2)]
            bmb = [A.at(123 + 2 * i, [512], F32, f"bmb{i}") for i in range(2)]

            def mod_block(cb):
                wsl, wr = A.ws_get()
                pb = A.bank(6)
                A.mm([(pb[0:2, :], csb.ap[:, kc, :], wsl[:, kc, :], kc == 0, kc == 15) for kc in range(16)],
                     [csb, wr], [PR[6]])
                yield
                s_, bm_ = stg[cb % 2], bmb[cb % 2]
                S.dma("sp", bm_.ap[0:2, :], bmod[0:1, cb * 512:(cb + 1) * 512].broadcast_to([2, 512]), writes=[bm_])
                A.tt(s_.ap[0:2, :], pb[0:2, :], bm_.ap[0:2, :], ALU.add, [PR[6], bm_], [s_])
                yield
                S.dma("sp", mod_d[:, cb * 512:(cb + 1) * 512], s_.ap[0:2, :], reads=[s_], writes=[Rd["mod"], RdMod[cb]])
                pt = A.bank(7)[:, 0:8].rearrange("p (a b) -> p a b", b=2)
                A.tr([(pt[:, j, :], s_.ap[0:2, j * 128:(j + 1) * 128], ident[0:2, 0:2]) for j in range(4)], [s_, CONST], [PR[7]])
                yield
                A.cp(modT[:, cb * 4:(cb + 1) * 4, :], pt, [PR[7]], [MODT])
                yield

            def modgen():
                for cb in range(8, 24):
                    yield from mod_block(cb)
                    if cb == 19:
                        A.cp(msc[:, 4, :], modT[:, 48:64, 0], [MODT], [MSC2])
                        A.ts(msc[:, 5, :], modT[:, 64:80, 0], 1.0, ALU.add, [MODT], [MSC2])
                    for _ in range(5):
                        yield

            gatb = A.at(72, [12, NTILE, 16], F32, "gat")
            gat, GAT = gatb.ap, gatb.rg
            S.op("dve", lambda e: e.memset(gat.rearrange("p a b c -> p (a b c)"), 0.0), [], [GAT])
            NB_A = 6
            xt = [A.at(0 + 8 * i, [D], F32, f"xt{i}") for i in range(NB_A)]
            yn = xt
            uTt = [A.at(48 + 4 * i, [KC, 128], BF16, f"uTt{i}") for i in range(NB_A)]
            gsmv = [gsm[:, i, :] for i in range(NB_A)]
            GATt = [Region(f"gat{t}") for t in range(NTILE)]
            for r_ in GATt:
                r_.w = GAT.w

            GALL = A.at(86, [NTILE, 32], F32, "GALL")
            GSA = A.at(88.5, [NTILE, 32], F32, "GSA")
            ELA = A.at(91, [NTILE, 16], F32, "ELA")
            LLA = A.at(92.25, [NTILE, 16], F32, "LLA")
            AMX = A.at(93.5, [4], F32, "AMX")
            DGS = A.at(94, [3, 96], F32, "DGS")
            GALLt = [Region(f"gall{t}") for t in range(NTILE)]

            def tileA1(t):
                x_b, y_b = xt[t % NB_A], yn[t % NB_A]
                S.dma("sp", x_b.ap, xin[t * 128:(t + 1) * 128, :], writes=[x_b])
                yield
                rs, nmr, R = yield from A.ln_stats_g(x_b.ap, x_b.rg)
                A.act(y_b.ap, x_b.ap, AF.Identity, [x_b, R], [y_b], scale=rs, bias=nmr)
                yield

            def tileA(t, skip1=False):
                if not skip1:
                    yield from tileA1(t)
                yield from tileA2(t)

            def tileA2(t):
                x_b, y_b, u_b = xt[t % NB_A], yn[t % NB_A], uTt[t % NB_A]
                isctx = t < 2
                shv = msc[:, 2 if isctx else 0, :]
                scv = msc[:, 3 if isctx else 1, :]
                for g in range(4):
                    bi = (t * 4 + g) % 4
                    pb = A.bank(bi)
                    A.tr([(pb[:, j * 128:(j + 1) * 128], y_b.ap[:, (g * 4 + j) * 128:(g * 4 + j + 1) * 128], ident) for j in range(4)],
                         [y_b, CONST], [PR[bi]])
                    yield
                    for j in range(4):
                        kc = g * 4 + j
                        if (j + g) % 2 == 0:
                            A.ts(u_b.ap[:, kc, :], pb[:, j * 128:(j + 1) * 128], scv[:, kc:kc + 1], ALU.mult,
                                 [PR[bi], MSC], [u_b], s2=shv[:, kc:kc + 1], op1=ALU.add)
                        else:
                            A.act(u_b.ap[:, kc, :], pb[:, j * 128:(j + 1) * 128], AF.Identity, [PR[bi], MSC], [u_b],
                                  scale=scv[:, kc:kc + 1], bias=shv[:, kc:kc + 1])
                        yield
                S.dma("sp", uT_d[:, :, t * 128:(t + 1) * 128], u_b.ap, reads=[u_b], writes=[Rd["uT"]])
                if t == 17:
                    A.cp(halo[:, :, 0:1], u_b.ap[:, :, 127:128], [u_b], [HALO])
                if t == 2:
                    A.cp(halo[:, :, 1:2], u_b.ap[:, :, 0:1], [u_b], [HALO])
                pgb = 4 + t % 2
                pg = A.bank(pgb)
                A.mm([(pg[:, 0:32], u_b.ap[:, kc, :], wg[:, kc, :], kc == 0, kc == 15) for kc in range(16)],
                     [u_b, CONST], [PR[pgb]])
                yield
                A.cp(GALL.ap[:, t, :], pg[:, 0:32], [PR[pgb]], [GALLt[t]], eng="act" if t % 2 else "dve")
                yield

            def modgen0():
                for cb in range(8):
                    yield from mod_block(cb)

            NEARLY = NB_A - 1
            A.interleave([modgen0()] + [tileA1(t) for t in range(NEARLY)], NEARLY + 1, 0)
            A.cp(msc[:, 0, :], modT[:, 0:16, 0], [MODT], [MSC])
            A.ts(msc[:, 1, :], modT[:, 16:32, 0], 1.0, ALU.add, [MODT], [MSC])
            A.cp(msc[:, 2, :], modT[:, 0:16, 1], [MODT], [MSC])
            A.ts(msc[:, 3, :], modT[:, 16:32, 1], 1.0, ALU.add, [MODT], [MSC])

            A.interleave([modgen()] + [tileA(t, skip1=(t < NEARLY)) for t in range(NTILE)], 6, 5)
            NG = NTILE * 16
            A.tt(GSA.ap, GALL.ap, bif_bc[:].unsqueeze(1).broadcast_to([128, NTILE, 32]), ALU.add, GALLt + [CONST], [GSA])
            GS5 = GSA.ap.rearrange("p t (d j h) -> p t d j h", d=2, j=2)
            A.act(ELA.ap.rearrange("p t (d h) -> p t d h", d=2), GS5[:, :, :, 1, :], AF.Exp, [GSA], [ELA], scale=-1.0)
            A.act(LLA.ap, ELA.ap, AF.Ln, [ELA], [LLA], bias=1.0)
            pB = A.bank(0)
            pT = A.bank(1)
            A.mm([(pB[:, 0:144], trif, LLA.ap[:, :, 0:8], True, True),
                  (pB[:, 144:288], trib, LLA.ap[:, :, 8:16], True, True)], [LLA, CONST], [PR[0]])
            A.mm([(pT[:, 0:NG], ones, LLA.ap, True, True)], [LLA, CONST], [PR[1]])
            for d in range(2):
                pBd = pB[:, d * 144:(d + 1) * 144].rearrange("p (t h) -> p t h", h=8)
                A.tt(gat[:, 0, :, d * 8:(d + 1) * 8], GS5[:, :, d, 0, :], pBd, ALU.add, [GSA, PR[0]], [GAT])
                A.cp(gat[:, 1, :, d * 8:(d + 1) * 8], pBd, [PR[0]], [GAT], eng="act")
            A.cp(gat[:, 2, :, :], pT[:, 0:NG].rearrange("p (t h) -> p t h", h=16), [PR[1]], [GAT], eng="act")
            pX = A.bank(2)
            gA = gat[:, 0, :, :].rearrange("p t h -> p (t h)")
            A.tr([(pX[0:96, k * 128:(k + 1) * 128], gA[:, k * 96:(k + 1) * 96], ident) for k in range(3)], [GAT, CONST], [PR[2]])
            S.op("dve", lambda e: e.reduce_max(out=AMX.ap[0:96, 0:3], in_=pX[0:96, 0:384].rearrange("p (k t) -> p k t", t=128), axis=AX.X),
                 [PR[2]], [AMX])
            for k in range(3):
                A.ts(DGS.ap[0:96, k, :], ident[0:96, 0:96], AMX.ap[0:96, k:k + 1], ALU.mult, [AMX, CONST], [DGS])
            pC = A.bank(3)
            A.mm([(pC[:, k * 96:(k + 1) * 96], ones[0:96, :], DGS.ap[0:96, k, :], True, True) for k in range(3)], [DGS, CONST], [PR[3]])
            A.cp(gat[:, 3, :, :], pC[:, 0:NG].rearrange("p (t h) -> p t h", h=16), [PR[3]], [GAT], eng="act")

            if self.stop == "A3":
                S.dma("sp", dbg_d[:, 0:3456], gat.rearrange("p a b c -> p (a b c)"), reads=[GAT], writes=[Rd["dbg"]])
                return self.finish(nc, S)
            g_A, g_B, g_TOT, g_AMAX, g_SUF, g_TMP, g_MT, g_MNX, g_OFFK, g_LAMN, g_OFFP = range(11)
            seqs = {0: ([0, 1], list(range(10, 18))), 1: ([1, 0] + list(range(9, 1, -1)), list(range(17, 9, -1)))}
            for d in (0, 1):
                cd = slice(d * 8, d * 8 + 8)
                pre, own = seqs[d]
                prev = None
                for c in reversed(pre):
                    if prev is None:
                        A.ts(gat[:, g_SUF, c, cd], gat[:, g_TOT, c, cd], -1.0, ALU.mult, [GAT], [GAT])
                    else:
                        A.tt(gat[:, g_SUF, c, cd], gat[:, g_SUF, prev, cd], gat[:, g_TOT, c, cd], ALU.subtract, [GAT], [GAT])
                    prev = c
                mcur = mp[:, cd]
                for i, c in enumerate(pre):
                    A.tt(gat[:, g_TMP, c, cd], gat[:, g_AMAX, c, cd], gat[:, g_SUF, c, cd], ALU.add, [GAT], [GAT])
                    A.tt(mcur, gat[:, g_SUF, pre[0], cd] if i == 0 else mcur, gat[:, g_TMP, c, cd], ALU.max, [GAT], [GAT])
                mprev = mcur
                for c in own:
                    A.tt(gat[:, g_MT, c, cd], mprev, gat[:, g_AMAX, c, cd], ALU.max, [GAT], [GAT])
                    A.tt(gat[:, g_MNX, c, cd], gat[:, g_MT, c, cd], gat[:, g_TOT, c, cd], ALU.subtract, [GAT], [GAT])
                    mprev = gat[:, g_MNX, c, cd]
                for i, c in enumerate(own[:-1]):
                    nxt = own[i + 1]
                    A.stt(gat[:, g_OFFK, c, cd], gat[:, g_TOT, c, cd], -1.0, gat[:, g_MT, nxt, cd], ALU.mult, ALU.subtract, [GAT], [GAT])
                    A.tt(gat[:, g_LAMN, c, cd], gat[:, g_MNX, c, cd], gat[:, g_MT, nxt, cd], ALU.subtract, [GAT], [GAT])
                for c in pre:
                    A.tt(gat[:, g_OFFP, c, cd], gat[:, g_SUF, c, cd], gat[:, g_MT, own[0], cd], ALU.subtract, [GAT], [GAT])
            if self.stop == "A4":
                S.dma("sp", dbg_d[:, 0:3456], gat.rearrange("p a b c -> p (a b c)"), reads=[GAT], writes=[Rd["dbg"]])
                return self.finish(nc, S)
            fl = lambda ap: ap.rearrange("p a b -> p (a b)")
            own_s = slice(10, 18)
            A.tt(fl(gex[:, 0, own_s, :]), fl(gat[:, g_A, own_s, :]), fl(gat[:, g_MT, own_s, :]), ALU.subtract, [GAT], [GEX])
            A.tt(fl(gex[:, 1, own_s, :]), fl(gat[:, g_A, own_s, :]), fl(gat[:, g_OFFK, own_s, :]), ALU.add, [GAT], [GEX])
            A.tt(fl(gex[:, 2, own_s, :]), fl(gat[:, g_B, own_s, :]), fl(gat[:, g_MT, own_s, :]), ALU.subtract, [GAT], [GEX])
            A.cp(fl(gex[:, 3, own_s, :]), fl(gat[:, g_LAMN, own_s, :]), [GAT], [GEX])
            A.tt(fl(gex[:, 4, 0:10, :]), fl(gat[:, g_A, 0:10, :]), fl(gat[:, g_OFFP, 0:10, :]), ALU.add, [GAT], [GEX])
            for i4 in range(4):
                A.act(fl(gex[:, i4, own_s, :]), fl(gex[:, i4, own_s, :]), AF.Exp, [GEX], [GEX])
            A.act(fl(gex[:, 4, 0:10, :]), fl(gex[:, 4, 0:10, :]), AF.Exp, [GEX], [GEX])
            GS_, GK_, FL_, LAM_, GP_ = (gex[:, i, :, :] for i in range(5))
            if dbg_d is not None:
                S.dma("sp", dbg_d[:, 0:3456], gat.rearrange("p a b c -> p (a b c)"), reads=[GAT], writes=[Rd["dbg"]])
                S.dma("sp", dbg_d[:, 3456:4896], gex[:].rearrange("p a b c -> p (a b c)"), reads=[GEX], writes=[Rd["dbg"]])
            S.fence()
            if self.stop == "A":
                return self.finish(nc, S)

            St = A.at(103.25, [2, 8, 257], F32, "S")
            Sb_ = A.at(119.3125, [2, 8, 257], BF16, "Sb")
            SR = [[Region(f"S{d}{h}") for h in range(8)] for d in range(2)]
            SbR = [[Region(f"Sb{d}{h}") for h in range(8)] for d in range(2)]
            NB = 1281
            uToc = A.at(0, [KC, NB], BF16, "uToc")
            kToc = A.at(40.25, [8, 1280], BF16, "kToc")
            v1oc = A.at(60.25, [10, 8, 257], BF16, "v1oc")
            kpre = A.at(100.5, [1284], BF16, "kpre")
            ktp = Buf(A.at(0, [12, 8, 128], BF16).ap, uToc.rg)
            S.dma("sp", uToc.ap[:, :, 0:256], uT_d[:, :, 0:256], reads=[Rd["uT"]], writes=[uToc])
            S.dma("sp", uToc.ap[:, :, 257:1281], uT_d[:, :, 256:1280], reads=[Rd["uT"]], writes=[uToc])
            A.cp(uToc.ap[:, :, 256:257], halo[:, :, 0:1], [HALO], [uToc])
            S.op("dve", lambda e: e.memset(kpre.ap, 0.0), [], [kpre])
            ntl = [(0, 512), (512, 1024), (1024, NB)]
            if self.stop == "B1":
                return self.finish(nc, S)
            kpre2 = [kpre, A.at(61, [1284], BF16, "kpre2")]
            S.op("dve", lambda e: e.memset(kpre2[1].ap, 0.0), [], [kpre2[1]])
            wcurB = [None, None]

            def projB(h):
                if h % 4 == 0:
                    wcurB[0], wcurB[1] = A.ws_get()
                wsl, wr = wcurB
                sub = h % 4
                pb0 = 0 if h % 2 == 0 else 3
                pp = A.bank(pb0, 3)
                kp = kpre2[h % 2]
                mms = []
                for kc in range(16):
                    for (n0, n1) in ntl:
                        mms.append((pp[:, n0:n1], wsl[:, kc, sub * 128:(sub + 1) * 128], uToc.ap[:, kc, n0:n1], kc == 0, kc == 15))
                A.mm(mms, [uToc, wr], PR[pb0:pb0 + 3])
                A.cp(kp.ap[:, 1:257], pp[:, 0:256], PR[pb0:pb0 + 1], [kp], eng="act")
                A.cp(kp.ap[:, 258:1283], pp[:, 256:1281], PR[pb0:pb0 + 3], [kp])

            def convB(h):
                c = 8 + h
                kp = kpre2[h % 2]
                p6, p7 = A.bank(6), A.bank(7)
                mms = []
                for j in range(3):
                    mms.append((p6[:, 0:256], dq[:, c, j, :], kp.ap[:, j:j + 256], j == 0, j == 2))
                    mms.append((p7, dq[:, c, j, :], kp.ap[:, 258 + j:258 + j + 512], j == 0, j == 2))
                A.mm(mms, [kp, DQ], PR[6:8])
                A.act(kToc.ap[:, h, 0:256], p6[:, 0:256], AF.Silu, [PR[6], CONST], [kToc], bias=pfm[:, 48 + c:49 + c])
                A.act(kToc.ap[:, h, 256:768], p7, AF.Silu, [PR[7], CONST], [kToc], bias=pfm[:, 48 + c:49 + c])
                A.mm([(p6, dq[:, c, j, :], kp.ap[:, 770 + j:770 + j + 512], j == 0, j == 2) for j in range(3)], [kp, DQ], [PR[6]])
                A.act(kToc.ap[:, h, 768:1280], p6, AF.Silu, [PR[6], CONST], [kToc], bias=pfm[:, 48 + c:49 + c])

            projB(0)
            for h in range(8):
                if h + 1 < 8:
                    projB(h + 1)
                convB(h)
            S.op("dve", lambda e: e.memset(v1oc.ap.rearrange("p a b c -> p (a b c)"), 1.0), [], [v1oc, kpre2[1]])
            if self.stop == "B3":
                return self.finish(nc, S)
            tcol = lambda t: t * 128 if t < 2 else 257 + (t - 2) * 128
            for blk in range(4):
                wsl, wr = A.ws_get()
                for t in range(10):
                    b = 6 + t % 2
                    pb = A.bank(b)
                    A.mm([(pb, uToc.ap[:, kc, tcol(t):tcol(t) + 128], wsl[:, kc, :], kc == 0, kc == 15) for kc in range(16)],
                         [uToc, wr], [PR[b]])
                    A.cp(v1oc.ap[:, t, 2 * blk:2 * blk + 2, 0:256], pb.rearrange("p (h e) -> p h e", e=256), [PR[b]], [v1oc],
                         eng="act" if t % 2 else "dve")
            if self.stop == "B4":
                return self.finish(nc, S)
            kidx = {}
            n = 0
            for t in range(10):
                for d in ((0, 1) if t < 2 else (1,)):
                    kidx[(t, d)] = n
                    n += 1
            for t in range(10):
                b = t % 2
                pbb = A.bank(b).bitcast(BF16)
                kc0 = t * 128
                A.tr([(pbb[:, h * 128:(h + 1) * 128], kToc.ap[:, h, kc0:kc0 + 128], identb[:]) for h in range(8)],
                     [kToc, CONST], [PR[b]])
                for d in ((0, 1) if t < 2 else (1,)):
                    A.tt(ktp.ap[:, kidx[(t, d)], :, :], pbb.rearrange("p (h k) -> p h k", k=128),
                         GP_[:, t, d * 8:d * 8 + 8].unsqueeze(2).broadcast_to([128, 8, 128]), ALU.mult, [PR[b], GEX], [ktp])
            if self.stop == "B5":
                return self.finish(nc, S)
            for d in (0, 1):
                pre = seqs[d][0]
                for h in range(8):
                    b = 2 + h % 2
                    pb = A.bank(b)
                    A.mm([(pb[:, 0:257], ktp.ap[:, kidx[(t, d)], h, :], v1oc.ap[:, t, h, :], i == 0, i == len(pre) - 1)
                          for i, t in enumerate(pre)], [ktp, v1oc], [PR[b]])
                    if self.stop == "B6":
                        return self.finish(nc, S)
                    A.cp(St.ap[:, d, h, :], pb[:, 0:257], [PR[b]], [SR[d][h]], eng="act")
                    if self.stop == "B7":
                        return self.finish(nc, S)
                    A.cp(Sb_.ap[:, d, h, :], pb[:, 0:257], [PR[b]], [SbR[d][h]])
                    if self.stop == "B8":
                        return self.finish(nc, S)
            if dbg_d is not None:
                S.dma("sp", dbg_d[:, 4896:4896 + 4112], St.ap.rearrange("p a b c -> p (a b c)"),
                      reads=[r for rr in SR for r in rr], writes=[Rd["dbg"]])
            S.fence()
            if self.stop == "B":
                return self.finish(nc, S)

            qT = A.at(0, [8, NOWN], BF16, "qT")
            kT = A.at(16, [8, NOWN], BF16, "kT")
            v1 = A.at(32, [8, 8, 257], BF16, "v1")
            uTo = A.at(64.5, [KC, 1025], BF16, "uTo")
            pre_ = A.at(96.75, [1026], BF16, "pre")
            tmpq = A.at(99.0, [512], F32, "tmpq")
            S.dma("sp", uTo.ap[:, :, 0:1024], uT_d[:, :, 1280:2304], reads=[Rd["uT"]], writes=[uTo])
            A.cp(uTo.ap[:, :, 1024:1025], halo[:, :, 1:2], [HALO], [uTo])
            S.op("dve", lambda e: e.memset(pre_.ap, 0.0), [], [pre_])
            S.op("dve", lambda e: e.memset(v1.ap.rearrange("p a b c -> p (a b c)"), 1.0), [], [v1])
            ntl = [(0, 512), (512, 1024), (1024, 1025)]
            pre2 = [pre_, A.at(101, [1026], BF16, "pre2")]
            S.op("dve", lambda e: e.memset(pre2[1].ap, 0.0), [], [pre2[1]])
            wcur = [None, None]

            def projC(c):
                if c % 4 == 0:
                    wcur[0], wcur[1] = A.ws_get()
                wsl, wr = wcur
                sub = c % 4
                pb0 = 0 if c % 2 == 0 else 5
                pp = A.bank(pb0, 3)
                mms = []
                for kc in range(16):
                    for (n0, n1) in ntl:
                        mms.append((pp[:, n0:n1], wsl[:, kc, sub * 128:(sub + 1) * 128], uTo.ap[:, kc, n0:n1], kc == 0, kc == 15))
                A.mm(mms, [uTo, wr], PR[pb0:pb0 + 3])
                A.cp(pre2[c % 2].ap[:, 1:1026], pp[:, 0:1025], PR[pb0:pb0 + 3], [pre2[c % 2]])

            def convC(c):
                p_ = pre2[c % 2]
                pc_ = A.bank(3, 2)
                mms = []
                for j in range(3):
                    for hf in range(2):
                        mms.append((pc_[:, hf * 512:(hf + 1) * 512], dq[:, c, j, :], p_.ap[:, hf * 512 + j:hf * 512 + j + 512], j == 0, j == 2))
                A.mm(mms, [p_, DQ], PR[3:5])
                if c < 8:
                    for hf in range(2):
                        A.act(tmpq.ap, pc_[:, hf * 512:(hf + 1) * 512], AF.Silu, [PR[3 + hf], CONST], [tmpq], bias=pfm[:, 48 + c:49 + c])
                        A.ts(qT.ap[:, c, hf * 512:(hf + 1) * 512], tmpq.ap, QSCALE, ALU.mult, [tmpq], [qT])
                else:
                    A.act(kT.ap[:, c - 8, :], pc_, AF.Silu, PR[3:5] + [CONST], [kT], bias=pfm[:, 48 + c:49 + c])

            projC(0)
            for c in range(16):
                if c + 1 < 16:
                    projC(c + 1)
                convC(c)
            for blk in range(4):
                wsl, wr = A.ws_get()
                for t in range(8):
                    b = 6 + t % 2
                    pb = A.bank(b)
                    A.mm([(pb, uTo.ap[:, kc, t * 128:(t + 1) * 128], wsl[:, kc, :], kc == 0, kc == 15) for kc in range(16)],
                         [uTo, wr], [PR[b]])
                    A.cp(v1.ap[:, t, 2 * blk:2 * blk + 2, 0:256], pb.rearrange("p (h e) -> p h e", e=256), [PR[b]], [v1],
                         eng="act" if t % 2 else "dve")
            S.fence()

            ktl = [[A.at(65 + 4 * d + 2 * i, [8, 128], BF16, f"ktl{d}{i}") for i in range(2)] for d in range(2)]
            HT = [A.at(73 + 8 * d, [8, 256], F32, f"HT{d}") for d in range(2)]
            sTb = [[A.at(89 + 0.25 * (2 * d + i), [128], BF16, f"sTb{d}{i}") for i in range(2)] for d in range(2)]
            dn = [[A.at(90 + 0.0625 * (2 * d + i), [2], F32, f"dn{d}{i}") for i in range(2)] for d in range(2)]

            HTR = [[Region(f"HT{d}{g}") for g in range(2)] for d in range(2)]

            def scan_stream(d, hg):
                k_ = d * 2 + hg
                bX, bY = 2 * k_, 2 * k_ + 1
                X, Y = A.bank(bX), A.bank(bY)
                Ybf = Y.bitcast(BF16)
                heads = range(hg * 4, hg * 4 + 4)
                for i in range(8):
                    ti = i if d == 0 else 7 - i
                    tc_ = slice(ti * 128, (ti + 1) * 128)
                    gt = 10 + ti
                    mask = trif if d == 0 else trib
                    kt_ = ktl[d][i % 2]
                    ktR = ktlR[d][i % 2][hg]
                    if i < 7:
                        A.tr([(Ybf[:, 512 + j * 128:512 + (j + 1) * 128], kT.ap[:, h, tc_], identb[:]) for j, h in enumerate(heads)],
                             [kT, CONST], [PR[bY]])
                        yield
                        A.tt(kt_.ap[:, hg * 4:hg * 4 + 4, :], Ybf[:, 512:1024].rearrange("p (h k) -> p h k", k=128),
                             GK_[:, gt, d * 8 + hg * 4:d * 8 + hg * 4 + 4].unsqueeze(2).broadcast_to([128, 4, 128]), ALU.mult,
                             [PR[bY], GEX], [ktR])
                        yield
                    for h in heads:
                        col = d * 8 + h
                        psc = X[:, 0:128]
                        A.mm([(psc, kT.ap[:, h, tc_], qT.ap[:, h, tc_], True, True)], [kT, qT], [PR[bX]])
                        yield
                        sb_ = sTb[d][hg]
                        dn_ = dn[d][hg]
                        A.stt(sb_.ap, psc, GS_[:, gt, col:col + 1], mask, ALU.mult, ALU.mult, [PR[bX], GEX, CONST], [sb_])
                        yield
                        if i < 7:
                            A.mm([(Y[:, 0:256], kt_.ap[:, h, :], v1.ap[:, ti, h, 0:256], True, True)], [ktR, v1], [PR[bY]])
                            yield
                        pnd = X[:, 128:385]
                        mms = [(pnd, sb_.ap, v1.ap[:, ti, h, :], True, False),
                               (pnd, qT.ap[:, h, tc_], Sb_.ap[:, d, h, :], False, True)]
                        if i < 7:
                            mms.append((X[:, 385:386], kt_.ap[:, h, :], v1.ap[:, ti, h, 256:257], True, True))
                        A.mm(mms, [sb_, v1, qT, SbR[d][h], ktR], [PR[bX]])
                        yield
                        S.op("dve", lambda e, dn_=dn_, pnd=pnd: e.tensor_reduce(out=dn_.ap[:, 0:1], in_=pnd[:, 256:257], axis=AX.X, op=ALU.max,
                                                                                apply_absolute_value=True), [PR[bX]], [dn_])
                        yield
                        A.ts(dn_.ap[:, 0:1], dn_.ap[:, 0:1], FL_[:, gt, col:col + 1], ALU.max, [dn_, GEX], [dn_])
                        yield
                        S.op("dve", lambda e, dn_=dn_: e.reciprocal(out=dn_.ap[:, 1:2], in_=dn_.ap[:, 0:1]), [dn_], [dn_])
                        yield
                        A.act(HT[d].ap[:, h, :], pnd[:, 0:256], AF.Copy, [PR[bX], dn_], [HTR[d][hg]], scale=dn_.ap[:, 1:2])
                        yield
                        if i < 7:
                            A.stt(St.ap[:, d, h, 256:257], St.ap[:, d, h, 256:257], LAM_[:, gt, col:col + 1], X[:, 385:386], ALU.mult, ALU.add,
                                  [SR[d][h], PR[bX], GEX], [SR[d][h]])
                            yield
                            A.stt(St.ap[:, d, h, 0:256], St.ap[:, d, h, 0:256], LAM_[:, gt, col:col + 1], Y[:, 0:256], ALU.mult, ALU.add,
                                  [SR[d][h], PR[bY], GEX], [SR[d][h]])
                            yield
                            A.cp(Sb_.ap[:, d, h, :], St.ap[:, d, h, :], [SR[d][h]], [SbR[d][h]], eng="act")
                            yield
                    S.dma("sp", (HF_d if d == 0 else HB_d)[ti * 128:(ti + 1) * 128, hg * 1024:(hg + 1) * 1024],
                          HT[d].ap[:, hg * 4:hg * 4 + 4, :].rearrange("p h e -> p (h e)"),
                          reads=[HTR[d][hg]], writes=[Rd["HF" if d == 0 else "HB"]])
                    yield

            ktlR = [[[Region(f"ktl{d}{i}{g}") for g in range(2)] for i in range(2)] for d in range(2)]
            A.interleave([scan_stream(0, 0), scan_stream(1, 0), scan_stream(0, 1), scan_stream(1, 1)], 4, 0)
            S.fence()
            NBC = 4
            hfb = [A.at(0 + 8 * i, [D], F32, f"hfb{i}") for i in range(NBC)]
            hbb = [A.at(32 + 8 * i, [D], F32, f"hbb{i}") for i in range(NBC)]
            hnb = [A.at(64 + 8 * i, [D], F32, f"hnb{i}") for i in range(NBC)]
            h8 = [A.at(96 + 0.5 * i, [96], F32, f"h8{i}") for i in range(NBC)]

            def tileC(ti):
                a_, b_, n_, s8 = hfb[ti % NBC], hbb[ti % NBC], hnb[ti % NBC], h8[ti % NBC]
                rows = slice(ti * 128, (ti + 1) * 128)
                S.dma("sp", a_.ap, HF_d[rows, :], reads=[Rd["HF"]], writes=[a_])
                S.dma("sp", b_.ap, HB_d[rows, :], reads=[Rd["HB"]], writes=[b_])
                yield
                A.tt(a_.ap, a_.ap, b_.ap, ALU.add, [a_, b_], [a_])
                yield
                st8 = s8.ap[:, 0:48].rearrange("p (h s) -> p h s", s=6)
                mv8 = s8.ap[:, 48:64].rearrange("p (h s) -> p h s", s=2)
                rs8 = s8.ap[:, 64:72]
                nm8 = s8.ap[:, 72:80]
                for h in range(8):
                    S.op("dve", lambda e, h=h, a_=a_, st8=st8: e.bn_stats(out=st8[:, h, :], in_=a_.ap[:, h * 256:(h + 1) * 256]), [a_], [s8])
                    yield
                for h in range(8):
                    S.op("dve", lambda e, h=h, st8=st8, mv8=mv8: e.bn_aggr(out=mv8[:, h, :], in_=st8[:, h, :]), [s8], [s8])
                    yield
                A.act(rs8, mv8[:, :, 1], AF.Ln, [s8], [s8], bias=EPS)
                yield
                A.act(rs8, rs8, AF.Exp, [s8], [s8], scale=-0.5)
                yield
                A.stt(nm8, mv8[:, :, 0], -1.0, rs8, ALU.mult, ALU.mult, [s8], [s8])
                yield
                for h in range(8):
                    if h % 2 == 0:
                        A.act(n_.ap[:, h * 256:(h + 1) * 256], a_.ap[:, h * 256:(h + 1) * 256], AF.Identity, [a_, s8], [n_],
                              scale=rs8[:, h:h + 1], bias=nm8[:, h:h + 1])
                    else:
                        A.ts(n_.ap[:, h * 256:(h + 1) * 256], a_.ap[:, h * 256:(h + 1) * 256], rs8[:, h:h + 1], ALU.mult, [a_, s8], [n_],
                             s2=nm8[:, h:h + 1], op1=ALU.add)
                    yield
                S.dma("sp", HN_d[rows, :], n_.ap, reads=[n_], writes=[Rd["HN"]])
                yield

            A.interleave([tileC(ti) for ti in range(8)], 4, 9)
            S.fence()
            if self.stop == "C":
                return self.finish(nc, S)

            uT = A.at(0, [KC, NOWN], BF16, "uT")
            hmoT = A.at(32, [KC, NOWN], BF16, "hmoT")
            S.dma("sp", uT.ap, uT_d[:, :, 1280:2304], reads=[Rd["uT"]], writes=[uT])
            so = [A.at(64 + 2 * i, [512], F32, f"so{i}") for i in range(4)]
            hnk = [A.at(72 + 2 * i, [512], F32, f"hnk{i}") for i in range(4)]
            wD = [None, None]

            def d_mm(n):
                blk, t = divmod(n, 8)
                if t == 0:
                    wD[0], wD[1] = A.ws_get()
                wsl, wr = wD
                cols = slice(blk * 512, (blk + 1) * 512)
                so_, hn_ = so[n % 4], hnk[n % 4]
                b = n % 4
                pb = A.bank(b)
                S.dma("sp", hn_.ap, HN_d[t * 128:(t + 1) * 128, cols], reads=[Rd["HN"]], writes=[hn_])
                A.mm([(pb, uT.ap[:, kc, t * 128:(t + 1) * 128], wsl[:, kc, :], kc == 0, kc == 15) for kc in range(16)],
                     [uT, wr], [PR[b]])
                A.act(so_.ap, pb, AF.Sigmoid, [PR[b]], [so_])
                A.tt(so_.ap, so_.ap, hn_.ap, ALU.mult, [so_, hn_], [so_])

            def d_rest(n):
                blk, t = divmod(n, 8)
                so_ = so[n % 4]
                b2 = 4 + n % 4
                pt = A.bank(b2)
                A.tr([(pt[:, j * 128:(j + 1) * 128], so_.ap[:, j * 128:(j + 1) * 128], ident) for j in range(4)], [so_, CONST], [PR[b2]])
                A.tt(hmoT.ap[:, blk * 4:blk * 4 + 4, t * 128:(t + 1) * 128], pt.rearrange("p (a b) -> p a b", b=128),
                     pfm[:, 576 + blk * 4:576 + blk * 4 + 4].unsqueeze(2).broadcast_to([128, 4, 128]), ALU.mult,
                     [PR[b2], CONST], [hmoT])

            d_mm(0)
            d_mm(1)
            for n in range(32):
                if n + 2 < 32:
                    d_mm(n + 2)
                d_rest(n)
            S.fence()

            sg = [A.at(64 + 4 * i, [NOWN], F32, f"sg{i}") for i in range(2)]
            zmb = [A.at(72 + 4 * i, [NOWN], F32, f"zmb{i}") for i in range(2)]
            n = 0
            for blk in range(4):
                w1, r1_ = A.ws_get()
                w2, r2_ = A.ws_get()
                for sub in range(4):
                    c = blk * 4 + sub
                    sg_, zm_ = sg[n % 2], zmb[n % 2]
                    n += 1
                    pbase = 4 * (n % 2)
                    py = A.bank(pbase, 2)
                    pg_ = A.bank(pbase + 2, 2)
                    mms = []
                    for kc in range(16):
                        for hf in range(2):
                            mms.append((py[:, hf * 512:(hf + 1) * 512], w1[:, kc, sub * 128:(sub + 1) * 128],
                                        hmoT.ap[:, kc, hf * 512:(hf + 1) * 512], kc == 0, kc == 15))
                    A.mm(mms, [hmoT, r1_], PR[pbase:pbase + 2])
                    mms = []
                    for kc in range(16):
                        for hf in range(2):
                            mms.append((pg_[:, hf * 512:(hf + 1) * 512], w2[:, kc, sub * 128:(sub + 1) * 128],
                                        uT.ap[:, kc, hf * 512:(hf + 1) * 512], kc == 0, kc == 15))
                    A.mm(mms, [uT, r2_], PR[pbase + 2:pbase + 4])
                    A.act(sg_.ap, pg_, AF.Sigmoid, PR[pbase + 2:pbase + 4], [sg_])
                    A.tt(zm_.ap, sg_.ap, py, ALU.mult, [sg_] + PR[pbase:pbase + 2], [zm_])
                    S.dma("sp", ZM_d[c * 128:(c + 1) * 128, :], zm_.ap, reads=[zm_], writes=[Rd["ZM"]])
            S.fence()

            yc = A.at(32, [KC, NOWN], F32, "yc")
            ycin = Buf(A.at(0, [KC, NOWN], BF16).ap, uT.rg)
            ypad = [A.at(96 + 3 * i, [16, 94], BF16, f"ypad{i}") for i in range(2)]
            sgl = [A.at(102, [512], F32, "sgl0")] * 2
            sq = [A.at(104 + 4 * i, [NOWN], F32, f"sq{i}") for i in range(2)]
            dw = [A.at(112 + 7.75 * i, [31, 128], BF16, f"dw{i}") for i in range(2)]
            ypR = [[Region(f"ypR{i}{hf}") for hf in range(2)] for i in range(2)]
            for i in range(2):
                S.op("dve", lambda e, i=i: e.memset(ypad[i].ap.rearrange("p a b -> p (a b)"), 0.0), [], [ypad[i]] + ypR[i])
            STAT = PR[4:8]
            psum_s = A.bank(4, 2)
            psum_q = A.bank(6, 2)
            n = 0
            for c in range(16):
                wa, ra = A.ws_get() if c % 4 == 0 else (wa, ra)
                wl_, rl = A.ws_get() if c % 4 == 0 else (wl_, rl)
                sub = c % 4
                yp, dw_, sq_ = ypad[c % 2], dw[c % 2], sq[c % 2]
                for j in range(31):
                    A.ts(dw_.ap[:, j, :], ident, pfm[:, 64 + j * 16 + c:65 + j * 16 + c], ALU.mult, [CONST], [dw_])
                for hf in range(2):
                    pa = A.bank(0)
                    pl = A.bank(1)
                    sgl_ = sgl[n % 2]
                    n += 1
                    mms = []
                    for kc in range(16):
                        mms.append((pa, wa[:, kc, sub * 128:(sub + 1) * 128], uT.ap[:, kc, hf * 512:(hf + 1) * 512], kc == 0, kc == 15))
                        mms.append((pl, wl_[:, kc, sub * 128:(sub + 1) * 128], uT.ap[:, kc, hf * 512:(hf + 1) * 512], kc == 0, kc == 15))
                    A.mm(mms, [uT, ra, rl], PR[0:2])
                    A.act(sgl_.ap, pl, AF.Sigmoid, [PR[1]], [sgl_])
                    A.tt(yp.ap[:, hf * 8:(hf + 1) * 8, 15:79], sgl_.ap.rearrange("p (r t) -> p r t", t=64),
                         pa.rearrange("p (r t) -> p r t", t=64), ALU.mult, [sgl_, PR[0]], [ypR[c % 2][hf]])
                pcv = A.bank(2, 2)
                for hf in range(2):
                    A.mm([(pcv[:, hf * 512:(hf + 1) * 512], dw_.ap[:, j, :], yp.ap[:, hf * 8:(hf + 1) * 8, j:j + 64], j == 0, j == 30)
                          for j in range(31)], [ypR[c % 2][hf], dw_], [PR[2 + hf]])
                A.act(yc.ap[:, c, :], pcv, AF.Identity, PR[2:4] + [CONST], [yc], bias=pfm[:, 560 + c:561 + c])
                A.act(sq_.ap, yc.ap[:, c, :], AF.Square, [yc], [sq_])
                mms = []
                for hf in range(2):
                    mms.append((psum_s[:, hf * 512:(hf + 1) * 512], ones, yc.ap[:, c, hf * 512:(hf + 1) * 512], c == 0, c == 15))
                    mms.append((psum_q[:, hf * 512:(hf + 1) * 512], ones, sq_.ap[:, hf * 512:(hf + 1) * 512], c == 0, c == 15))
                A.mm(mms, [yc, sq_, CONST], STAT)
            mean_bc = A.at(96, [NOWN], F32, "mean_bc")
            rstd_bc = A.at(100, [NOWN], F32, "rstd_bc")
            tb = [A.at(104 + 4 * i, [NOWN], F32, f"tb{i}") for i in range(2)]
            S.fence()
            A.ts(mean_bc.ap, psum_s, 1.0 / D, ALU.mult, PR[4:6], [mean_bc])
            A.ts(rstd_bc.ap, psum_q, 1.0 / D, ALU.mult, PR[6:8], [rstd_bc])
            A.tt(tb[0].ap, mean_bc.ap, mean_bc.ap, ALU.mult, [mean_bc], [tb[0]])
            A.tt(rstd_bc.ap, rstd_bc.ap, tb[0].ap, ALU.subtract, [rstd_bc, tb[0]], [rstd_bc])
            A.act(rstd_bc.ap, rstd_bc.ap, AF.Ln, [rstd_bc], [rstd_bc], bias=EPS)
            A.act(rstd_bc.ap, rstd_bc.ap, AF.Exp, [rstd_bc], [rstd_bc], scale=-0.5)
            for c in range(16):
                t_ = tb[c % 2]
                A.tt(t_.ap, yc.ap[:, c, :], mean_bc.ap, ALU.subtract, [yc, mean_bc], [t_])
                A.tt(t_.ap, t_.ap, rstd_bc.ap, ALU.mult, [t_, rstd_bc], [t_])
                A.act(ycin.ap[:, c, :], t_.ap, AF.Silu, [t_, CONST], [ycin], scale=pfm[:, 592 + c:593 + c], bias=pfm[:, 608 + c:609 + c])
            S.fence()

            uT2 = A.at(32, [KC, NOWN], BF16, "uT2")
            zT = A.at(64, [KC, NOWN], BF16, "zT")
            S.dma("sp", uT2.ap, uT_d[:, :, 1280:2304], reads=[Rd["uT"]], writes=[uT2])
            sg = [A.at(96 + 4 * i, [NOWN], F32, f"sgb{i}") for i in range(2)]
            zmb = [A.at(104 + 4 * i, [NOWN], F32, f"zml{i}") for i in range(2)]
            n = 0
            for blk in range(4):
                w1, r1_ = A.ws_get()
                w2, r2_ = A.ws_get()
                for sub in range(4):
                    c = blk * 4 + sub
                    sg_, zm_ = sg[n % 2], zmb[n % 2]
                    n += 1
                    pbase = 4 * (n % 2)
                    py = A.bank(pbase, 2)
                    pg_ = A.bank(pbase + 2, 2)
                    S.dma("sp", zm_.ap, ZM_d[c * 128:(c + 1) * 128, :], reads=[Rd["ZM"]], writes=[zm_])
                    mms = []
                    for kc in range(16):
                        for hf in range(2):
                            mms.append((py[:, hf * 512:(hf + 1) * 512], w1[:, kc, sub * 128:(sub + 1) * 128],
                                        ycin.ap[:, kc, hf * 512:(hf + 1) * 512], kc == 0, kc == 15))
                    A.mm(mms, [ycin, r1_], PR[pbase:pbase + 2])
                    mms = []
                    for kc in range(16):
                        for hf in range(2):
                            mms.append((pg_[:, hf * 512:(hf + 1) * 512], w2[:, kc, sub * 128:(sub + 1) * 128],
                                        uT2.ap[:, kc, hf * 512:(hf + 1) * 512], kc == 0, kc == 15))
                    A.mm(mms, [uT2, r2_], PR[pbase + 2:pbase + 4])
                    A.act(sg_.ap, pg_, AF.Sigmoid, PR[pbase + 2:pbase + 4], [sg_])
                    A.tt(sg_.ap, sg_.ap, py, ALU.mult, [sg_] + PR[pbase:pbase + 2], [sg_])
                    A.tt(zT.ap[:, c, :], sg_.ap, zm_.ap, ALU.add, [sg_, zm_], [zT])
            S.fence()

            g1bc = A.at(0, [D], F32, "g1bc")
            S.dma("sp", g1bc.ap, mod_d[0:1, 2 * D:3 * D].broadcast_to([128, D]), reads=[Rd["mod"]], writes=[g1bc])
            xb = [A.at(8 + 2 * i, [512], F32, f"xb{i}") for i in range(2)]
            t1 = [A.at(12 + 2 * i, [512], F32, f"t1{i}") for i in range(2)]
            st1 = A.at(127, [8, 4, 6], F32, "st1")
            n = 0
            for blk in range(4):
                wsl, wr = A.ws_get()
                cols = slice(blk * 512, (blk + 1) * 512)
                for t in range(8):
                    b = n % 2
                    x_, t_ = xb[n % 2], t1[n % 2]
                    n += 1
                    pb = A.bank(b)
                    S.dma("sp", x_.ap, xin[(10 + t) * 128:(11 + t) * 128, cols], writes=[x_])
                    A.mm([(pb, zT.ap[:, kc, t * 128:(t + 1) * 128], wsl[:, kc, :], kc == 0, kc == 15) for kc in range(16)],
                         [zT, wr], [PR[b]])
                    A.tt(t_.ap, pb, g1bc.ap[:, cols], ALU.mult, [PR[b], g1bc], [t_])
                    A.stt(t_.ap, x_.ap, ALPHA, t_.ap, ALU.mult, ALU.add, [x_, t_], [t_])
                    S.op("dve", lambda e, t=t, blk=blk, t_=t_: e.bn_stats(out=st1.ap[:, t, blk, :], in_=t_.ap), [t_], [st1])
                    S.dma("sp", R1_d[t * 128:(t + 1) * 128, cols], t_.ap, reads=[t_], writes=[Rd["R1"]])
            S.fence()

            xmT = A.at(96, [KC, NOWN], BF16, "xmT")
            lg = A.at(0, [D], F32, "ln1g")
            lb = A.at(8, [D], F32, "ln1b")
            S.dma("sp", lg.ap, lnrow[0:1, :].broadcast_to([128, D]), writes=[lg])
            S.dma("sp", lb.ap, lnrow[1:2, :].broadcast_to([128, D]), writes=[lb])
            NBH = 4
            rt = [A.at(16 + 8 * i, [D], F32, f"rt{i}") for i in range(NBH)]
            x1b = [A.at(48 + 8 * i, [D], F32, f"x1b{i}") for i in range(NBH)]
            s1 = [A.at(88 + 0.0625 * i, [4], F32, f"s1{i}") for i in range(NBH)]

            def tileH(t):
                r_, x_, s_ = rt[t % NBH], x1b[t % NBH], s1[t % NBH]
                y_ = r_
                rows = slice(t * 128, (t + 1) * 128)
                S.dma("sp", r_.ap, R1_d[rows, :], reads=[Rd["R1"]], writes=[r_])
                S.op("dve", lambda e, t=t, s_=s_: e.bn_aggr(out=s_.ap[:, 0:2], in_=st1.ap[:, t, :, :].rearrange("p a b -> p (a b)")), [st1], [s_])
                yield
                yield from A.rstd_chain_g(s_.ap[:, 1:2], s_.ap[:, 0:1], s_.ap[:, 2:3], s_.ap[:, 3:4], s_.rg)
                A.act(x_.ap, r_.ap, AF.Identity, [r_, s_], [x_], scale=s_.ap[:, 2:3], bias=s_.ap[:, 3:4])
                yield
                A.tt(x_.ap, x_.ap, lg.ap, ALU.mult, [x_, lg], [x_])
                yield
                A.tt(x_.ap, x_.ap, lb.ap, ALU.add, [x_, lb], [x_])
                yield
                S.dma("sp", X1_d[rows, :], x_.ap, reads=[x_], writes=[Rd["X1"]])
                rs, nmr, R = yield from A.ln_stats_g(x_.ap, x_.rg)
                A.act(y_.ap, x_.ap, AF.Identity, [x_, R], [y_], scale=rs, bias=nmr)
                yield
                for g in range(4):
                    bi = (t * 4 + g) % 8
                    pb = A.bank(bi)
                    A.tr([(pb[:, j * 128:(j + 1) * 128], y_.ap[:, (g * 4 + j) * 128:(g * 4 + j + 1) * 128], ident) for j in range(4)],
                         [y_, CONST], [PR[bi]])
                    yield
                    for j in range(4):
                        kc = g * 4 + j
                        if (j + g) % 2 == 0:
                            A.ts(xmT.ap[:, kc, rows], pb[:, j * 128:(j + 1) * 128], msc[:, 5, kc:kc + 1], ALU.mult,
                                 [PR[bi], MSC2], [xmTR[t]], s2=msc[:, 4, kc:kc + 1], op1=ALU.add)
                        else:
                            A.act(xmT.ap[:, kc, rows], pb[:, j * 128:(j + 1) * 128], AF.Identity, [PR[bi], MSC2], [xmTR[t]],
                                  scale=msc[:, 5, kc:kc + 1], bias=msc[:, 4, kc:kc + 1])
                        yield

            xmTR = [Region(f"xmT{t}") for t in range(8)]
            A.interleave([tileH(t) for t in range(8)], 3, 14)
            S.op("dve", lambda e: e.memset(mp[:, 0:1], 0.0), xmTR, [xmT])
            S.fence()
            if self.stop == "H":
                return self.finish(nc, S)

            hid = A.at(0, [22, NOWN], BF16, "hid")
            sa = [A.at(44 + 2 * i, [512], F32, f"sa{i}") for i in range(2)]
            g2bc = A.at(48, [D], F32, "g2bc")
            S.dma("sp", g2bc.ap, mod_d[0:1, 5 * D:6 * D].broadcast_to([128, D]), reads=[Rd["mod"]], writes=[g2bc])
            xb = [A.at(56 + 2 * i, [512], F32, f"x1k{i}") for i in range(2)]
            t1 = [A.at(60 + 2 * i, [512], F32, f"t2{i}") for i in range(2)]
            st2 = A.at(127, [8, 4, 6], F32, "st2")
            RdR2 = [[Region(f"R2_{i}_{j}") for j in range(8)] for i in range(4)]
            for hh in range(2):
                n = 0
                for r in range(6):
                    wa, ra = A.ws_get()
                    wb_, rb = A.ws_get()
                    nsub = 4 if r < 5 else 2
                    for sub in range(nsub):
                        j = r * 4 + sub
                        for hf in range(2):
                            b = (n % 2) * 2
                            sa_ = sa[n % 2]
                            n += 1
                            pa, pb_ = A.bank(b), A.bank(b + 1)
                            mms = []
                            for kc in range(16):
                                mms.append((pa, wa[:, kc, sub * 128:(sub + 1) * 128], xmT.ap[:, kc, hf * 512:(hf + 1) * 512], kc == 0, kc == 15))
                                mms.append((pb_, wb_[:, kc, sub * 128:(sub + 1) * 128], xmT.ap[:, kc, hf * 512:(hf + 1) * 512], kc == 0, kc == 15))
                            A.mm(mms, [xmT, ra, rb], PR[b:b + 2])
                            A.act(sa_.ap, pa, AF.Silu, [PR[b]], [sa_])
                            A.tt(hid.ap[:, j, hf * 512:(hf + 1) * 512], sa_.ap, pb_, ALU.mult, [sa_, PR[b + 1]], [hid])
                n = 0
                for blk in range(4):
                    cols = slice(blk * 512, (blk + 1) * 512)
                    for pc in range(2):
                        wsl, wr = A.ws_get()
                        for t in range(8):
                            pb = A.bank(t)
                            A.mm([(pb, hid.ap[:, pc * 11 + k, t * 128:(t + 1) * 128], wsl[:, k, :], pc == 0 and k == 0, pc == 1 and k == 10)
                                  for k in range(11)], [hid, wr], [PR[t]])
                    for t in range(8):
                        pb = A.bank(t)
                        x_, t_ = xb[n % 2], t1[n % 2]
                        n += 1
                        rows = slice(t * 128, (t + 1) * 128)
                        A.tt(t_.ap, pb, g2bc.ap[:, cols], ALU.mult, [PR[t], g2bc], [t_])
                        if hh == 0:
                            S.dma("sp", x_.ap, X1_d[rows, cols], reads=[Rd["X1"]], writes=[x_])
                            A.stt(t_.ap, x_.ap, ALPHA, t_.ap, ALU.mult, ALU.add, [x_, t_], [t_])
                        else:
                            S.dma("sp", x_.ap, R2_d[rows, cols], reads=[RdR2[blk][t]], writes=[x_])
                            A.tt(t_.ap, t_.ap, x_.ap, ALU.add, [x_, t_], [t_])
                            S.op("dve", lambda e, t=t, blk=blk, t_=t_: e.bn_stats(out=st2.ap[:, t, blk, :], in_=t_.ap), [t_], [st2])
                        S.dma("sp", R2_d[rows, cols], t_.ap, reads=[t_], writes=[Rd["R2"], RdR2[blk][t]])

            S.fence()

            lg = A.at(0, [D], F32, "ln2g")
            lb = A.at(8, [D], F32, "ln2b")
            S.dma("sp", lg.ap, lnrow[2:3, :].broadcast_to([128, D]), writes=[lg])
            S.dma("sp", lb.ap, lnrow[3:4, :].broadcast_to([128, D]), writes=[lb])
            NBJ = 4
            rt = [A.at(16 + 8 * i, [D], F32, f"rt2{i}") for i in range(NBJ)]
            ob = [A.at(48 + 8 * i, [D], F32, f"ob{i}") for i in range(NBJ)]
            s1 = [A.at(88 + 0.0625 * i, [4], F32, f"s2{i}") for i in range(NBJ)]

            def tileJ(t):
                r_, o_, s_ = rt[t % NBJ], ob[t % NBJ], s1[t % NBJ]
                rows = slice(t * 128, (t + 1) * 128)
                S.dma("sp", r_.ap, R2_d[rows, :], reads=[Rd["R2"]], writes=[r_])
                S.op("dve", lambda e, t=t, s_=s_: e.bn_aggr(out=s_.ap[:, 0:2], in_=st2.ap[:, t, :, :].rearrange("p a b -> p (a b)")), [st2], [s_])
                yield
                yield from A.rstd_chain_g(s_.ap[:, 1:2], s_.ap[:, 0:1], s_.ap[:, 2:3], s_.ap[:, 3:4], s_.rg)
                A.act(o_.ap, r_.ap, AF.Identity, [r_, s_], [o_], scale=s_.ap[:, 2:3], bias=s_.ap[:, 3:4])
                yield
                A.tt(o_.ap, o_.ap, lg.ap, ALU.mult, [o_, lg], [o_])
                yield
                A.tt(o_.ap, o_.ap, lb.ap, ALU.add, [o_, lb], [o_])
                yield
                S.dma("sp", out[rows, :], o_.ap, reads=[o_], writes=[Rd["out"]])
                yield

            A.interleave([tileJ(t) for t in range(8)], 4, 2)
            return self.finish(nc, S)

    def finish(self, nc, S):
        S.wait_final()
        S.emit()
        return nc


def _fm(v):
    return np.ascontiguousarray(v.reshape(16, 128).T)


def make_in_maps(x, c, ctx, c_ctx, w_mod, b_mod, w_in, b_if, w_qk_conv, b_qk_conv, mh_norm_g, w_dw, b_dw,
                 conv_ln_g, conv_ln_b, w_proj_m, w_proj_c, w_out, ln1_g, ln1_b, w_ffn_in, w_ffn_out, ln2_g, ln2_b):
    f32 = np.float32
    A = lambda a: np.ascontiguousarray(np.asarray(a, dtype=f32))
    w_mod0, w_in0 = A(w_mod[0]), A(w_in[0])
    w_g_n = np.ascontiguousarray(w_in0[:, 6144:6176])
    w_g_f = np.ascontiguousarray(np.concatenate([w_in0[:, 6160:6176], w_in0[:, 6144:6160]], axis=1))
    bif_n = A(b_if[0]).reshape(1, 32)
    bif_f = np.ascontiguousarray(np.concatenate([bif_n[:, 16:32], bif_n[:, 0:16]], axis=1))
    cf = np.zeros((128, 4, 128), f32)
    cf[:, 0, :] = np.eye(128)
    idx = np.arange(128)
    cf[:, 1, :] = (idx[:, None] <= idx[None, :])
    cf[:, 2, :] = (idx[:, None] >= idx[None, :])
    cf[:, 3, :] = 1.0
    cf = cf.reshape(128, 512)
    identb = np.eye(128, dtype=f32).astype(ml_dtypes.bfloat16)
    lnrow = A(np.stack([ln1_g[0], ln1_b[0], ln2_g[0], ln2_b[0]]))
    shared = dict(w_mod=w_mod0, bmod=A(b_mod[0]).reshape(1, -1), w_in=w_in0, w_pm=A(w_proj_m[0]), w_pc=A(w_proj_c[0]),
                  w_o=A(w_out[0]), lnrow=lnrow, w_f1=A(w_ffn_in[0]), w_f2=A(w_ffn_out[0]), cf32=cf, identb=identb)

    def pfm_of(flip):
        p = np.zeros((128, NPF), f32)
        wq = A(w_qk_conv[0])
        wd = A(w_dw[0])
        if flip:
            wq = wq[::-1]
            wd = wd[::-1]
        for j in range(3):
            p[:, j * 16:(j + 1) * 16] = _fm(wq[j])
        p[:, 48:64] = _fm(A(b_qk_conv[0]))
        for j in range(31):
            p[:, 64 + j * 16:64 + (j + 1) * 16] = _fm(wd[j])
        p[:, 560:576] = _fm(A(b_dw[0]))
        p[:, 576:592] = _fm(A(mh_norm_g[0]))
        p[:, 592:608] = _fm(A(conv_ln_g[0]))
        p[:, 608:624] = _fm(A(conv_ln_b[0]))
        return p
    pfms = [pfm_of(False), pfm_of(True)]
    x, ctx, c, c_ctx = A(x), A(ctx), A(c), A(c_ctx)
    in_maps = []
    for r in range(8):
        b, s = r // 2, r % 2
        xb, cb = x[b], ctx[b]
        if s:
            xb, cb = xb[::-1], cb[::-1]
        xin = np.ascontiguousarray(np.concatenate([cb, xb[1024:], xb[:1024]], axis=0))
        cv = np.zeros((128, 16, 2), f32)
        cv[:, :, 0] = _fm(c[b])
        cv[:, :, 1] = _fm(c_ctx)
        m = dict(shared)
        m.update(xin=xin, cvec=cv.reshape(128, 32), w_g=w_g_f if s else w_g_n, bif=bif_f if s else bif_n, pfm=pfms[s])
        in_maps.append(m)
    return in_maps


_NC_CACHE = {}


def kernel(**inputs):
    if "nc" not in _NC_CACHE:
        bld = Builder()
        bld.stop = None
        _NC_CACHE["nc"] = bld.build()
    nc = _NC_CACHE["nc"]
    in_maps = make_in_maps(**inputs)
    res = run_bass_kernel_spmd(nc, in_maps, core_ids=list(range(8)))
    out = np.zeros((4, 2048, 2048), np.float32)
    for r in range(8):
        b, s = r // 2, r % 2
        o = np.asarray(res.results[r]["out"], dtype=np.float32)
        if s:
            out[b, 1024:] = o[::-1]
        else:
            out[b, :1024] = o
    return out
```
